# Optimizing a Trainium2 kernel written in Bass

```python
import jax
import jax.numpy as jnp
from jax import lax
import numpy as np

D_MODEL = 1024
BATCH = 1
SEQ = 16384
DEPTH = 2
DEC_BATCH = 32
DEC_SEQ = 64
PAST_LEN = 1024

CHUNK = 64
N_EVEN = (DEPTH + 1) // 2
N_ODD = DEPTH // 2

POOL_WIDTH = D_MODEL // 2
POOL_GROUPS = 4
POOL_GC = POOL_WIDTH // POOL_GROUPS
POOL_WINDOWS = (2, 4, 8, 16)
POOL_CTX = max(POOL_WINDOWS) - 1

HEAD_DIM = 64
N_Q_HEADS = (D_MODEL // 2) // HEAD_DIM
N_KV_HEADS = 2
Q_PER_KV = N_Q_HEADS // N_KV_HEADS
WINDOW = 128
SWA_CTX = -(-WINDOW // CHUNK) * CHUNK

CONV_WIDTH = D_MODEL // 2
CONV_K = 31
CONV_CTX = CONV_K - 1

GMLP_WIDTH = D_MODEL // 2
GMLP_GROUPS = 4
GMLP_GC = GMLP_WIDTH // GMLP_GROUPS
GMLP_CHUNK = 128

D_FF = -(-(8 * D_MODEL) // (3 * 256)) * 256

EVEN_IN = POOL_WIDTH + N_Q_HEADS * HEAD_DIM + 2 * N_KV_HEADS * HEAD_DIM
EVEN_OUT = POOL_WIDTH + N_Q_HEADS * HEAD_DIM
ODD_IN = 2 * CONV_WIDTH + 2 * GMLP_WIDTH
ODD_OUT = CONV_WIDTH + GMLP_WIDTH

kernel_name = 'hybrid_streaming_pool_swa_conformer_gmlp_step'


def rms_norm(x, g, eps=1e-6):
    xf = x.astype(jnp.float32)
    y = xf * lax.rsqrt(jnp.mean(xf * xf, axis=-1, keepdims=True) + eps)
    return (y * g.astype(jnp.float32)).astype(x.dtype)


def layer_norm(x, g, b, eps=1e-5):
    xf = x.astype(jnp.float32)
    mu = jnp.mean(xf, axis=-1, keepdims=True)
    xc = xf - mu
    var = jnp.mean(xc * xc, axis=-1, keepdims=True)
    return (xc * lax.rsqrt(var + eps) * g.astype(jnp.float32) + b.astype(jnp.float32)).astype(x.dtype)


def alibi_slopes():
    return 2.0 ** (-8.0 * jnp.arange(1, N_Q_HEADS + 1, dtype=jnp.float32) / N_Q_HEADS)


def pool_mixer(u, prefix, pos0, w_pool, scale):
    b, t = u.shape[0], u.shape[1]
    ext = jnp.concatenate([prefix, u], axis=1).astype(jnp.float32)
    cs = jnp.cumsum(ext, axis=1)
    cs = jnp.concatenate([jnp.zeros_like(cs[:, :1]), cs], axis=1)
    pos = pos0 + jnp.arange(t)
    means = []
    for g, w in enumerate(POOL_WINDOWS):
        sl = slice(g * POOL_GC, (g + 1) * POOL_GC)
        hi = cs[:, POOL_CTX + 1:POOL_CTX + 1 + t, sl]
        lo = cs[:, POOL_CTX + 1 - w:POOL_CTX + 1 - w + t, sl]
        cnt = jnp.minimum(pos + 1, w).astype(jnp.float32)[None, :, None]
        means.append((hi - lo) / cnt)
    pooled = jnp.concatenate(means, axis=-1) - u.astype(jnp.float32)
    pg = pooled.reshape(b, t, POOL_GROUPS, POOL_GC)
    y = jnp.einsum('btgc,gcd->btgd', pg, w_pool.astype(jnp.float32)).reshape(b, t, POOL_WIDTH)
    return (y * scale.astype(jnp.float32)).astype(u.dtype)


def swa_attention(q, k, v, k_prev, v_prev, pos0, sinks):
    b, t = q.shape[0], q.shape[1]
    n_blk = -(-t // CHUNK)
    pad = n_blk * CHUNK - t
    nc = SWA_CTX // CHUNK
    span = (nc + 1) * CHUNK
    padw = ((0, 0), (0, pad), (0, 0), (0, 0))
    q = jnp.pad(q, padw)
    k_ext = jnp.concatenate([k_prev, jnp.pad(k, padw)], axis=1)
    v_ext = jnp.concatenate([v_prev, jnp.pad(v, padw)], axis=1)
    k_ch = k_ext.reshape(b, n_blk + nc, CHUNK, N_KV_HEADS, HEAD_DIM)
    v_ch = v_ext.reshape(b, n_blk + nc, CHUNK, N_KV_HEADS, HEAD_DIM)
    k_blk = jnp.concatenate([k_ch[:, j:j + n_blk] for j in range(nc + 1)], axis=2)
    v_blk = jnp.concatenate([v_ch[:, j:j + n_blk] for j in range(nc + 1)], axis=2)
    q_blk = q.reshape(b, n_blk, CHUNK, N_KV_HEADS, Q_PER_KV, HEAD_DIM)
    s = jnp.einsum('bnqkgd,bnskd->bnkgqs', q_blk, k_blk,
                   preferred_element_type=jnp.float32) * (HEAD_DIM ** -0.5)
    q_pos = pos0 + jnp.arange(n_blk * CHUNK).reshape(n_blk, CHUNK)
    k_pos = pos0 - SWA_CTX + jnp.arange(n_blk)[:, None] * CHUNK + jnp.arange(span)[None, :]
    dist = jnp.abs(q_pos[:, :, None] - k_pos[:, None, :]).astype(jnp.float32)
    slopes = alibi_slopes().reshape(N_KV_HEADS, Q_PER_KV)
    s = s - slopes[None, None, :, :, None, None] * dist[None, :, None, None, :, :]
    valid = (k_pos >= 0) & (k_pos < pos0 + t)
    s = jnp.where(valid[None, :, None, None, None, :], s, -jnp.inf)
    sink = sinks.astype(jnp.float32).reshape(N_KV_HEADS, Q_PER_KV)[None, None, :, :, None, None]
    m = jnp.maximum(jnp.max(s, axis=-1, keepdims=True), sink)
    p = jnp.exp(s - m)
    denom = jnp.sum(p, axis=-1, keepdims=True) + jnp.exp(sink - m)
    o = jnp.einsum('bnkgqs,bnskd->bnqkgd', (p / denom).astype(v.dtype), v_blk)
    return o.reshape(b, n_blk * CHUNK, N_Q_HEADS * HEAD_DIM)[:, :t]


def depthwise_causal_conv(ext, w, bias):
    y = lax.conv_general_dilated(ext, w[:, None, :].astype(ext.dtype), window_strides=(1,),
                                 padding='VALID', dimension_numbers=('NWC', 'WIO', 'NWC'),
                                 feature_group_count=ext.shape[-1])
    return y + bias.astype(ext.dtype)


def gmlp_gate(u, v, ln_g, ln_b, w_sp, b_sp):
    b, t = v.shape[0], v.shape[1]
    vn = layer_norm(v, ln_g, ln_b)
    n = -(-t // GMLP_CHUNK)
    pad = n * GMLP_CHUNK - t
    vp = jnp.pad(vn, ((0, 0), (0, pad), (0, 0))).reshape(b, n, GMLP_CHUNK, GMLP_GROUPS, GMLP_GC)
    mask = jnp.tril(jnp.ones((GMLP_CHUNK, GMLP_CHUNK), dtype=w_sp.dtype))
    w = (w_sp * mask[None]).astype(v.dtype)
    s = jnp.einsum('gij,bnjgc->bnigc', w, vp) + b_sp.T.astype(v.dtype)[None, None, :, :, None]
    s = s.reshape(b, n * GMLP_CHUNK, GMLP_WIDTH)[:, :t]
    return u * s, vn


def even_mixer(x, g_mix, pool_prev, k_prev, v_prev, pos0,
               w_in, q_g, k_g, sinks, w_pool, pool_scale, w_out):
    b, t = x.shape[0], x.shape[1]
    h = rms_norm(x, g_mix)
    z = h @ w_in
    o1 = POOL_WIDTH
    o2 = o1 + N_Q_HEADS * HEAD_DIM
    o3 = o2 + N_KV_HEADS * HEAD_DIM
    u = z[..., :o1]
    q = rms_norm(z[..., o1:o2].reshape(b, t, N_Q_HEADS, HEAD_DIM), q_g)
    k = rms_norm(z[..., o2:o3].reshape(b, t, N_KV_HEADS, HEAD_DIM), k_g)
    v = z[..., o3:].reshape(b, t, N_KV_HEADS, HEAD_DIM)
    a_out = pool_mixer(u, pool_prev, pos0, w_pool, pool_scale)
    b_out = swa_attention(q, k, v, k_prev, v_prev, pos0, sinks)
    x = x + jnp.concatenate([a_out, b_out], axis=-1) @ w_out
    new_pool = jnp.concatenate([pool_prev, u], axis=1)[:, -POOL_CTX:]
    new_k = jnp.concatenate([k_prev, k], axis=1)[:, -SWA_CTX:]
    new_v = jnp.concatenate([v_prev, v], axis=1)[:, -SWA_CTX:]
    return x, new_pool, new_k, new_v


def odd_mixer(x, g_mix, conv_prev, w_in, conv_w, conv_b, conv_ln_g, conv_ln_b,
              gmlp_ln_g, gmlp_ln_b, gmlp_w, gmlp_b, w_out):
    h = rms_norm(x, g_mix)
    z = h @ w_in
    c1 = CONV_WIDTH
    c2 = 2 * CONV_WIDTH
    c3 = c2 + GMLP_WIDTH
    glu = z[..., :c1] * jax.nn.sigmoid(z[..., c1:c2])
    ext = jnp.concatenate([conv_prev, glu], axis=1)
    c = depthwise_causal_conv(ext, conv_w, conv_b)
    c = jax.nn.silu(layer_norm(c, conv_ln_g, conv_ln_b))
    zu = jax.nn.gelu(z[..., c2:c3], approximate=False)
    zv = jax.nn.gelu(z[..., c3:], approximate=False)
    d, vn = gmlp_gate(zu, zv, gmlp_ln_g, gmlp_ln_b, gmlp_w, gmlp_b)
    x = x + jnp.concatenate([c, d], axis=-1) @ w_out
    return x, ext[:, -CONV_CTX:], vn


def swiglu_block(x, g, w_gate, w_up, w_down):
    h = rms_norm(x, g)
    return x + (jax.nn.silu(h @ w_gate) * (h @ w_up)) @ w_down


def setup_inputs(seed: int = 0) -> dict:
    key = jax.random.key(seed)
    ks = iter(jax.random.split(key, 40))
    f32 = jnp.float32

    def nrm(shape, scale=1.0):
        return jax.random.normal(next(ks), shape, f32) * scale

    def gain(shape):
        return 1.0 + 0.05 * jax.random.normal(next(ks), shape, f32)

    return {
        'x_prompt': nrm((BATCH, SEQ, D_MODEL)),
        'x_sample': nrm((DEC_BATCH, DEC_SEQ, D_MODEL)),
        'state_pool': nrm((N_EVEN, DEC_BATCH, POOL_CTX, POOL_WIDTH), 0.6),
        'state_swa_k': nrm((N_EVEN, DEC_BATCH, SWA_CTX, N_KV_HEADS, HEAD_DIM)),
        'state_swa_v': nrm((N_EVEN, DEC_BATCH, SWA_CTX, N_KV_HEADS, HEAD_DIM), 0.6),
        'state_conv': nrm((N_ODD, DEC_BATCH, CONV_CTX, CONV_WIDTH), 0.4),
        'norm_mix': gain((DEPTH, D_MODEL)),
        'norm_ffn': gain((DEPTH, D_MODEL)),
        'w_in_even': nrm((N_EVEN, D_MODEL, EVEN_IN), D_MODEL ** -0.5),
        'q_norm': gain((N_EVEN, HEAD_DIM)),
        'k_norm': gain((N_EVEN, HEAD_DIM)),
        'attn_sinks': nrm((N_EVEN, N_Q_HEADS), 0.5),
        'pool_w': nrm((N_EVEN, POOL_GROUPS, POOL_GC, POOL_GC), POOL_GC ** -0.5),
        'pool_scale': gain((N_EVEN, POOL_WIDTH)),
        'w_out_even': nrm((N_EVEN, EVEN_OUT, D_MODEL), EVEN_OUT ** -0.5),
        'w_in_odd': nrm((N_ODD, D_MODEL, ODD_IN), D_MODEL ** -0.5),
        'conv_w': nrm((N_ODD, CONV_K, CONV_WIDTH), CONV_K ** -0.5),
        'conv_b': nrm((N_ODD, CONV_WIDTH), 0.02),
        'conv_ln_g': gain((N_ODD, CONV_WIDTH)),
        'conv_ln_b': nrm((N_ODD, CONV_WIDTH), 0.02),
        'gmlp_ln_g': gain((N_ODD, GMLP_WIDTH)),
        'gmlp_ln_b': nrm((N_ODD, GMLP_WIDTH), 0.02),
        'gmlp_w': nrm((N_ODD, GMLP_GROUPS, GMLP_CHUNK, GMLP_CHUNK), GMLP_CHUNK ** -0.5),
        'gmlp_b': gain((N_ODD, GMLP_GROUPS, GMLP_CHUNK)),
        'w_out_odd': nrm((N_ODD, ODD_OUT, D_MODEL), ODD_OUT ** -0.5),
        'ffn_gate': nrm((DEPTH, D_MODEL, D_FF), D_MODEL ** -0.5),
        'ffn_up': nrm((DEPTH, D_MODEL, D_FF), D_MODEL ** -0.5),
        'ffn_down': nrm((DEPTH, D_FF, D_MODEL), D_FF ** -0.5),
    }


def reference(x_prompt, x_sample, state_pool, state_swa_k, state_swa_v, state_conv,
              norm_mix, norm_ffn, w_in_even, q_norm, k_norm, attn_sinks, pool_w, pool_scale,
              w_out_even, w_in_odd, conv_w, conv_b, conv_ln_g, conv_ln_b, gmlp_ln_g, gmlp_ln_b,
              gmlp_w, gmlp_b, w_out_odd, ffn_gate, ffn_up, ffn_down):
    xp, xs = x_prompt, x_sample
    bp = xp.shape[0]
    pool_p, pool_s, kp_l, ks_l, vp_l, vs_l, conv_p, conv_s, gv_s = [], [], [], [], [], [], [], [], []
    for layer in range(DEPTH):
        if layer % 2 == 0:
            i = layer // 2
            prm = (w_in_even[i], q_norm[i], k_norm[i], attn_sinks[i], pool_w[i], pool_scale[i], w_out_even[i])
            zero_pool = jnp.zeros((bp, POOL_CTX, POOL_WIDTH), xp.dtype)
            zero_kv = jnp.zeros((bp, SWA_CTX, N_KV_HEADS, HEAD_DIM), xp.dtype)
            xp, pp, kk, vv = even_mixer(xp, norm_mix[layer], zero_pool, zero_kv, zero_kv, 0, *prm)
            pool_p.append(pp)
            kp_l.append(kk)
            vp_l.append(vv)
            xs, pp, kk, vv = even_mixer(xs, norm_mix[layer], state_pool[i], state_swa_k[i],
                                        state_swa_v[i], PAST_LEN, *prm)
            pool_s.append(pp)
            ks_l.append(kk)
            vs_l.append(vv)
        else:
            j = layer // 2
            prm = (w_in_odd[j], conv_w[j], conv_b[j], conv_ln_g[j], conv_ln_b[j],
                   gmlp_ln_g[j], gmlp_ln_b[j], gmlp_w[j], gmlp_b[j], w_out_odd[j])
            zero_conv = jnp.zeros((bp, CONV_CTX, CONV_WIDTH), xp.dtype)
            xp, cc, _ = odd_mixer(xp, norm_mix[layer], zero_conv, *prm)
            conv_p.append(cc)
            xs, cc, gv = odd_mixer(xs, norm_mix[layer], state_conv[j], *prm)
            conv_s.append(cc)
            gv_s.append(gv)
        xp = swiglu_block(xp, norm_ffn[layer], ffn_gate[layer], ffn_up[layer], ffn_down[layer])
        xs = swiglu_block(xs, norm_ffn[layer], ffn_gate[layer], ffn_up[layer], ffn_down[layer])
    return (xp, xs, jnp.stack(pool_p), jnp.stack(pool_s), jnp.stack(kp_l), jnp.stack(ks_l),
            jnp.stack(vp_l), jnp.stack(vs_l), jnp.stack(conv_p), jnp.stack(conv_s), jnp.stack(gv_s))
```

```python
import numpy as np
from contextlib import ExitStack
import concourse.bass as bass
import concourse.mybir as mybir
from concourse.bass_utils import run_bass_kernel_spmd

F32 = mybir.dt.float32
BF16 = mybir.dt.bfloat16
ALU = mybir.AluOpType
AF = mybir.ActivationFunctionType
AX = mybir.AxisListType

SAME_ENGINE_SYNC = True
N_CORES = 8
SBUF_BASE = 16512
SBUF_LIMIT = 229376

C_GMIX = 0
C_GFFN = 16
C_PSC = 32
C_GQ = 36
C_GK = 37
C_SINK = 38
C_CB = 46
C_LG = 50
C_LB = 54
C_CW = 58
NCOL = 58 + 124
K_ID = 0
K_TRIU = 128
K_ND = 256
K_NDF = 512
K_ICNT = 768
K_LNG = 832
K_LNB = 1344
K_GB = 1856
K_BD = 2368
K_TRIU64 = 2496
NCONST = 2560

_STAGE = 99
_LMASK = 7
_SMASK = 255
FCH = [3, 3, 3, 3, 3, 3, 3, 1]


class Buf:
    __slots__ = ("name", "ws", "rs")

    def __init__(self, name=""):
        self.name = name
        self.ws = {}
        self.rs = {}


class Chan:
    def __init__(self, sem, wait_all=False):
        self.sem = sem
        self.n = 0
        self.wait_all = wait_all


class Op:
    __slots__ = ("eng", "fn", "deps", "signal", "cnt", "chan", "chan_cnt", "idx")


def _key(o):
    return o.eng if o.chan is None else ("c", id(o.chan))


class Prog:
    ENGS = ("pe", "act", "dve", "pool", "sp")

    def __init__(self):
        self.ops = {e: [] for e in self.ENGS}
        self.n = 0
        self.last = {}

    def op(self, eng, fn, reads=(), writes=(), chan=None, extra=()):
        o = Op()
        o.eng = eng
        o.fn = fn
        o.signal = False
        o.cnt = 0
        o.chan = chan
        o.idx = self.n
        self.n += 1
        if chan is not None:
            chan.n += 1
            o.chan_cnt = chan.n
        else:
            o.chan_cnt = 0
        deps = {}

        def add(p, raw):
            if p is o:
                return
            if p.chan is None and o.chan is None and p.eng == eng:
                if eng == "pe" or not SAME_ENGINE_SYNC or (not raw and eng != "pool"):
                    return
            k = _key(p)
            q = deps.get(k)
            if q is None or q.idx < p.idx:
                deps[k] = p

        for b in reads:
            for w in b.ws.values():
                add(w, True)
        for b in writes:
            for r in b.rs.values():
                add(r, False)
            for w in b.ws.values():
                add(w, False)
        for p in extra:
            if p is not None:
                add(p, True)
        k = _key(o)
        for b in reads:
            b.rs[k] = o
        for b in writes:
            if b.rs:
                b.ws = {}
                b.rs = {}
            b.ws[k] = o
        o.deps = list(deps.values())
        for p in o.deps:
            if p.chan is None:
                p.signal = True
        self.ops[eng].append(o)
        if chan is None and fn is not None:
            self.last[eng] = o
        return o

    def lower(self, block, sems):
        for e in self.ENGS:
            c = 0
            for o in self.ops[e]:
                if o.chan is None and o.signal:
                    c += 1
                    o.cnt = c

        def run(ename):
            def body(eng):
                waited = {}
                for o in self.ops[ename]:
                    need = {}
                    for p in o.deps:
                        if p.chan is not None:
                            s = p.chan.sem
                            v = 16 * (p.chan.n if p.chan.wait_all else p.chan_cnt)
                        else:
                            s, v = sems[p.eng], p.cnt
                        k = id(s)
                        if need.get(k, (None, 0))[1] < v:
                            need[k] = (s, v)
                    for k, (s, v) in need.items():
                        if waited.get(k, 0) >= v:
                            continue
                        waited[k] = v
                        eng.wait_ge(s, v)
                    if o.fn is None:
                        continue
                    ins = o.fn(eng)
                    if o.chan is not None:
                        ins.then_inc(o.chan.sem, 16)
                    elif o.signal:
                        ins.then_inc(sems[ename], 1)
            return body

        block.tensor(run("pe"))
        block.scalar(run("act"))
        block.vector(run("dve"))
        block.gpsimd(run("pool"))
        block.sync(run("sp"))


def bcast_mid(ap, n):
    a = ap.ap
    return bass.AP(tensor=ap.tensor, offset=ap.offset, ap=[list(a[0]), [0, n]] + [list(x) for x in a[1:]])


def build_nc(stg=99):
    nc = bass.Bass("TRN2", target_bir_lowering=False)

    def din(name, shape):
        return nc.dram_tensor(name, shape, F32, kind="ExternalInput").ap()

    def dout(name, shape):
        return nc.dram_tensor(name, shape, F32, kind="ExternalOutput").ap()

    xin = din("xin", [2560, 1024])
    cols_d = din("cols", [128, NCOL])
    consts_d = din("consts", [128, NCONST])
    w_in0_d = din("w_in0", [1024, 1280])
    w_out0_d = din("w_out0", [1024, 1024])
    pool_w_d = din("pool_w", [512, 128])
    w_in1_d = din("w_in1", [1024, 2048])
    w_out1_d = din("w_out1", [1024, 1024])
    gwT_d = din("gwT", [512, 128])
    fgate_d = din("fgate", [2048, 2816])
    fup_d = din("fup", [2048, 2816])
    fdown_d = din("fdown", [5632, 1024])
    st_pool_d = din("st_pool", [60, 512])
    st_k_d = din("st_k", [512, 128])
    st_v_d = din("st_v", [512, 128])
    st_conv_d = din("st_conv", [120, 512])
    yout = dout("yout", [2304, 1024])
    pool_o = dout("pool_o", [5 * 128, 512])
    k_o = dout("k_o", [5 * 128, 128])
    v_o = dout("v_o", [5 * 128, 128])
    conv_o = dout("conv_o", [5 * 128, 512])
    gv_o = dout("gv_o", [4 * 64, 512])

    cnt = [0]

    class Region:
        def __init__(self, start, limit):
            self.off = start
            self.limit = limit

        def alloc(self, shape, dt):
            size = 1
            for s in shape[1:]:
                size *= s
            size *= 4 if dt == F32 else 2
            size = (size + 63) // 64 * 64
            at = self.off
            self.off += size
            assert self.off <= self.limit, ("SBUF overflow", self.off, self.limit)
            cnt[0] += 1
            return nc.alloc_sbuf_tensor_at("sb%d" % cnt[0], list(shape), dt, offset=at)

    perm = Region(SBUF_BASE, SBUF_LIMIT)
    x_res = perm.alloc([128, 8, 1280], F32)
    h_all = perm.alloc([128, 8, 1280], BF16)
    wmi = perm.alloc([128, 8, 2048], BF16)
    wmo = perm.alloc([128, 8, 1024], BF16)
    wgA = perm.alloc([128, 8, 384], BF16)
    wuA = perm.alloc([128, 8, 384], BF16)
    wdA = perm.alloc([128, 3, 1024], BF16)
    xs = [perm.alloc([128, 512], F32) for _ in range(2)]
    cols = perm.alloc([128, NCOL], F32)
    consts = perm.alloc([128, NCONST], F32)
    pw = perm.alloc([128, 4, 128], BF16)
    wm = perm.alloc([128, 4, 128], BF16)
    wms = perm.alloc([128, 4, 64], BF16)
    ident_bf = perm.alloc([128, 128], BF16)
    ones_bf = perm.alloc([128, 128], BF16)
    bd_bf = perm.alloc([128, 128], BF16)
    kTstz = [perm.alloc([128, 4, 128], BF16) for _ in range(2)]
    btab = perm.alloc([128, 8, 256], BF16)
    maskf = perm.alloc([128, 256], BF16)
    negsink = perm.alloc([128, 8], F32)
    Vst = perm.alloc([128, 4, 128], BF16)
    ulead_s = perm.alloc([128, 4, 4, 15], F32)
    clead_s = perm.alloc([128, 4, 4, 30], F32)
    gq8 = perm.alloc([128, 2], F32)
    kT_c = perm.alloc([128, 128], BF16)
    V_c = perm.alloc([128, 128], BF16)
    u_c = perm.alloc([128, 4, 16], F32)
    glu_c = perm.alloc([128, 4, 32], F32)
    xsq = perm.alloc([128, 8, 512], BF16)
    rs = perm.alloc([128, 512], F32)
    s_sb = [perm.alloc([128, 512], F32) for _ in range(2)]
    hid = [perm.alloc([128, 3, 512], BF16) for _ in range(2)]
    SCR = perm.off
    rb = Region(SCR, SBUF_LIMIT)
    wgB = rb.alloc([128, 8, 384], BF16)
    wuB = rb.alloc([128, 8, 384], BF16)
    wdB = rb.alloc([128, 3, 1024], BF16)
    r0 = Region(SCR, SBUF_LIMIT)
    u_ext = r0.alloc([128, 4, 320], F32)
    ptmp = [r0.alloc([128, 320], F32) for _ in range(2)]
    pooled = r0.alloc([128, 4, 256], BF16)
    cat = r0.alloc([128, 8, 256], BF16)
    rq2 = [r0.alloc([128, 2, 256], F32) for _ in range(2)]
    qn = r0.alloc([128, 4, 256], BF16)
    kn = r0.alloc([128, 256], F32)
    kTz = [r0.alloc([128, 384], BF16) for _ in range(2)]
    V_blk = r0.alloc([128, 4, 128], BF16)
    att_pn = [r0.alloc([128, 256], BF16) for _ in range(8)]
    att_PT = [r0.alloc([128, 1024], BF16) for _ in range(2)]
    att_sm = r0.alloc([128, 32, 4], F32)
    ptiny = r0.alloc([128, 4, 16], F32)
    r1 = Region(SCR, SBUF_LIMIT)
    glu_ext = r1.alloc([128, 4, 384], F32)
    glu_bf = r1.alloc([128, 4, 384], BF16)
    dgb = [r1.alloc([128, 16, 128], BF16) for _ in range(2)]
    mv = r1.alloc([128, 3, 256], F32)
    ctmp2 = r1.alloc([128, 2, 256], F32)
    ctmp = [ctmp2[:, 0, :], ctmp2[:, 1, :]]
    zu = r1.alloc([128, 4, 256], BF16)
    zg = r1.alloc([128, 512], F32)
    sig = zg[:].rearrange("p (a n) -> p a n", a=2)
    vn_bf = r1.alloc([128, 2, 512], BF16)
    cat1 = r1.alloc([128, 8, 256], BF16)
    st6 = r1.alloc([128, 2, 8], F32)
    mvz = r1.alloc([128, 2, 4], F32)
    gtmp = ctmp
    qsq = xsq

    psf = [nc.alloc_psum_tensor("psf%d" % i, [128, 512], F32) for i in range(7)]
    psb = nc.alloc_psum_tensor("psb", [128, 1024], BF16)

    with ExitStack() as es:
        def sem(name):
            return es.enter_context(nc.semaphore(name))

        sems = {e: sem("s_" + e) for e in ("pe", "act", "dve", "pool")}
        ch_setup = Chan(sem("c_setup"), wait_all=True)
        ch_setup2 = Chan(sem("c_setup2"), wait_all=True)
        ch_wmi = Chan(sem("c_wmi"))
        ch_wmo = Chan(sem("c_wmo"))
        ch_fA = Chan(sem("c_fA"))
        ch_fB = Chan(sem("c_fB"))
        ch_fAd = Chan(sem("c_fAd"))
        ch_fBd = Chan(sem("c_fBd"))
        ch_xin = [Chan(sem("c_xin%d" % i)) for i in range(2)]
        ch_out = [Chan(sem("c_out%d" % i)) for i in range(2)]
        ch_misc = Chan(sem("c_misc"))
        block = es.enter_context(nc.Block())
        P = Prog()

        XR = [Buf("xr%d" % t) for t in range(10)]
        HA = [Buf("ha%d" % t) for t in range(10)]
        B = {}

        def bf(name):
            if name not in B:
                B[name] = Buf(name)
            return B[name]

        PSB = [Buf("ps%d" % i) for i in range(7)]
        PTB = Buf("ptb")
        XS = [Buf("xs0"), Buf("xs1")]
        OUT = Buf("out")
        ps_rr = [0]
        xs_rr = [0]

        pinned = set()

        def psum(pin=False):
            while True:
                i = ps_rr[0] % 7
                ps_rr[0] += 1
                if i not in pinned:
                    break
            if pin:
                pinned.add(i)
            return psf[i], PSB[i]

        def unpin(pb):
            pinned.discard(PSB.index(pb))

        def stage():
            i = xs_rr[0] % 2
            xs_rr[0] += 1
            return xs[i], XS[i], i

        def mm(out, lhsT, rhs, start, stop, r, w):
            return P.op("pe", lambda e: e.matmul(out, lhsT, rhs, start=start, stop=stop), reads=r, writes=w)

        def tr(out, in_, ident, r, w):
            return P.op("pe", lambda e: e.transpose(out, in_, ident), reads=r, writes=w)

        def act(out, in_, func, r, w, bias=None, scale=None, accum=None):
            kw = {}
            if bias is not None:
                kw["bias"] = bias
            if scale is not None:
                kw["scale"] = scale
            if accum is not None:
                kw["accum_out"] = accum
            return P.op("act", lambda e: e.activation(out=out, in_=in_, func=func, **kw), reads=r, writes=w)

        def dve(fn, r, w):
            return P.op("dve", fn, reads=r, writes=w)

        def tt(out, in0, in1, op, r, w):
            return dve(lambda e: e.tensor_tensor(out, in0, in1, op), r, w)

        def stt(out, in0, scalar, in1, op0, op1, r, w):
            return dve(lambda e: e.scalar_tensor_tensor(out, in0, scalar, in1, op0, op1), r, w)

        def ts(out, in0, s1, s2, op0, op1, r, w):
            if op1 is None:
                return dve(lambda e: e.tensor_scalar(out, in0, s1, s2, op0), r, w)
            return dve(lambda e: e.tensor_scalar(out, in0, s1, s2, op0, op1), r, w)

        def cpy(out, in_, r, w):
            return dve(lambda e: e.tensor_copy(out, in_), r, w)

        def recip(out, in_, r, w):
            return dve(lambda e: e.reciprocal(out, in_), r, w)

        def mset(ap, v, w):
            return dve(lambda e: e.memset(ap, v), [], w)

        chan_last = {}

        def dma(q, out, in_, chan, r, w):
            o = P.op(q, lambda e: e.dma_start(out=out, in_=in_), reads=r, writes=w, chan=chan)
            chan_last[id(chan)] = o
            return o

        def fence(eng, deps):
            P.op(eng, None, extra=deps)

        scr_dma = []

        def dma_scr(q, out, in_, chan, r, w):
            scr_dma.append(dma(q, out, in_, chan, r, w))

        CONST = bf("consts")
        COLS = bf("cols")

        def col(i):
            return cols[:, i:i + 1]

        ident_f = consts[:, K_ID:K_ID + 128]

        def setup():
            dma("sp", cols[:], cols_d[:, :], ch_setup, [], [COLS])
            dma("sp", consts[:], consts_d[:, :], ch_setup, [], [CONST])
            act(ident_bf[:], ident_f, AF.Copy, [CONST], [bf("ident_bf")])
            act(bd_bf[:], consts[:, K_BD:K_BD + 128], AF.Copy, [CONST], [bf("bd_bf")])
            mset(ones_bf[:], 1.0, [bf("ones_bf")])
            ts(gq8[:, 0:1], col(C_GQ), 0.125, None, ALU.mult, None, [COLS], [bf("gq8")])
            mset(kT_c[:], 0.0, [bf("kT_c")])
            for h in range(8):
                ts(btab[:, h, :], consts[:, K_ND:K_ND + 256], 2.0 ** (-(h + 1)), None, ALU.mult, None, [CONST], [bf("btab")])
            act(maskf[:], consts[:, K_NDF:K_NDF + 256], AF.Copy, [CONST], [bf("maskf")])
            ts(negsink[:], cols[:, C_SINK:C_SINK + 8], -1.0, None, ALU.mult, None, [COLS], [bf("negsink")])
            mset(V_c[:], 0.0, [bf("V_c")])
            mset(u_c[:], 0.0, [bf("u_c")])
            mset(glu_c[:], 0.0, [bf("glu_c")])
            if stg == -1:
                return
            dma("pool", pw[:], pool_w_d.rearrange("(g c) d -> c g d", c=128), ch_setup2, [], [bf("pw")])
            dma("pool", Vst[:], st_v_d.rearrange("(s t) c -> t s c", t=128), ch_setup2, [], [bf("Vst")])
            if stg == -2:
                return
            s0, sb0, i0 = stage()
            dma("sp", s0[:, 0:512].rearrange("p (g i) -> p g i", g=4), gwT_d.rearrange("(g j) i -> j g i", j=128), ch_xin[i0], [], [sb0])
            tt(wm[:], s0[:, 0:512].rearrange("p (g i) -> p g i", g=4), bcast_mid(consts[:, K_TRIU:K_TRIU + 128], 4), ALU.mult,
               [sb0, CONST], [bf("wm")])
            gv = gwT_d.rearrange("(g j) i -> j g i", j=128)
            s0b, sb0b, i0b = stage()
            dma("sp", s0b[0:64, 0:256].rearrange("p (g i) -> p g i", g=4), gv[0:64, :, 0:64], ch_xin[i0b], [], [sb0b])
            dma("sp", s0b[64:128, 0:256].rearrange("p (g i) -> p g i", g=4), gv[0:64, :, 0:64], ch_xin[i0b], [], [sb0b])
            tt(wms[:], s0b[:, 0:256].rearrange("p (g i) -> p g i", g=4), bcast_mid(consts[:, K_TRIU64:K_TRIU64 + 64], 4), ALU.mult,
               [sb0b, CONST], [bf("wms")])
            if stg == -3:
                return
            s1, sb1, i1 = stage()
            dma("sp", s1[:, 0:512].rearrange("p (s c) -> p s c", s=4), st_k_d.rearrange("(s t) c -> t s c", t=128), ch_xin[i1], [], [sb1])
            ps, pb = psum()
            for s in range(4):
                tr(ps[:, s * 128:(s + 1) * 128], s1[:, s * 128:(s + 1) * 128], ident_f, [sb1, CONST], [pb])
            mset(kTstz[0][:], 0.0, [bf("kTst")])
            mset(kTstz[1][:], 0.0, [bf("kTst")])
            act(kTstz[0][0:64, :, :], ps[0:64, :].rearrange("p (s t) -> p s t", s=4), AF.Copy, [pb], [bf("kTst")])
            act(kTstz[1][64:128, :, :], ps[64:128, :].rearrange("p (s t) -> p s t", s=4), AF.Copy, [pb], [bf("kTst")])
            for s in range(4):
                dma("sp", k_o[(1 + s) * 128:(1 + s) * 128 + 64, :], s1[64:128, s * 128:(s + 1) * 128], ch_out[i1], [sb1], [OUT])
            if stg == -4:
                return
            s2, sb2, i2 = stage()
            dma("sp", s2[0:60, 0:512], st_pool_d[:, :], ch_xin[i2], [], [sb2])
            ps, pb = psum()
            for g in range(4):
                tr(ps[:, g * 64:g * 64 + 60], s2[0:60, g * 128:(g + 1) * 128], consts[0:60, K_ID:K_ID + 60], [sb2, CONST], [pb])
            act(ulead_s[:], ps[:, 0:256].rearrange("p (g n) -> p g n", g=4)[:, :, 0:60].rearrange("p g (s r) -> p g s r", s=4),
                AF.Copy, [pb], [bf("ulead_s")])
            if stg == -5:
                return
            s3, sb3, i3 = stage()
            dma("sp", s3[0:120, 0:512], st_conv_d[:, :], ch_xin[i3], [], [sb3])
            ps, pb = psum()
            for g in range(4):
                tr(ps[:, g * 128:g * 128 + 120], s3[0:120, g * 128:(g + 1) * 128], consts[0:120, K_ID:K_ID + 120], [sb3, CONST], [pb])
            act(clead_s[:], ps[:].rearrange("p (g n) -> p g n", g=4)[:, :, 0:120].rearrange("p g (s r) -> p g s r", s=4),
                AF.Copy, [pb], [bf("clead_s")])
            if stg == -6:
                return
            s4, sb4, i4 = stage()
            dma("sp", s4[:, 0:512].rearrange("p (s c) -> p s c", s=4), st_v_d.rearrange("(s t) c -> t s c", t=128), ch_xin[i4], [], [sb4])
            for s in range(4):
                dma("sp", v_o[(1 + s) * 128:(1 + s) * 128 + 64, :], s4[64:128, s * 128:(s + 1) * 128], ch_out[i4], [sb4], [OUT])

        WMI = bf("wmi")
        WMO = bf("wmo")
        SLA = bf("slotA")
        SLB = bf("slotB")
        SLAd = bf("slotAd")
        SLBd = bf("slotBd")

        def load_mixer_weights(layer, defer=False):
            jobs = []

            def dq(*args):
                jobs.append(lambda: dma(*args))

            if layer == 0:
                v = w_in0_d.rearrange("(kc p) n -> p kc n", p=128)
                dq("pool", wmi[:, :, 0:512], v[:, :, 0:512], ch_wmi, [], [WMI])
                for j in range(4):
                    for slot in range(2):
                        h = slot * 4 + j
                        dq("pool", wmi[:, :, 512 + j * 128 + slot * 64:512 + j * 128 + slot * 64 + 64],
                            v[:, :, 512 + h * 64:512 + h * 64 + 64], ch_wmi, [], [WMI])
                dq("pool", wmi[:, :, 1024:1280], v[:, :, 1024:1280], ch_wmi, [], [WMI])
                vo = w_out0_d
                dq("pool", wmo[:, 0:4, :], vo[0:512, :].rearrange("(kc p) n -> p kc n", p=128), ch_wmo, [], [WMO])
                for j in range(4):
                    for slot in range(2):
                        h = slot * 4 + j
                        dq("pool", wmo[slot * 64:(slot + 1) * 64, 4 + j, :], vo[512 + h * 64:512 + h * 64 + 64, :], ch_wmo, [], [WMO])
            else:
                v = w_in1_d.rearrange("(kc p) n -> p kc n", p=128)
                dq("pool", wmi[:, :, 0:1024], v[:, :, 0:1024], ch_wmi, [], [WMI])
                dq("pool", wmi[:, :, 1024:2048], v[:, :, 1024:2048], ch_wmi, [], [WMI])
                dq("pool", wmo[:], w_out1_d.rearrange("(kc p) n -> p kc n", p=128), ch_wmo, [], [WMO])
            if defer:
                return jobs
            for j_ in jobs:
                j_()
            return []

        def chunk_slot(c):
            if c % 2 == 0:
                return wgA, wuA, wdA, SLA, ch_fA
            return wgB, wuB, wdB, SLB, ch_fB

        def chunk_slot_d(c):
            if c % 2 == 0:
                return SLAd, ch_fAd
            return SLBd, ch_fBd

        def load_chunk(layer, c, extra=()):
            wg, wu, wd, sb_, ch = chunk_slot(c)
            nf = FCH[c]
            f0 = sum(FCH[:c])
            gv = fgate_d[layer * 1024:(layer + 1) * 1024, :].rearrange("(kc p) n -> p kc n", p=128)
            uv = fup_d[layer * 1024:(layer + 1) * 1024, :].rearrange("(kc p) n -> p kc n", p=128)
            dvw = fdown_d[layer * 2816 + f0 * 128:layer * 2816 + (f0 + nf) * 128, :].rearrange("(f p) n -> p f n", p=128)
            if extra:
                fence("pool", extra)
            dma("pool", wg[:, :, 0:nf * 128], gv[:, :, f0 * 128:(f0 + nf) * 128], ch, [], [sb_])
            dma("pool", wu[:, :, 0:nf * 128], uv[:, :, f0 * 128:(f0 + nf) * 128], ch, [], [sb_])
            sbd_, chd_ = chunk_slot_d(c)
            dma("pool", wd[:, 0:nf, :], dvw, chd_, [], [sbd_])

        def load_tile(g, t):
            row = (g * 10 + t) * 128
            for half in range(2):
                s, sb_, i = stage()
                dma("sp", s[:], xin[row:row + 128, half * 512:(half + 1) * 512], ch_xin[i], [], [sb_])
                ps, pb = psum()
                for k in range(4):
                    tr(ps[:, k * 128:(k + 1) * 128], s[:, k * 128:(k + 1) * 128], ident_f, [sb_, CONST], [pb])
                act(x_res[:, half * 4:half * 4 + 4, t * 128:(t + 1) * 128], ps[:].rearrange("p (k n) -> p k n", k=4), AF.Copy,
                    [pb], [XR[t]])

        def store_tile(g, t):
            row = (g * 10 + t - 2) * 128
            for half in range(2):
                s, sb_, i = stage()
                ps, pb = psum()
                for k in range(4):
                    kc = half * 4 + k
                    tr(ps[:, k * 128:(k + 1) * 128], x_res[:, kc, t * 128:(t + 1) * 128], ident_f, [XR[t], CONST], [pb])
                act(s[:], ps[:], AF.Copy, [pb], [sb_])
                dma("sp", yout[row:row + 128, half * 512:(half + 1) * 512], s[:], ch_out[i], [sb_], [OUT])

        def rows_out(wins, r0, nrows, dst, r):
            s, sb_, i = stage()
            ps, pb = psum()
            for g_, a in enumerate(wins):
                tr(ps[:, g_ * 128:(g_ + 1) * 128], a, ident_f, r + [CONST], [pb])
            act(s[:, 0:512], ps[:, :], AF.Copy, [pb], [sb_])
            dma("sp", dst, s[:, 0:512], ch_out[i], [sb_], [OUT])

        XSQ = bf("xsq")
        RS = bf("rs")

        def tiles_of(c0, c1):
            return list(range(c0 // 128, (c1 + 127) // 128))

        def norm(c0, c1, gcol):
            W = c1 - c0
            tl = tiles_of(c0, c1)
            xr = [XR[t] for t in tl]
            ha = [HA[t] for t in tl]
            for half in range(2):
                act(xsq[:, half * 4:half * 4 + 4, 0:W], x_res[:, half * 4:half * 4 + 4, c0:c1], AF.Square, xr, [XSQ])
            ps, pb = psum()
            for kc in range(8):
                mm(ps[:, 0:W], ones_bf[:], xsq[:, kc, 0:W], kc == 0, kc == 7, [XSQ, bf("ones_bf")], [pb])
            act(rs[:, 0:W], ps[:, 0:W], AF.Ln, [pb], [RS], bias=1e-6, scale=1.0 / 1024)
            act(rs[:, 0:W], rs[:, 0:W], AF.Exp, [RS], [RS], scale=-0.5)
            for kc in range(8):
                stt(h_all[:, kc, c0:c1], x_res[:, kc, c0:c1], col(gcol + kc), rs[:, 0:W], ALU.mult, ALU.mult,
                    xr + [RS, COLS], ha)

        def wout_residual(catt, CATB, c0):
            tl = tiles_of(c0, c0 + 256)
            xr = [XR[t] for t in tl]
            for pair in range(4):
                ps, pb = psum()
                for i in range(2):
                    m = pair * 2 + i
                    for kc in range(8):
                        mm(ps[:, i * 256:(i + 1) * 256], wmo[:, kc, m * 128:(m + 1) * 128], catt[:, kc, :], kc == 0, kc == 7,
                           [WMO, CATB], [pb])
                xv = x_res[:, pair * 2:pair * 2 + 2, c0:c0 + 256]
                tt(xv, ps[:].rearrange("p (a n) -> p a n", a=2), xv, ALU.add, [pb] + xr, xr)

        deferred_norm = []

        def flush_norm():
            while deferred_norm:
                deferred_norm.pop(0)()

        def mixer0(g, b, kind):
            c0 = b * 256
            tl = [2 * b, 2 * b + 1]
            ha = [HA[t] for t in tl]
            first = (g == 0 and b == 1)
            lastp = (g == 1 and b == 3)
            sample = kind == "sample"
            UE, PL, CAT, QN, KN, KTB, VB = bf("u_ext"), bf("pooled"), bf("cat"), bf("qn"), bf("kn"), bf("kTz"), bf("V_blk")
            hs = h_all[:, :, c0:c0 + 256]
            if sample:
                mset(u_ext[:], 0.0, [UE])
                mset(kTz[0][64:128, :], 0.0, [KTB])
                mset(kTz[1][0:64, :], 0.0, [KTB])
                for gI in range(4):
                    act(u_ext[:, gI, :].rearrange("p (s n) -> p s n", s=4)[:, :, 1:16], ulead_s[:, gI, :, :], AF.Copy, [bf("ulead_s")], [UE])
            else:
                cpy(u_ext[:, :, 0:16], u_c[:], [bf("u_c")], [UE])
                mset(kTz[0][64:128, :], 0.0, [KTB])
                mset(kTz[1][0:64, :], 0.0, [KTB])
                act(kTz[0][0:64, 0:128], kT_c[0:64, :], AF.Copy, [bf("kT_c")], [KTB])
                act(kTz[1][64:128, 0:128], kT_c[64:128, :], AF.Copy, [bf("kT_c")], [KTB])
                act(V_blk[:, 0, :], V_c[:], AF.Copy, [bf("V_c")], [VB])
            for pair in range(2):
                ps, pb = psum()
                for i in range(2):
                    m = pair * 2 + i
                    for kc in range(8):
                        mm(ps[:, i * 256:(i + 1) * 256], wmi[:, kc, m * 128:(m + 1) * 128], hs[:, kc, :], kc == 0, kc == 7,
                           [WMI] + ha, [pb])
                if not sample:
                    act(u_ext[:, pair * 2:pair * 2 + 2, 16:272], ps[:].rearrange("p (a n) -> p a n", a=2), AF.Copy, [pb], [UE])
                else:
                    for i in range(2):
                        m = pair * 2 + i
                        act(u_ext[:, m, :].rearrange("p (s n) -> p s n", s=4)[:, :, 16:80],
                            ps[:, i * 256:(i + 1) * 256].rearrange("p (s n) -> p s n", s=4), AF.Copy, [pb], [UE])
            QSQ = XSQ
            qps = []
            for pair in range(2):
                ps, pb = psum(pin=True)
                for i in range(2):
                    m = pair * 2 + i
                    for kc in range(8):
                        mm(ps[:, i * 256:(i + 1) * 256], wmi[:, kc, 512 + m * 128:512 + (m + 1) * 128], hs[:, kc, :], kc == 0, kc == 7,
                           [WMI] + ha, [pb])
                act(qsq[:, pair * 2:pair * 2 + 2, 0:256], ps[:].rearrange("p (a n) -> p a n", a=2), AF.Square, [pb], [QSQ])
                qps.append((ps, pb))
            psk, pbk = psum(pin=True)
            for kc in range(8):
                mm(psk[:, 0:256], wmi[:, kc, 1024:1152], hs[:, kc, :], kc == 0, kc == 7, [WMI] + ha, [pbk])
            act(qsq[:, 4, 0:256], psk[:, 0:256], AF.Square, [pbk], [QSQ])
            psv, pbv = psum()
            if not sample:
                for t_ in range(2):
                    for kc in range(8):
                        mm(psv[:, t_ * 128:(t_ + 1) * 128], h_all[:, kc, c0 + t_ * 128:c0 + (t_ + 1) * 128], wmi[:, kc, 1152:1280],
                           kc == 0, kc == 7, [WMI] + ha, [pbv])
                act(V_blk[:, 1:3, :], psv[:, 0:256].rearrange("p (a n) -> p a n", a=2), AF.Copy, [pbv], [VB])
                if lastp and (_LMASK & 1):
                    sv_, sbv_, iv_ = stage()
                    act(sv_[:, 0:128], psv[:, 128:256], AF.Copy, [pbv], [sbv_])
                    dma("sp", v_o[0:128, :], sv_[:, 0:128], ch_out[iv_], [sbv_], [OUT])
            else:
                for s in range(4):
                    for kc in range(8):
                        mm(psv[0:64, s * 128:(s + 1) * 128], h_all[:, kc, c0 + s * 64:c0 + (s + 1) * 64], wmi[:, kc, 1152:1280],
                           kc == 0, kc == 7, [WMI] + ha, [pbv])
                act(V_blk[0:64, :, :], psv[0:64, :].rearrange("p (a n) -> p a n", a=4), AF.Copy, [pbv], [VB])
                sv_, sbv_, iv_ = stage()
                act(sv_[0:64, 0:512], psv[0:64, :], AF.Copy, [pbv], [sbv_])
                for s in range(4 if (_SMASK & 1) else 0):
                    dma("sp", v_o[(1 + s) * 128 + 64:(2 + s) * 128, :], sv_[0:64, s * 128:(s + 1) * 128], ch_out[iv_], [sbv_], [OUT])
            for pair in range(3):
                RQ = bf("rq%d" % (pair % 2))
                rq = rq2[pair % 2]
                ps, pb = psum()
                n_ = 2 if pair < 2 else 1
                for i in range(n_):
                    m = pair * 2 + i
                    mm(ps[:, i * 256:(i + 1) * 256], bd_bf[:], qsq[:, m, 0:256], True, True, [bf("bd_bf"), QSQ], [pb])
                act(rq[:, 0:n_, :], ps[:, 0:n_ * 256].rearrange("p (a n) -> p a n", a=n_), AF.Ln, [pb], [RQ], bias=1e-6, scale=1.0 / 64)
                act(rq[:, 0:n_, :], rq[:, 0:n_, :], AF.Exp, [RQ], [RQ], scale=-0.5)
                if pair < 2:
                    qp, qb = qps[pair]
                    for i in range(2):
                        m = pair * 2 + i
                        stt(qn[:, m, :], qp[:, i * 256:(i + 1) * 256], gq8[:, 0:1], rq[:, i, :], ALU.mult, ALU.mult,
                            [qb, RQ, bf("gq8")], [QN])
                    unpin(qb)
                else:
                    stt(kn[:], psk[:, 0:256], col(C_GK), rq[:, 0, :], ALU.mult, ALU.mult, [pbk, RQ, COLS], [KN])
                    act(kTz[0][0:64, 128:384], kn[0:64, :], AF.Copy, [KN], [KTB])
                    act(kTz[1][64:128, 128:384], kn[64:128, :], AF.Copy, [KN], [KTB])
                    unpin(pbk)
            flush_norm()
            if lastp and (_LMASK & 4):
                s_, sb_, i_ = stage()
                ps, pb = psum()
                tr(ps[:, 0:128], kn[:, 128:256], ident_f, [KN, CONST], [pb])
                act(s_[:, 0:128], ps[:, 0:128], AF.Copy, [pb], [sb_])
                dma("sp", k_o[0:128, :], s_[:, 0:128], ch_out[i_], [sb_], [OUT])
            if sample and (_SMASK & 4):
                s_, sb_, i_ = stage()
                ps, pb = psum()
                for t_ in range(2):
                    tr(ps[:, t_ * 128:(t_ + 1) * 128], kn[:, t_ * 128:(t_ + 1) * 128], ident_f, [KN, CONST], [pb])
                act(s_[:, 0:256], ps[:, 0:256], AF.Copy, [pb], [sb_])
                for s in range(4):
                    dma("sp", k_o[(1 + s) * 128 + 64:(2 + s) * 128, :],
                        s_[(s % 2) * 64:(s % 2) * 64 + 64, (s // 2) * 128:(s // 2) * 128 + 128], ch_out[i_], [sb_], [OUT])
            SM = bf("att_sm")
            units = []
            if not sample:
                for t_i in range(2):
                    segs = [(t_i * 128, None, V_blk[:, t_i, :], 128, KTB, VB),
                            ((t_i + 1) * 128, None, V_blk[:, t_i + 1, :], 128, KTB, VB)]
                    units.append((128, t_i * 128, segs, first and t_i == 0, t_i))
            elif _SMASK & 8:
                for s in range(4):
                    segs = [(None, s, Vst[:, s, :], 128, bf("kTst"), bf("Vst")),
                            (128 + s * 64, None, V_blk[0:64, s, :], 64, KTB, VB)]
                    units.append((64, s * 64, segs, False, s // 2))
            waves = [(u, slot) for u in units for slot in range(2)]
            nW = len(waves)
            SMW = [Buf("smw%d" % wi) for wi in range(nW)]
            mset(att_sm[:], 0.0, SMW + [SM])
            wstate = {}
            po_tiles = {}
            last_wave_of_tile = {}
            for wi, (u, slot) in enumerate(waves):
                last_wave_of_tile[u[4]] = wi

            def kseg(slot, sg):
                c_, st_, _, n, _, _ = sg
                if st_ is not None:
                    return kTstz[slot][:, st_, 0:n]
                return kTz[slot][:, c_:c_ + n]

            def emitS(i):
                (Mq, a, segs, masked, tk), slot = waves[i]
                nk = sum(sg[3] for sg in segs)
                banks = [psum(), psum()]
                wstate[i] = banks
                for j in range(4):
                    h = slot * 4 + j
                    pS, pSb = banks[j // 2]
                    off = (j % 2) * 256
                    mm(pS[0:Mq, off:off + nk], ident_bf[:, 0:Mq], btab[:, h, 0:nk], True, False, [bf("ident_bf"), bf("btab")], [pSb])
                    if masked:
                        mm(pS[0:Mq, off:off + nk], ident_bf[:, 0:Mq], maskf[:, 0:nk], False, False, [bf("ident_bf"), bf("maskf")], [pSb])
                    ko = 0
                    for si, sg in enumerate(segs):
                        n = sg[3]
                        mm(pS[0:Mq, off + ko:off + ko + n], qn[:, j, a:a + Mq], kseg(slot, sg), False, si == len(segs) - 1,
                           [QN, sg[4]], [pSb])
                        ko += n

            def emitSoft(i):
                (Mq, a, segs, masked, tk), slot = waves[i]
                nk = sum(sg[3] for sg in segs)
                set_ = i % 2
                banks = wstate[i]
                PNB = bf("att_pn%d" % set_)
                smw = att_sm[0:Mq, i * 4:(i + 1) * 4, :]
                for j in range(4):
                    h = slot * 4 + j
                    pS, pSb = banks[j // 2]
                    off = (j % 2) * 256
                    hb = set_ * 4 + j
                    act(att_pn[hb][0:Mq, 0:nk], pS[0:Mq, off:off + nk], AF.Exp, [pSb, bf("negsink"), SMW[i]], [PNB, SMW[i]],
                        bias=negsink[0:Mq, h:h + 1], scale=1.0, accum=att_sm[0:Mq, i * 4 + j, 0:1])
                act(smw[:, :, 1:2], smw[:, :, 0:1], AF.Ln, [SMW[i]], [SMW[i]], bias=1.0, scale=1.0)
                act(smw[:, :, 2:3], smw[:, :, 1:2], AF.Exp, [SMW[i]], [SMW[i]], scale=-1.0)
                for j in range(4):
                    hb = set_ * 4 + j
                    ts(att_pn[hb][0:Mq, 0:nk], att_pn[hb][0:Mq, 0:nk], att_sm[0:Mq, i * 4 + j, 2:3], None, ALU.mult, None, [PNB, SMW[i]], [PNB])

            def emitT(i):
                (Mq, a, segs, masked, tk), slot = waves[i]
                set_ = i % 2
                PNB = bf("att_pn%d" % set_)
                PTSB = bf("att_PT%d" % set_)
                for j in range(4):
                    hb = set_ * 4 + j
                    ko = 0
                    for si, sg in enumerate(segs):
                        n = sg[3]
                        tr(psb[0:n, j * 256 + si * 128:j * 256 + si * 128 + Mq], att_pn[hb][0:Mq, ko:ko + n],
                           ident_bf[0:Mq, 0:Mq], [PNB, bf("ident_bf")], [PTB])
                        ko += n
                if Mq == 128 and all(sg[3] == 128 for sg in segs):
                    cpy(att_PT[set_][:, :], psb[:, :], [PTB], [PTSB])
                else:
                    for j in range(4):
                        for si, sg in enumerate(segs):
                            n = sg[3]
                            cpy(att_PT[set_][0:n, j * 256 + si * 128:j * 256 + si * 128 + Mq],
                                psb[0:n, j * 256 + si * 128:j * 256 + si * 128 + Mq], [PTB], [PTSB])

            def emitPV(i):
                (Mq, a, segs, masked, tk), slot = waves[i]
                set_ = i % 2
                PTSB = bf("att_PT%d" % set_)
                if tk not in po_tiles:
                    po_tiles[tk] = psum(pin=True)
                ps_o, pb_o = po_tiles[tk]
                for j in range(4):
                    for si, sg in enumerate(segs):
                        n = sg[3]
                        mm(ps_o[slot * 64:(slot + 1) * 64, j * 128 + (a % 128):j * 128 + (a % 128) + Mq],
                           sg[2][0:n, slot * 64:(slot + 1) * 64], att_PT[set_][0:n, j * 256 + si * 128:j * 256 + si * 128 + Mq],
                           si == 0, si == len(segs) - 1, [sg[5], PTSB], [pb_o])
                if last_wave_of_tile[tk] == i:
                    act(cat[:, 4:8, tk * 128:(tk + 1) * 128], ps_o[:].rearrange("p (j n) -> p j n", j=4), AF.Copy, [pb_o], [CAT])
                    unpin(pb_o)

            for i in range(nW + 2):
                if i < nW:
                    emitS(i)
                if 0 <= i - 1 < nW:
                    emitT(i - 1)
                if 0 <= i - 2 < nW:
                    emitPV(i - 2)
                if i < nW:
                    emitSoft(i)
            W = 320 if sample else 272
            PT0, PT1 = bf("ptmp0"), bf("ptmp1")
            A_, B_ = ptmp[0], ptmp[1]

            def dview(ap2):
                if sample:
                    return ap2.rearrange("p (s n) -> p s n", s=4)[:, :, 16:80]
                return ap2[:, 16:272]

            def oview(ap2):
                if sample:
                    return ap2.rearrange("p (s n) -> p s n", s=4)
                return ap2

            def ptt(out, in0, in1, r, w):
                return P.op("pool", lambda e: e.tensor_tensor(out, in0, in1, ALU.add), reads=r, writes=w)

            for gI, wnd in enumerate((2, 4, 8, 16)):
                u_ = u_ext[:, gI, :]
                ptt(A_[:, 1:W], u_[:, 1:W], u_[:, 0:W - 1], [UE], [PT0])
                cur, curB = A_, PT0
                if wnd >= 4:
                    ptt(B_[:, 3:W], A_[:, 3:W], A_[:, 1:W - 2], [PT0], [PT1])
                    cur, curB = B_, PT1
                if wnd >= 8:
                    ptt(A_[:, 7:W], B_[:, 7:W], B_[:, 3:W - 4], [PT1], [PT0])
                    cur, curB = A_, PT0
                if wnd >= 16:
                    ptt(B_[:, 15:W], A_[:, 15:W], A_[:, 7:W - 8], [PT0], [PT1])
                    cur, curB = B_, PT1
                stt(oview(pooled[:, gI, :]), dview(cur[:, 0:W]), 1.0 / wnd, dview(u_), ALU.mult, ALU.subtract, [curB, UE], [PL])
                if first:
                    tt(ptiny[:, gI, :], cur[:, 16:32], consts[:, K_ICNT + gI * 16:K_ICNT + (gI + 1) * 16], ALU.mult,
                       [curB, CONST], [bf("ptiny")])
                    tt(pooled[:, gI, 0:16], ptiny[:, gI, :], u_[:, 16:32], ALU.subtract, [bf("ptiny"), UE, PL], [PL])
            if not sample:
                cpy(u_c[:], u_ext[:, :, 256:272], [UE], [bf("u_c")])
            for pair in range(2):
                ps, pb = psum()
                for i in range(2):
                    gI = pair * 2 + i
                    mm(ps[:, i * 256:(i + 1) * 256], pw[:, gI, :], pooled[:, gI, :], True, True, [bf("pw"), PL], [pb])
                for i in range(2):
                    gI = pair * 2 + i
                    act(cat[:, gI, :], ps[:, i * 256:(i + 1) * 256], AF.Copy, [pb, COLS], [CAT], scale=col(C_PSC + gI))
            if lastp and (_LMASK & 2):
                rows_out([u_ext[:, gI, 144:272] for gI in range(4)], 112, 16, pool_o[0:128, :], [UE])
            if sample and (_SMASK & 2):
                for s in range(4):
                    w0 = min(max(s * 80 + 80 - 128, 0), 320 - 128)
                    rows_out([u_ext[:, gI, w0:w0 + 128] for gI in range(4)], s * 80 + 64 - w0, 16, pool_o[(1 + s) * 128:(2 + s) * 128, :], [UE])
            wout_residual(cat, CAT, c0)
            if not sample:
                act(kT_c[0:64, :], kTz[0][0:64, 256:384], AF.Copy, [KTB], [bf("kT_c")])
                act(kT_c[64:128, :], kTz[1][64:128, 256:384], AF.Copy, [KTB], [bf("kT_c")])
                act(V_c[:], V_blk[:, 2, :], AF.Copy, [VB], [bf("V_c")])
            deferred_norm.append(lambda c0=c0: norm(c0, c0 + 256, C_GFFN + 0))

        PTAPS = []
        NPT = [0]

        def tap_regions():
            regs = [(wgA[:].rearrange("p a (b c) -> p (a b) c", c=128), [SLA]),
                    (wuA[:].rearrange("p a (b c) -> p (a b) c", c=128), [SLA]),
                    (wdA[:].rearrange("p a (b c) -> p (a b) c", c=128), [SLAd]),
                    (hid[0][:].rearrange("p a (b c) -> p (a b) c", c=128), [bf("hid0")]),
                    (hid[1][:].rearrange("p a (b c) -> p (a b) c", c=128), [bf("hid1")])]
            return regs

        def build_resident_taps():
            del PTAPS[:]
            t0 = 0
            for view, bufs in tap_regions():
                n = min(view.shape[1], 124 - t0)
                if n <= 0:
                    break
                wsrc = cols[:, C_CW + t0:C_CW + t0 + n]
                wbc = bass.AP(tensor=wsrc.tensor, offset=wsrc.offset, ap=[list(x) for x in wsrc.ap] + [[0, 128]])
                tt(view[:, 0:n, :], bcast_mid(ident_bf[:], n), wbc, ALU.mult, [bf("ident_bf"), COLS], bufs)
                for j in range(n):
                    PTAPS.append((view[:, j, :], bufs))
                t0 += n
            NPT[0] = t0

        def mixer1(g, b, kind):
            c0 = b * 256
            tl = [2 * b, 2 * b + 1]
            ha = [HA[t] for t in tl]
            lastp = (g == 1 and b == 3)
            sample = kind == "sample"
            halo = kind == "halo"
            GE, GBF, SIG, CAT1 = bf("glu_ext"), bf("glu_bf"), bf("zg"), bf("cat1")
            hs = h_all[:, :, c0:c0 + 256]
            W = 384 if sample else 288
            if sample:
                mset(glu_ext[:], 0.0, [GE])
                for ci in range(4):
                    act(glu_ext[:, ci, :].rearrange("p (s n) -> p s n", s=4)[:, :, 2:32], clead_s[:, ci, :, :], AF.Copy, [bf("clead_s")], [GE])
            else:
                cpy(glu_ext[:, :, 0:32], glu_c[:], [bf("glu_c")], [GE])
            for pair in range(2):
                psa, pba = psum()
                psg, pbg = psum()
                for (ps_, pb_, base) in ((psa, pba, 0), (psg, pbg, 512)):
                    for i in range(2):
                        m = pair * 2 + i
                        for kc in range(8):
                            mm(ps_[:, i * 256:(i + 1) * 256], wmi[:, kc, base + m * 128:base + (m + 1) * 128], hs[:, kc, :], kc == 0, kc == 7,
                               [WMI] + ha, [pb_])
                act(sig, psg[:].rearrange("p (a n) -> p a n", a=2), AF.Sigmoid, [pbg], [SIG])
                if not sample:
                    tt(glu_ext[:, pair * 2:pair * 2 + 2, 32:288], psa[:].rearrange("p (a n) -> p a n", a=2), sig, ALU.mult,
                       [pba, SIG], [GE])
                else:
                    for i in range(2):
                        m = pair * 2 + i
                        tt(glu_ext[:, m, :].rearrange("p (s n) -> p s n", s=4)[:, :, 32:96],
                           psa[:, i * 256:(i + 1) * 256].rearrange("p (s n) -> p s n", s=4),
                           sig[:, i, :].rearrange("p (s n) -> p s n", s=4), ALU.mult, [pba, SIG], [GE])
            flush_norm()
            if not sample:
                cpy(glu_c[:], glu_ext[:, :, 256:288], [GE], [bf("glu_c")])
            if lastp:
                rows_out([glu_ext[:, ci, 160:288] for ci in range(4)], 96, 32, conv_o[0:128, :], [GE])
            if sample and (_SMASK & 32):
                for s in range(4):
                    w0 = min(max(s * 96 + 96 - 128, 0), 384 - 128)
                    rows_out([glu_ext[:, ci, w0:w0 + 128] for ci in range(4)], s * 96 + 64 - w0, 32, conv_o[(1 + s) * 128:(2 + s) * 128, :], [GE])
            if halo:
                return
            act(glu_bf[:, :, 0:W], glu_ext[:, :, 0:W], AF.Copy, [GE], [GBF])
            CSQ = XSQ
            csq = xsq[:, 0:4, 0:256]
            c_bf = xsq[:, 4:8, 0:256]
            dgc = [0]
            conv_ps = []
            built = {}

            def tap_operand(t):
                if t < NPT[0]:
                    return PTAPS[t]
                t0 = NPT[0] + ((t - NPT[0]) // 16) * 16
                if t0 not in built:
                    nt = min(16, 124 - t0)
                    r_ = dgc[0] % 2
                    dgc[0] += 1
                    DGB = bf("dgb%d" % r_)
                    wsrc = cols[:, C_CW + t0:C_CW + t0 + nt]
                    wbc = bass.AP(tensor=wsrc.tensor, offset=wsrc.offset, ap=[list(x) for x in wsrc.ap] + [[0, 128]])
                    tt(dgb[r_][:, 0:nt, :], bcast_mid(ident_bf[:], nt), wbc, ALU.mult, [bf("ident_bf"), COLS], [DGB])
                    built[t0] = (r_, DGB)
                r_, DGB = built[t0]
                return dgb[r_][:, t - t0, :], [DGB]

            for t_ in range(NPT[0], 124, 16):
                tap_operand(t_)
            ZU = bf("zu")
            for pair in range(2):
                ps, pb = psum()
                for i in range(2):
                    m = pair * 2 + i
                    for kc in range(8):
                        mm(ps[:, i * 256:(i + 1) * 256], wmi[:, kc, 1024 + m * 128:1024 + (m + 1) * 128], hs[:, kc, :], kc == 0, kc == 7,
                           [WMI] + ha, [pb])
                act(zu[:, pair * 2:pair * 2 + 2, :], ps[:].rearrange("p (a n) -> p a n", a=2), AF.Gelu, [pb], [ZU])
            VNB, MVZ = bf("vn_bf"), bf("mvz")
            zgt = [zg[:], ctmp2[:].rearrange("p a n -> p (a n)")]
            ZGt = [[bf("zg")], [bf("ctmp0"), bf("ctmp1")]]
            for t_i in range(2):
                ps, pb = psum()
                for kc in range(8):
                    mm(ps[:], h_all[:, kc, c0 + t_i * 128:c0 + (t_i + 1) * 128], wmi[:, kc, 1536:2048], kc == 0, kc == 7, [WMI] + ha, [pb])
                act(zgt[t_i], ps[:], AF.Gelu, [pb], ZGt[t_i])
            for t_i in range(2):
                dve(lambda e, t_i=t_i: e.bn_stats(st6[:, t_i, 0:6], zgt[t_i]), ZGt[t_i], [bf("st6")])
                dve(lambda e, t_i=t_i: e.bn_aggr(mvz[:, t_i, 0:2], st6[:, t_i, 0:6]), [bf("st6")], [MVZ])
            act(mvz[:, :, 2:3], mvz[:, :, 1:2], AF.Ln, [MVZ], [MVZ], bias=1e-5, scale=1.0)
            act(mvz[:, :, 3:4], mvz[:, :, 2:3], AF.Exp, [MVZ], [MVZ], scale=-0.5)
            for t_i in range(2):
                z_ = zgt[t_i]
                ts(z_, z_, mvz[:, t_i, 0:1], mvz[:, t_i, 3:4], ALU.subtract, ALU.mult, ZGt[t_i] + [MVZ], ZGt[t_i])
                tt(z_, z_, consts[:, K_LNG:K_LNG + 512], ALU.mult, ZGt[t_i] + [CONST], ZGt[t_i])
                tt(z_, z_, consts[:, K_LNB:K_LNB + 512], ALU.add, ZGt[t_i] + [CONST], ZGt[t_i])
                act(vn_bf[:, t_i, :], z_, AF.Copy, ZGt[t_i], [VNB])
                if sample:
                    for half in range(2):
                        s = t_i * 2 + half
                        if _SMASK & 16:
                            dma_scr("sp", gv_o[s * 64:(s + 1) * 64, :], z_[half * 64:(half + 1) * 64, :], ch_misc, ZGt[t_i], [OUT])
            for pair in range(2):
                ps, pb = psum(pin=True)
                for i in range(2):
                    ci = pair * 2 + i
                    for k in range(31):
                        lw, lwb = tap_operand(ci * 31 + k)
                        if not sample:
                            mm(ps[:, i * 256:(i + 1) * 256], lw, glu_bf[:, ci, 2 + k:2 + k + 256], k == 0, k == 30, lwb + [GBF], [pb])
                        else:
                            for s in range(4):
                                P.op("pe", lambda e, o_=ps[:, i * 256 + s * 64:i * 256 + (s + 1) * 64], l_=lw,
                                     r2_=glu_bf[:, ci, s * 96 + 2 + k:s * 96 + 2 + k + 64], st_=(k == 0 and s == 0), sp_=(k == 30 and s == 3):
                                     e.matmul(o_, l_, r2_, start=st_, stop=sp_, skip_group_check=True), reads=lwb + [GBF], writes=[pb])
                for i in range(2):
                    ci = pair * 2 + i
                    act(c_bf[:, ci, :], ps[:, i * 256:(i + 1) * 256], AF.Identity, [pb, COLS], [CSQ], bias=col(C_CB + ci))
                    act(csq[:, ci, :], ps[:, i * 256:(i + 1) * 256], AF.Square, [pb, COLS], [CSQ], bias=col(C_CB + ci))
                conv_ps.append((ps, pb))
            if not sample:
                for pair in range(2):
                    ps, pb = psum()
                    for i in range(2):
                        gg = pair * 2 + i
                        for t_i in range(2):
                            mm(ps[:, i * 256 + t_i * 128:i * 256 + (t_i + 1) * 128], vn_bf[:, t_i, gg * 128:(gg + 1) * 128], wm[:, gg, :],
                               True, True, [VNB, bf("wm")], [pb])
                    for i in range(2):
                        gg = pair * 2 + i
                        GT = bf("ctmp%d" % i)
                        tt(gtmp[i][:].rearrange("p (a n) -> p a n", a=2), ps[:, i * 256:(i + 1) * 256].rearrange("p (a n) -> p a n", a=2),
                           bcast_mid(consts[:, K_GB + gg * 128:K_GB + (gg + 1) * 128], 2), ALU.add, [pb, CONST], [GT])
                        tt(cat1[:, 4 + gg, :], gtmp[i][:], zu[:, gg, :], ALU.mult, [GT, ZU], [CAT1])
            else:
                psH = [psum(pin=True), psum(pin=True)]
                for half in range(2):
                    ps, pb = psH[half]
                    for gg in range(4):
                        for t_i in range(2):
                            o_ = gg * 128 + t_i * 64
                            mm(ps[:, o_:o_ + 64], vn_bf[half * 64:(half + 1) * 64, t_i, gg * 128:(gg + 1) * 128],
                               wms[half * 64:(half + 1) * 64, gg, :], True, True, [VNB, bf("wms")], [pb])
                for gg in range(4):
                    i = gg % 2
                    GT = bf("ctmp%d" % i)
                    for half in range(2):
                        ps, pb = psH[half]
                        tt(gtmp[i][:].rearrange("p (t h n) -> p t h n", t=2, h=2)[:, :, half, :],
                           ps[:, gg * 128:(gg + 1) * 128].rearrange("p (t n) -> p t n", t=2),
                           bcast_mid(consts[:, K_GB + gg * 128:K_GB + gg * 128 + 64], 2), ALU.add, [pb, CONST], [GT])
                    tt(cat1[:, 4 + gg, :], gtmp[i][:], zu[:, gg, :], ALU.mult, [GT, ZU], [CAT1])
                unpin(psH[0][1])
                unpin(psH[1][1])
            MV = bf("mv")
            pst, pbt = psum()
            for ci in range(4):
                mm(pst[:, 0:256], ones_bf[:], c_bf[:, ci, :], ci == 0, ci == 3, [CSQ, bf("ones_bf")], [pbt])
            for ci in range(4):
                mm(pst[:, 256:512], ones_bf[:], csq[:, ci, :], ci == 0, ci == 3, [CSQ, bf("ones_bf")], [pbt])
            act(mv[:, 0, :], pst[:, 0:256], AF.Copy, [pbt], [MV], scale=1.0 / 512)
            tt(mv[:, 2, :], mv[:, 0, :], mv[:, 0, :], ALU.mult, [MV], [MV])
            stt(mv[:, 1, :], pst[:, 256:512], 1.0 / 512, mv[:, 2, :], ALU.mult, ALU.subtract, [pbt, MV], [MV])
            act(mv[:, 1, :], mv[:, 1, :], AF.Ln, [MV], [MV], bias=1e-5, scale=1.0)
            act(mv[:, 1, :], mv[:, 1, :], AF.Exp, [MV], [MV], scale=-0.5)
            for ci in range(4):
                ps, pb = conv_ps[ci // 2]
                i = ci % 2
                CT = bf("ctmp%d" % (ci % 2))
                stt(ctmp[ci % 2][:], ps[:, i * 256:(i + 1) * 256], col(C_CB + ci), mv[:, 0, :], ALU.add, ALU.subtract, [pb, MV, COLS], [CT])
                tt(ctmp[ci % 2][:], ctmp[ci % 2][:], mv[:, 1, :], ALU.mult, [CT, MV], [CT])
                act(cat1[:, ci, :], ctmp[ci % 2][:], AF.Silu, [CT, COLS], [CAT1], bias=col(C_LB + ci), scale=col(C_LG + ci))
                if i == 1:
                    unpin(pb)
            wout_residual(cat1, CAT1, c0)
            deferred_norm.append(lambda c0=c0: norm(c0, c0 + 256, C_GFFN + 8))

        def ffn(layer, blocks, after_block, next_chunk0_loader, mixer_end_ops, bg_loads=()):
            bg = list(bg_loads)
            pending = [None]
            HID = [bf("hid0"), bf("hid1")]
            SSB = [bf("ssb0"), bf("ssb1")]
            it = [0]

            def down(c, blk, hb):
                wg, wu, wd, sb_, ch = chunk_slot(c)
                nf = FCH[c]
                c0_, c1_ = blk
                W = c1_ - c0_
                xr = [XR[t] for t in tiles_of(c0_, c1_)]
                for m in range(8):
                    ps, pb = psum()
                    for fi in range(nf):
                        mm(ps[:, 0:W], wd[:, fi, m * 128:(m + 1) * 128], hid[hb][:, fi, 0:W], fi == 0, fi == nf - 1, [chunk_slot_d(c)[0], HID[hb]], [pb])
                    tt(x_res[:, m, c0_:c1_], ps[:, 0:W], x_res[:, m, c0_:c1_], ALU.add, [pb] + xr, xr)
                if c == len(FCH) - 1:
                    after_block(blk)

            load_chunk(layer, 1, extra=mixer_end_ops)
            for c in range(len(FCH)):
                wg, wu, wd, sb_, ch = chunk_slot(c)
                nf = FCH[c]
                for blk in blocks:
                    c0_, c1_ = blk
                    W = c1_ - c0_
                    ha = [HA[t] for t in tiles_of(c0_, c1_)]
                    hb = it[0] % 2
                    it[0] += 1
                    for fi in range(nf):
                        psg, pbg = psum()
                        for kc in range(8):
                            mm(psg[:, 0:W], wg[:, kc, fi * 128:(fi + 1) * 128], h_all[:, kc, c0_:c1_], kc == 0, kc == 7, [sb_] + ha, [pbg])
                        psu, pbu = psum()
                        for kc in range(8):
                            mm(psu[:, 0:W], wu[:, kc, fi * 128:(fi + 1) * 128], h_all[:, kc, c0_:c1_], kc == 0, kc == 7, [sb_] + ha, [pbu])
                        k_ = fi % 2
                        act(s_sb[k_][:, 0:W], psg[:, 0:W], AF.Silu, [pbg], [SSB[k_]])
                        tt(hid[hb][:, fi, 0:W], psu[:, 0:W], s_sb[k_][:, 0:W], ALU.mult, [pbu, SSB[k_]], [HID[hb]])
                    if pending[0] is not None:
                        down(*pending[0])
                    pending[0] = (c, blk, hb)
                if c + 2 < len(FCH):
                    if pending[0] is not None:
                        down(*pending[0])
                        pending[0] = None
                    load_chunk(layer, c + 2)
                    for _ in range(4):
                        if bg:
                            bg.pop(0)()
            if pending[0] is not None:
                down(*pending[0])
                pending[0] = None
            while bg:
                bg.pop(0)()
            if next_chunk0_loader is not None:
                next_chunk0_loader()

        setup()
        if stg >= 1:
            load_mixer_weights(0)
            load_chunk(0, 0)

        kinds = [["halo", "prompt", "prompt", "prompt", "prompt"], ["prompt", "prompt", "prompt", "prompt", "sample"]]
        ffn_blocks = [
            [[(128, 256), (256, 768), (768, 1280)], [(256, 768), (768, 1280)]],
            [[(0, 512), (512, 1024), (1024, 1280)], [(0, 512), (512, 1024), (1024, 1280)]],
        ]

        for g in range(0 if stg < 1 else (2 if stg >= 6 else 1)):
            gstg = stg if (g == 0 or stg >= 99) else {6: 2, 7: 2, 8: 4, 9: 4, 10: 1, 11: 2}[stg]
            nb0 = 4 if (g == 1 and stg == 6) else (3 if (g == 1 and stg == 11) else 5)
            nb1 = 4 if (g == 1 and stg == 8) else 5
            for b in range(5):
                for t in (2 * b, 2 * b + 1):
                    load_tile(g, t)
                norm(b * 256, b * 256 + 256, C_GMIX + 0)
            if gstg <= 1:
                for t in range(2, 10):
                    store_tile(g, t)
                continue
            for e_ in ("act", "dve", "pe"):
                fence(e_, list(SLB.rs.values()) + list(SLB.ws.values()) + list(SLBd.rs.values()) + list(SLBd.ws.values()))
            for b in range(nb0):
                mixer0(g, b, kinds[g][b])
            flush_norm()
            mix_end = [P.last.get(e_) for e_ in ("pe", "act", "dve")] + scr_dma[-1:]
            if gstg <= 2:
                for t in range(2, 10):
                    store_tile(g, t)
                continue
            load_mixer_weights(1)

            def after_l0(blk):
                norm(blk[0], blk[1], C_GMIX + 8)

            ffn(0, ffn_blocks[g][0], after_l0, None, mix_end)
            if gstg <= 3:
                for t in range(2, 10):
                    store_tile(g, t)
                continue
            for e_ in ("act", "dve", "pe"):
                fence(e_, list(SLB.rs.values()) + list(SLB.ws.values()) + list(SLBd.rs.values()) + list(SLBd.ws.values()))
            build_resident_taps()
            for b in range(nb1):
                mixer1(g, b, kinds[g][b])
            flush_norm()
            load_chunk(1, 0)
            mix_end = [P.last.get(e_) for e_ in ("pe", "act", "dve")] + scr_dma[-1:]
            if gstg <= 4:
                for t in range(2, 10):
                    store_tile(g, t)
                continue
            bg_jobs = load_mixer_weights(0, defer=True) if g == 0 else []

            def after_l1(blk, g=g):
                for t in tiles_of(blk[0], blk[1]):
                    store_tile(g, t)

            ffn(1, ffn_blocks[g][1], after_l1, (lambda: load_chunk(0, 0)) if g == 0 else None, mix_end, bg_jobs)

        P.op("sp", None, reads=[OUT], extra=list(chan_last.values()))
        P.lower(block, sems)
    return nc


def _host_consts(core):
    c = np.zeros((128, NCONST), np.float32)
    c[:, K_ID:K_ID + 128] = np.eye(128, dtype=np.float32)
    j = np.arange(128)[:, None]
    i = np.arange(128)[None, :]
    c[:, K_TRIU:K_TRIU + 128] = (j <= i).astype(np.float32)
    q = np.arange(128)[:, None]
    s = np.arange(256)[None, :]
    dist = np.abs(128 + q - s).astype(np.float32)
    qc = q // 64
    sc = s // 64
    allowed = (sc >= qc) & (sc <= qc + 2)
    nd = np.where(allowed, -dist, -1.0e6).astype(np.float32)
    c[:, K_ND:K_ND + 256] = nd
    mk = np.zeros((128, 256), np.float32)
    if core == 0:
        mk[:, 0:128] = -30000.0
    c[:, K_NDF:K_NDF + 256] = mk
    for gI, w in enumerate((2, 4, 8, 16)):
        pos = np.arange(16)
        cntv = np.minimum(pos + 1, w) if core == 0 else np.full(16, w)
        c[:, K_ICNT + gI * 16:K_ICNT + (gI + 1) * 16] = (1.0 / cntv.astype(np.float32))[None, :]
    bd = np.zeros((128, 128), np.float32)
    bd[0:64, 0:64] = 1.0
    bd[64:128, 64:128] = 1.0
    c[:, K_BD:K_BD + 128] = bd
    j64 = (np.arange(128) % 64)[:, None]
    i64 = np.arange(64)[None, :]
    c[:, K_TRIU64:K_TRIU64 + 64] = (j64 <= i64).astype(np.float32)
    return c


def kernel(x_prompt, x_sample, state_pool, state_swa_k, state_swa_v, state_conv,
           norm_mix, norm_ffn, w_in_even, q_norm, k_norm, attn_sinks, pool_w, pool_scale,
           w_out_even, w_in_odd, conv_w, conv_b, conv_ln_g, conv_ln_b, gmlp_ln_g, gmlp_ln_b,
           gmlp_w, gmlp_b, w_out_odd, ffn_gate, ffn_up, ffn_down):
    f = lambda a: np.ascontiguousarray(np.asarray(a, dtype=np.float32))
    xp = f(x_prompt)[0]
    xsm = f(x_sample).reshape(32 * 64, 1024)
    cols = np.zeros((128, NCOL), np.float32)
    nm, nf_ = f(norm_mix), f(norm_ffn)
    for l in range(2):
        cols[:, C_GMIX + 8 * l:C_GMIX + 8 * l + 8] = nm[l].reshape(8, 128).T
        cols[:, C_GFFN + 8 * l:C_GFFN + 8 * l + 8] = nf_[l].reshape(8, 128).T
    cols[:, C_PSC:C_PSC + 4] = f(pool_scale)[0].reshape(4, 128).T
    cols[:, C_GQ] = np.tile(f(q_norm)[0], 2)
    cols[:, C_GK] = np.tile(f(k_norm)[0], 2)
    cols[:, C_SINK:C_SINK + 8] = np.broadcast_to(f(attn_sinks)[0][None, :], (128, 8))
    cols[:, C_CB:C_CB + 4] = f(conv_b)[0].reshape(4, 128).T
    cols[:, C_LG:C_LG + 4] = f(conv_ln_g)[0].reshape(4, 128).T
    cols[:, C_LB:C_LB + 4] = f(conv_ln_b)[0].reshape(4, 128).T
    cw = f(conv_w)[0]
    cols[:, C_CW:C_CW + 124] = cw.reshape(31, 4, 128).transpose(2, 1, 0).reshape(128, 124)
    lng = np.broadcast_to(f(gmlp_ln_g)[0][None, :], (128, 512))
    lnb = np.broadcast_to(f(gmlp_ln_b)[0][None, :], (128, 512))
    gb = np.broadcast_to(f(gmlp_b)[0].reshape(1, 512), (128, 512))
    gwT = np.ascontiguousarray(f(gmlp_w)[0].transpose(0, 2, 1)).reshape(512, 128)
    shared = {
        "cols": cols,
        "w_in0": f(w_in_even)[0], "w_out0": f(w_out_even)[0], "pool_w": f(pool_w)[0].reshape(512, 128),
        "w_in1": f(w_in_odd)[0], "w_out1": f(w_out_odd)[0], "gwT": gwT,
        "fgate": f(ffn_gate).reshape(2048, 2816), "fup": f(ffn_up).reshape(2048, 2816), "fdown": f(ffn_down).reshape(5632, 1024),
    }
    sp, sk, sv, scv = f(state_pool)[0], f(state_swa_k)[0], f(state_swa_v)[0], f(state_conv)[0]
    in_maps = []
    for c in range(N_CORES):
        xin = np.zeros((2560, 1024), np.float32)
        if c > 0:
            xin[0:256] = xp[2048 * c - 256:2048 * c]
        xin[256:2304] = xp[2048 * c:2048 * (c + 1)]
        xin[2304:2560] = xsm[256 * c:256 * (c + 1)]
        cst = _host_consts(c)
        cst[:, K_LNG:K_LNG + 512] = lng
        cst[:, K_LNB:K_LNB + 512] = lnb
        cst[:, K_GB:K_GB + 512] = gb
        m = dict(shared)
        m.update({
            "xin": xin, "consts": cst,
            "st_pool": np.ascontiguousarray(sp[4 * c:4 * c + 4].reshape(60, 512)),
            "st_k": np.ascontiguousarray(sk[4 * c:4 * c + 4].reshape(512, 128)),
            "st_v": np.ascontiguousarray(sv[4 * c:4 * c + 4].reshape(512, 128)),
            "st_conv": np.ascontiguousarray(scv[4 * c:4 * c + 4].reshape(120, 512)),
        })
        in_maps.append(m)
    nc = build_nc(_STAGE)
    res = run_bass_kernel_spmd(nc, in_maps, core_ids=list(range(N_CORES)))
    if _STAGE < 99:
        return res.results
    R = res.results
    y_prompt = np.concatenate([R[c]["yout"][0:2048] for c in range(N_CORES)], 0)[None]
    y_sample = np.concatenate([R[c]["yout"][2048:2304] for c in range(N_CORES)], 0).reshape(32, 64, 1024)
    def _rows(a, n):
        a = a.reshape(5, 128, 512)
        out = [a[0, 128 - n:128]]
        for s_ in range(4):
            w = 320 if n == 16 else 384
            per = 80 if n == 16 else 96
            w0 = min(max(s_ * per + per - 128, 0), w - 128)
            r0 = s_ * per + 64 - w0
            out.append(a[1 + s_, r0:r0 + n])
        return np.stack(out, 0)

    po = [_rows(R[c]["pool_o"], 16) for c in range(N_CORES)]
    ko = [R[c]["k_o"].reshape(5, 128, 2, 64) for c in range(N_CORES)]
    vo = [R[c]["v_o"].reshape(5, 128, 2, 64) for c in range(N_CORES)]
    co = [_rows(R[c]["conv_o"], 32) for c in range(N_CORES)]
    gvo = [R[c]["gv_o"].reshape(4, 64, 512) for c in range(N_CORES)]
    pool_prompt = po[7][0:1, 1:16][None]
    pool_sample = np.concatenate([p[1:5, 1:16] for p in po], 0)[None]
    k_prompt = ko[7][0:1][None]
    k_sample = np.concatenate([k[1:5] for k in ko], 0)[None]
    v_prompt = vo[7][0:1][None]
    v_sample = np.concatenate([v[1:5] for v in vo], 0)[None]
    conv_prompt = co[7][0:1, 2:32][None]
    conv_sample = np.concatenate([x[1:5, 2:32] for x in co], 0)[None]
    gv_sample = np.concatenate(gvo, 0)[None]
    outs = (y_prompt, y_sample, pool_prompt, pool_sample, k_prompt, k_sample, v_prompt, v_sample,
            conv_prompt, conv_sample, gv_sample)
    return tuple(np.ascontiguousarray(o, dtype=np.float32) for o in outs)
```

```python
import numpy as np
from contextlib import ExitStack
import concourse.bass as bass
import concourse.mybir as mybir
from concourse.bass_utils import run_bass_kernel_spmd

F32 = mybir.dt.float32
BF16 = mybir.dt.bfloat16
ALU = mybir.AluOpType
AF = mybir.ActivationFunctionType
AX = mybir.AxisListType

SAME_ENGINE_SYNC = True
N_CORES = 8
SBUF_BASE = 16512
SBUF_LIMIT = 229376

C_GMIX = 0
C_GFFN = 16
C_PSC = 32
C_GQ = 36
C_GK = 37
C_SINK = 38
C_CB = 46
C_LG = 50
C_LB = 54
C_CW = 58
NCOL = 58 + 124
K_ID = 0
K_TRIU = 128
K_ND = 256
K_NDF = 512
K_ICNT = 768
K_LNG = 832
K_LNB = 1344
K_GB = 1856
K_BD = 2368
K_TRIU64 = 2496
NCONST = 2560

_STAGE = 99
_LMASK = 7
_SMASK = 255
FCH = [3, 3, 3, 3, 3, 3, 3, 1]


class Buf:
    __slots__ = ("name", "ws", "rs")

    def __init__(self, name=""):
        self.name = name
        self.ws = {}
        self.rs = {}


class Chan:
    def __init__(self, sem, wait_all=False):
        self.sem = sem
        self.n = 0
        self.wait_all = wait_all


class Op:
    __slots__ = ("eng", "fn", "deps", "signal", "cnt", "chan", "chan_cnt", "idx")


def _key(o):
    return o.eng if o.chan is None else ("c", id(o.chan))


class Prog:
    ENGS = ("pe", "act", "dve", "pool", "sp")

    def __init__(self):
        self.ops = {e: [] for e in self.ENGS}
        self.n = 0
        self.last = {}

    def op(self, eng, fn, reads=(), writes=(), chan=None, extra=()):
        o = Op()
        o.eng = eng
        o.fn = fn
        o.signal = False
        o.cnt = 0
        o.chan = chan
        o.idx = self.n
        self.n += 1
        if chan is not None:
            chan.n += 1
            o.chan_cnt = chan.n
        else:
            o.chan_cnt = 0
        deps = {}

        def add(p, raw):
            if p is o:
                return
            if p.chan is None and o.chan is None and p.eng == eng:
                if eng == "pe" or not SAME_ENGINE_SYNC or (not raw and eng != "pool"):
                    return
            k = _key(p)
            q = deps.get(k)
            if q is None or q.idx < p.idx:
                deps[k] = p

        for b in reads:
            for w in b.ws.values():
                add(w, True)
        for b in writes:
            for r in b.rs.values():
                add(r, False)
            for w in b.ws.values():
                add(w, False)
        for p in extra:
            if p is not None:
                add(p, True)
        k = _key(o)
        for b in reads:
            b.rs[k] = o
        for b in writes:
            if b.rs:
                b.ws = {}
                b.rs = {}
            b.ws[k] = o
        o.deps = list(deps.values())
        for p in o.deps:
            if p.chan is None:
                p.signal = True
        self.ops[eng].append(o)
        if chan is None and fn is not None:
            self.last[eng] = o
        return o

    def lower(self, block, sems):
        for e in self.ENGS:
            c = 0
            for o in self.ops[e]:
                if o.chan is None and o.signal:
                    c += 1
                    o.cnt = c

        def run(ename):
            def body(eng):
                waited = {}
                for o in self.ops[ename]:
                    need = {}
                    for p in o.deps:
                        if p.chan is not None:
                            s = p.chan.sem
                            v = 16 * (p.chan.n if p.chan.wait_all else p.chan_cnt)
                        else:
                            s, v = sems[p.eng], p.cnt
                        k = id(s)
                        if need.get(k, (None, 0))[1] < v:
                            need[k] = (s, v)
                    for k, (s, v) in need.items():
                        if waited.get(k, 0) >= v:
                            continue
                        waited[k] = v
                        eng.wait_ge(s, v)
                    if o.fn is None:
                        continue
                    ins = o.fn(eng)
                    if o.chan is not None:
                        ins.then_inc(o.chan.sem, 16)
                    elif o.signal:
                        ins.then_inc(sems[ename], 1)
            return body

        block.tensor(run("pe"))
        block.scalar(run("act"))
        block.vector(run("dve"))
        block.gpsimd(run("pool"))
        block.sync(run("sp"))


def bcast_mid(ap, n):
    a = ap.ap
    return bass.AP(tensor=ap.tensor, offset=ap.offset, ap=[list(a[0]), [0, n]] + [list(x) for x in a[1:]])


def build_nc(stg=99):
    nc = bass.Bass("TRN2", target_bir_lowering=False)

    def din(name, shape):
        return nc.dram_tensor(name, shape, F32, kind="ExternalInput").ap()

    def dout(name, shape):
        return nc.dram_tensor(name, shape, F32, kind="ExternalOutput").ap()

    xin = din("xin", [2560, 1024])
    cols_d = din("cols", [128, NCOL])
    consts_d = din("consts", [128, NCONST])
    w_in0_d = din("w_in0", [1024, 1280])
    w_out0_d = din("w_out0", [1024, 1024])
    pool_w_d = din("pool_w", [512, 128])
    w_in1_d = din("w_in1", [1024, 2048])
    w_out1_d = din("w_out1", [1024, 1024])
    gwT_d = din("gwT", [512, 128])
    fgate_d = din("fgate", [2048, 2816])
    fup_d = din("fup", [2048, 2816])
    fdown_d = din("fdown", [5632, 1024])
    st_pool_d = din("st_pool", [60, 512])
    st_k_d = din("st_k", [512, 128])
    st_v_d = din("st_v", [512, 128])
    st_conv_d = din("st_conv", [120, 512])
    yout = dout("yout", [2304, 1024])
    pool_o = dout("pool_o", [5 * 128, 512])
    k_o = dout("k_o", [5 * 128, 128])
    v_o = dout("v_o", [5 * 128, 128])
    conv_o = dout("conv_o", [5 * 128, 512])
    gv_o = dout("gv_o", [4 * 64, 512])

    cnt = [0]

    class Region:
        def __init__(self, start, limit):
            self.off = start
            self.limit = limit

        def alloc(self, shape, dt):
            size = 1
            for s in shape[1:]:
                size *= s
            size *= 4 if dt == F32 else 2
            size = (size + 63) // 64 * 64
            at = self.off
            self.off += size
            assert self.off <= self.limit, ("SBUF overflow", self.off, self.limit)
            cnt[0] += 1
            return nc.alloc_sbuf_tensor_at("sb%d" % cnt[0], list(shape), dt, offset=at)

    perm = Region(SBUF_BASE, SBUF_LIMIT)
    x_res = perm.alloc([128, 8, 1280], F32)
    h_all = perm.alloc([128, 8, 1280], BF16)
    wmi = perm.alloc([128, 8, 2048], BF16)
    wmo = perm.alloc([128, 8, 1024], BF16)
    wgA = perm.alloc([128, 8, 384], BF16)
    wuA = perm.alloc([128, 8, 384], BF16)
    wdA = perm.alloc([128, 3, 1024], BF16)
    xs = [perm.alloc([128, 512], F32) for _ in range(2)]
    cols = perm.alloc([128, NCOL], F32)
    consts = perm.alloc([128, NCONST], F32)
    pw = perm.alloc([128, 4, 128], BF16)
    wm = perm.alloc([128, 4, 128], BF16)
    wms = perm.alloc([128, 4, 64], BF16)
    ident_bf = perm.alloc([128, 128], BF16)
    ones_bf = perm.alloc([128, 128], BF16)
    bd_bf = perm.alloc([128, 128], BF16)
    kTstz = [perm.alloc([128, 4, 128], BF16) for _ in range(2)]
    btab = perm.alloc([128, 8, 256], BF16)
    maskf = perm.alloc([128, 256], BF16)
    negsink = perm.alloc([128, 8], F32)
    Vst = perm.alloc([128, 4, 128], BF16)
    ulead_s = perm.alloc([128, 4, 4, 15], F32)
    clead_s = perm.alloc([128, 4, 4, 30], F32)
    gq8 = perm.alloc([128, 2], F32)
    kT_c = perm.alloc([128, 128], BF16)
    V_c = perm.alloc([128, 128], BF16)
    u_c = perm.alloc([128, 4, 16], F32)
    glu_c = perm.alloc([128, 4, 32], F32)
    xsq = perm.alloc([128, 8, 512], BF16)
    rs = perm.alloc([128, 512], F32)
    s_sb = [perm.alloc([128, 512], F32) for _ in range(2)]
    hid = [perm.alloc([128, 3, 512], BF16) for _ in range(2)]
    SCR = perm.off
    rb = Region(SCR, SBUF_LIMIT)
    wgB = rb.alloc([128, 8, 384], BF16)
    wuB = rb.alloc([128, 8, 384], BF16)
    wdB = rb.alloc([128, 3, 1024], BF16)
    r0 = Region(SCR, SBUF_LIMIT)
    u_ext = r0.alloc([128, 4, 320], F32)
    ptmp = [r0.alloc([128, 320], F32) for _ in range(2)]
    pooled = r0.alloc([128, 4, 256], BF16)
    cat = r0.alloc([128, 8, 256], BF16)
    rq2 = [r0.alloc([128, 2, 256], F32) for _ in range(2)]
    qn = r0.alloc([128, 4, 256], BF16)
    kn = r0.alloc([128, 256], F32)
    kTz = [r0.alloc([128, 384], BF16) for _ in range(2)]
    V_blk = r0.alloc([128, 4, 128], BF16)
    att_pn = [r0.alloc([128, 256], BF16) for _ in range(8)]
    att_PT = [r0.alloc([128, 1024], BF16) for _ in range(2)]
    att_sm = r0.alloc([128, 32, 4], F32)
    ptiny = r0.alloc([128, 4, 16], F32)
    r1 = Region(SCR, SBUF_LIMIT)
    glu_ext = r1.alloc([128, 4, 384], F32)
    glu_bf = r1.alloc([128, 4, 384], BF16)
    dgb = [r1.alloc([128, 16, 128], BF16) for _ in range(2)]
    mv = r1.alloc([128, 3, 256], F32)
    ctmp2 = r1.alloc([128, 2, 256], F32)
    ctmp = [ctmp2[:, 0, :], ctmp2[:, 1, :]]
    zu = r1.alloc([128, 4, 256], BF16)
    zg = r1.alloc([128, 512], F32)
    sig = zg[:].rearrange("p (a n) -> p a n", a=2)
    vn_bf = r1.alloc([128, 2, 512], BF16)
    cat1 = r1.alloc([128, 8, 256], BF16)
    st6 = r1.alloc([128, 2, 8], F32)
    mvz = r1.alloc([128, 2, 4], F32)
    gtmp = ctmp
    qsq = xsq

    psf = [nc.alloc_psum_tensor("psf%d" % i, [128, 512], F32) for i in range(7)]
    psb = nc.alloc_psum_tensor("psb", [128, 1024], BF16)

    with ExitStack() as es:
        def sem(name):
            return es.enter_context(nc.semaphore(name))

        sems = {e: sem("s_" + e) for e in ("pe", "act", "dve", "pool")}
        ch_setup = Chan(sem("c_setup"), wait_all=True)
        ch_setup2 = Chan(sem("c_setup2"), wait_all=True)
        ch_wmi = Chan(sem("c_wmi"))
        ch_wmo = Chan(sem("c_wmo"))
        ch_fA = Chan(sem("c_fA"))
        ch_fB = Chan(sem("c_fB"))
        ch_fAd = Chan(sem("c_fAd"))
        ch_fBd = Chan(sem("c_fBd"))
        ch_xin = [Chan(sem("c_xin%d" % i)) for i in range(2)]
        ch_out = [Chan(sem("c_out%d" % i)) for i in range(2)]
        ch_misc = Chan(sem("c_misc"))
        block = es.enter_context(nc.Block())
        P = Prog()

        XR = [Buf("xr%d" % t) for t in range(10)]
        HA = [Buf("ha%d" % t) for t in range(10)]
        B = {}

        def bf(name):
            if name not in B:
                B[name] = Buf(name)
            return B[name]

        PSB = [Buf("ps%d" % i) for i in range(7)]
        PTB = Buf("ptb")
        XS = [Buf("xs0"), Buf("xs1")]
        OUT = Buf("out")
        ps_rr = [0]
        xs_rr = [0]

        pinned = set()

        def psum(pin=False):
            while True:
                i = ps_rr[0] % 7
                ps_rr[0] += 1
                if i not in pinned:
                    break
            if pin:
                pinned.add(i)
            return psf[i], PSB[i]

        def unpin(pb):
            pinned.discard(PSB.index(pb))

        def stage():
            i = xs_rr[0] % 2
            xs_rr[0] += 1
            return xs[i], XS[i], i

        def mm(out, lhsT, rhs, start, stop, r, w):
            return P.op("pe", lambda e: e.matmul(out, lhsT, rhs, start=start, stop=stop), reads=r, writes=w)

        def tr(out, in_, ident, r, w):
            return P.op("pe", lambda e: e.transpose(out, in_, ident), reads=r, writes=w)

        def act(out, in_, func, r, w, bias=None, scale=None, accum=None):
            kw = {}
            if bias is not None:
                kw["bias"] = bias
            if scale is not None:
                kw["scale"] = scale
            if accum is not None:
                kw["accum_out"] = accum
            return P.op("act", lambda e: e.activation(out=out, in_=in_, func=func, **kw), reads=r, writes=w)

        def dve(fn, r, w):
            return P.op("dve", fn, reads=r, writes=w)

        def tt(out, in0, in1, op, r, w):
            return dve(lambda e: e.tensor_tensor(out, in0, in1, op), r, w)

        def stt(out, in0, scalar, in1, op0, op1, r, w):
            return dve(lambda e: e.scalar_tensor_tensor(out, in0, scalar, in1, op0, op1), r, w)

        def ts(out, in0, s1, s2, op0, op1, r, w):
            if op1 is None:
                return dve(lambda e: e.tensor_scalar(out, in0, s1, s2, op0), r, w)
            return dve(lambda e: e.tensor_scalar(out, in0, s1, s2, op0, op1), r, w)

        def cpy(out, in_, r, w):
            return dve(lambda e: e.tensor_copy(out, in_), r, w)

        def recip(out, in_, r, w):
            return dve(lambda e: e.reciprocal(out, in_), r, w)

        def mset(ap, v, w):
            return dve(lambda e: e.memset(ap, v), [], w)

        chan_last = {}

        def dma(q, out, in_, chan, r, w):
            o = P.op(q, lambda e: e.dma_start(out=out, in_=in_), reads=r, writes=w, chan=chan)
            chan_last[id(chan)] = o
            return o

        def fence(eng, deps):
            P.op(eng, None, extra=deps)

        scr_dma = []

        def dma_scr(q, out, in_, chan, r, w):
            scr_dma.append(dma(q, out, in_, chan, r, w))

        CONST = bf("consts")
        COLS = bf("cols")

        def col(i):
            return cols[:, i:i + 1]

        ident_f = consts[:, K_ID:K_ID + 128]

        def setup():
            dma("sp", cols[:], cols_d[:, :], ch_setup, [], [COLS])
            dma("sp", consts[:], consts_d[:, :], ch_setup, [], [CONST])
            act(ident_bf[:], ident_f, AF.Copy, [CONST], [bf("ident_bf")])
            act(bd_bf[:], consts[:, K_BD:K_BD + 128], AF.Copy, [CONST], [bf("bd_bf")])
            mset(ones_bf[:], 1.0, [bf("ones_bf")])
            ts(gq8[:, 0:1], col(C_GQ), 0.125, None, ALU.mult, None, [COLS], [bf("gq8")])
            mset(kT_c[:], 0.0, [bf("kT_c")])
            for h in range(8):
                ts(btab[:, h, :], consts[:, K_ND:K_ND + 256], 2.0 ** (-(h + 1)), None, ALU.mult, None, [CONST], [bf("btab")])
            act(maskf[:], consts[:, K_NDF:K_NDF + 256], AF.Copy, [CONST], [bf("maskf")])
            ts(negsink[:], cols[:, C_SINK:C_SINK + 8], -1.0, None, ALU.mult, None, [COLS], [bf("negsink")])
            mset(V_c[:], 0.0, [bf("V_c")])
            mset(u_c[:], 0.0, [bf("u_c")])
            mset(glu_c[:], 0.0, [bf("glu_c")])
            if stg == -1:
                return
            dma("pool", pw[:], pool_w_d.rearrange("(g c) d -> c g d", c=128), ch_setup2, [], [bf("pw")])
            dma("pool", Vst[:], st_v_d.rearrange("(s t) c -> t s c", t=128), ch_setup2, [], [bf("Vst")])
            if stg == -2:
                return
            s0, sb0, i0 = stage()
            dma("sp", s0[:, 0:512].rearrange("p (g i) -> p g i", g=4), gwT_d.rearrange("(g j) i -> j g i", j=128), ch_xin[i0], [], [sb0])
            tt(wm[:], s0[:, 0:512].rearrange("p (g i) -> p g i", g=4), bcast_mid(consts[:, K_TRIU:K_TRIU + 128], 4), ALU.mult,
               [sb0, CONST], [bf("wm")])
            gv = gwT_d.rearrange("(g j) i -> j g i", j=128)
            s0b, sb0b, i0b = stage()
            dma("sp", s0b[0:64, 0:256].rearrange("p (g i) -> p g i", g=4), gv[0:64, :, 0:64], ch_xin[i0b], [], [sb0b])
            dma("sp", s0b[64:128, 0:256].rearrange("p (g i) -> p g i", g=4), gv[0:64, :, 0:64], ch_xin[i0b], [], [sb0b])
            tt(wms[:], s0b[:, 0:256].rearrange("p (g i) -> p g i", g=4), bcast_mid(consts[:, K_TRIU64:K_TRIU64 + 64], 4), ALU.mult,
               [sb0b, CONST], [bf("wms")])
            if stg == -3:
                return
            s1, sb1, i1 = stage()
            dma("sp", s1[:, 0:512].rearrange("p (s c) -> p s c", s=4), st_k_d.rearrange("(s t) c -> t s c", t=128), ch_xin[i1], [], [sb1])
            ps, pb = psum()
            for s in range(4):
                tr(ps[:, s * 128:(s + 1) * 128], s1[:, s * 128:(s + 1) * 128], ident_f, [sb1, CONST], [pb])
            mset(kTstz[0][:], 0.0, [bf("kTst")])
            mset(kTstz[1][:], 0.0, [bf("kTst")])
            act(kTstz[0][0:64, :, :], ps[0:64, :].rearrange("p (s t) -> p s t", s=4), AF.Copy, [pb], [bf("kTst")])
            act(kTstz[1][64:128, :, :], ps[64:128, :].rearrange("p (s t) -> p s t", s=4), AF.Copy, [pb], [bf("kTst")])
            for s in range(4):
                dma("sp", k_o[(1 + s) * 128:(1 + s) * 128 + 64, :], s1[64:128, s * 128:(s + 1) * 128], ch_out[i1], [sb1], [OUT])
            if stg == -4:
                return
            s2, sb2, i2 = stage()
            dma("sp", s2[0:60, 0:512], st_pool_d[:, :], ch_xin[i2], [], [sb2])
            ps, pb = psum()
            for g in range(4):
                tr(ps[:, g * 64:g * 64 + 60], s2[0:60, g * 128:(g + 1) * 128], consts[0:60, K_ID:K_ID + 60], [sb2, CONST], [pb])
            act(ulead_s[:], ps[:, 0:256].rearrange("p (g n) -> p g n", g=4)[:, :, 0:60].rearrange("p g (s r) -> p g s r", s=4),
                AF.Copy, [pb], [bf("ulead_s")])
            if stg == -5:
                return
            s3, sb3, i3 = stage()
            dma("sp", s3[0:120, 0:512], st_conv_d[:, :], ch_xin[i3], [], [sb3])
            ps, pb = psum()
            for g in range(4):
                tr(ps[:, g * 128:g * 128 + 120], s3[0:120, g * 128:(g + 1) * 128], consts[0:120, K_ID:K_ID + 120], [sb3, CONST], [pb])
            act(clead_s[:], ps[:].rearrange("p (g n) -> p g n", g=4)[:, :, 0:120].rearrange("p g (s r) -> p g s r", s=4),
                AF.Copy, [pb], [bf("clead_s")])
            if stg == -6:
                return
            s4, sb4, i4 = stage()
            dma("sp", s4[:, 0:512].rearrange("p (s c) -> p s c", s=4), st_v_d.rearrange("(s t) c -> t s c", t=128), ch_xin[i4], [], [sb4])
            for s in range(4):
                dma("sp", v_o[(1 + s) * 128:(1 + s) * 128 + 64, :], s4[64:128, s * 128:(s + 1) * 128], ch_out[i4], [sb4], [OUT])

        WMI = bf("wmi")
        WMO = bf("wmo")
        SLA = bf("slotA")
        SLB = bf("slotB")
        SLAd = bf("slotAd")
        SLBd = bf("slotBd")

        def load_mixer_weights(layer, defer=False):
            jobs = []

            def dq(*args):
                jobs.append(lambda: dma(*args))

            if layer == 0:
                v = w_in0_d.rearrange("(kc p) n -> p kc n", p=128)
                dq("pool", wmi[:, :, 0:512], v[:, :, 0:512], ch_wmi, [], [WMI])
                for j in range(4):
                    for slot in range(2):
                        h = slot * 4 + j
                        dq("pool", wmi[:, :, 512 + j * 128 + slot * 64:512 + j * 128 + slot * 64 + 64],
                            v[:, :, 512 + h * 64:512 + h * 64 + 64], ch_wmi, [], [WMI])
                dq("pool", wmi[:, :, 1024:1280], v[:, :, 1024:1280], ch_wmi, [], [WMI])
                vo = w_out0_d
                dq("pool", wmo[:, 0:4, :], vo[0:512, :].rearrange("(kc p) n -> p kc n", p=128), ch_wmo, [], [WMO])
                for j in range(4):
                    for slot in range(2):
                        h = slot * 4 + j
                        dq("pool", wmo[slot * 64:(slot + 1) * 64, 4 + j, :], vo[512 + h * 64:512 + h * 64 + 64, :], ch_wmo, [], [WMO])
            else:
                v = w_in1_d.rearrange("(kc p) n -> p kc n", p=128)
                dq("pool", wmi[:, :, 0:1024], v[:, :, 0:1024], ch_wmi, [], [WMI])
                dq("pool", wmi[:, :, 1024:2048], v[:, :, 1024:2048], ch_wmi, [], [WMI])
                dq("pool", wmo[:], w_out1_d.rearrange("(kc p) n -> p kc n", p=128), ch_wmo, [], [WMO])
            if defer:
                return jobs
            for j_ in jobs:
                j_()
            return []

        def chunk_slot(c):
            if c % 2 == 0:
                return wgA, wuA, wdA, SLA, ch_fA
            return wgB, wuB, wdB, SLB, ch_fB

        def chunk_slot_d(c):
            if c % 2 == 0:
                return SLAd, ch_fAd
            return SLBd, ch_fBd

        def load_chunk(layer, c, extra=()):
            wg, wu, wd, sb_, ch = chunk_slot(c)
            nf = FCH[c]
            f0 = sum(FCH[:c])
            gv = fgate_d[layer * 1024:(layer + 1) * 1024, :].rearrange("(kc p) n -> p kc n", p=128)
            uv = fup_d[layer * 1024:(layer + 1) * 1024, :].rearrange("(kc p) n -> p kc n", p=128)
            dvw = fdown_d[layer * 2816 + f0 * 128:layer * 2816 + (f0 + nf) * 128, :].rearrange("(f p) n -> p f n", p=128)
            if extra:
                fence("pool", extra)
            dma("pool", wg[:, :, 0:nf * 128], gv[:, :, f0 * 128:(f0 + nf) * 128], ch, [], [sb_])
            dma("pool", wu[:, :, 0:nf * 128], uv[:, :, f0 * 128:(f0 + nf) * 128], ch, [], [sb_])
            sbd_, chd_ = chunk_slot_d(c)
            dma("pool", wd[:, 0:nf, :], dvw, chd_, [], [sbd_])

        def load_tile(g, t):
            row = (g * 10 + t) * 128
            for half in range(2):
                s, sb_, i = stage()
                dma("sp", s[:], xin[row:row + 128, half * 512:(half + 1) * 512], ch_xin[i], [], [sb_])
                ps, pb = psum()
                for k in range(4):
                    tr(ps[:, k * 128:(k + 1) * 128], s[:, k * 128:(k + 1) * 128], ident_f, [sb_, CONST], [pb])
                act(x_res[:, half * 4:half * 4 + 4, t * 128:(t + 1) * 128], ps[:].rearrange("p (k n) -> p k n", k=4), AF.Copy,
                    [pb], [XR[t]])

        def store_tile(g, t):
            row = (g * 10 + t - 2) * 128
            for half in range(2):
                s, sb_, i = stage()
                ps, pb = psum()
                for k in range(4):
                    kc = half * 4 + k
                    tr(ps[:, k * 128:(k + 1) * 128], x_res[:, kc, t * 128:(t + 1) * 128], ident_f, [XR[t], CONST], [pb])
                act(s[:], ps[:], AF.Copy, [pb], [sb_])
                dma("sp", yout[row:row + 128, half * 512:(half + 1) * 512], s[:], ch_out[i], [sb_], [OUT])

        def rows_out(wins, r0, nrows, dst, r):
            s, sb_, i = stage()
            ps, pb = psum()
            for g_, a in enumerate(wins):
                tr(ps[:, g_ * 128:(g_ + 1) * 128], a, ident_f, r + [CONST], [pb])
            act(s[:, 0:512], ps[:, :], AF.Copy, [pb], [sb_])
            dma("sp", dst, s[:, 0:512], ch_out[i], [sb_], [OUT])

        XSQ = bf("xsq")
        RS = bf("rs")

        def tiles_of(c0, c1):
            return list(range(c0 // 128, (c1 + 127) // 128))

        def norm(c0, c1, gcol):
            W = c1 - c0
            tl = tiles_of(c0, c1)
            xr = [XR[t] for t in tl]
            ha = [HA[t] for t in tl]
            for half in range(2):
                act(xsq[:, half * 4:half * 4 + 4, 0:W], x_res[:, half * 4:half * 4 + 4, c0:c1], AF.Square, xr, [XSQ])
            ps, pb = psum()
            for kc in range(8):
                mm(ps[:, 0:W], ones_bf[:], xsq[:, kc, 0:W], kc == 0, kc == 7, [XSQ, bf("ones_bf")], [pb])
            act(rs[:, 0:W], ps[:, 0:W], AF.Ln, [pb], [RS], bias=1e-6, scale=1.0 / 1024)
            act(rs[:, 0:W], rs[:, 0:W], AF.Exp, [RS], [RS], scale=-0.5)
            for kc in range(8):
                stt(h_all[:, kc, c0:c1], x_res[:, kc, c0:c1], col(gcol + kc), rs[:, 0:W], ALU.mult, ALU.mult,
                    xr + [RS, COLS], ha)

        def wout_residual(catt, CATB, c0):
            tl = tiles_of(c0, c0 + 256)
            xr = [XR[t] for t in tl]
            for pair in range(4):
                ps, pb = psum()
                for i in range(2):
                    m = pair * 2 + i
                    for kc in range(8):
                        mm(ps[:, i * 256:(i + 1) * 256], wmo[:, kc, m * 128:(m + 1) * 128], catt[:, kc, :], kc == 0, kc == 7,
                           [WMO, CATB], [pb])
                xv = x_res[:, pair * 2:pair * 2 + 2, c0:c0 + 256]
                tt(xv, ps[:].rearrange("p (a n) -> p a n", a=2), xv, ALU.add, [pb] + xr, xr)

        deferred_norm = []

        def flush_norm():
            while deferred_norm:
                deferred_norm.pop(0)()

        def mixer0(g, b, kind):
            c0 = b * 256
            tl = [2 * b, 2 * b + 1]
            ha = [HA[t] for t in tl]
            first = (g == 0 and b == 1)
            lastp = (g == 1 and b == 3)
            sample = kind == "sample"
            UE, PL, CAT, QN, KN, KTB, VB = bf("u_ext"), bf("pooled"), bf("cat"), bf("qn"), bf("kn"), bf("kTz"), bf("V_blk")
            hs = h_all[:, :, c0:c0 + 256]
            if sample:
                mset(u_ext[:], 0.0, [UE])
                mset(kTz[0][64:128, :], 0.0, [KTB])
                mset(kTz[1][0:64, :], 0.0, [KTB])
                for gI in range(4):
                    act(u_ext[:, gI, :].rearrange("p (s n) -> p s n", s=4)[:, :, 1:16], ulead_s[:, gI, :, :], AF.Copy, [bf("ulead_s")], [UE])
            else:
                cpy(u_ext[:, :, 0:16], u_c[:], [bf("u_c")], [UE])
                mset(kTz[0][64:128, :], 0.0, [KTB])
                mset(kTz[1][0:64, :], 0.0, [KTB])
                act(kTz[0][0:64, 0:128], kT_c[0:64, :], AF.Copy, [bf("kT_c")], [KTB])
                act(kTz[1][64:128, 0:128], kT_c[64:128, :], AF.Copy, [bf("kT_c")], [KTB])
                act(V_blk[:, 0, :], V_c[:], AF.Copy, [bf("V_c")], [VB])
            for pair in range(2):
                ps, pb = psum()
                for i in range(2):
                    m = pair * 2 + i
                    for kc in range(8):
                        mm(ps[:, i * 256:(i + 1) * 256], wmi[:, kc, m * 128:(m + 1) * 128], hs[:, kc, :], kc == 0, kc == 7,
                           [WMI] + ha, [pb])
                if not sample:
                    act(u_ext[:, pair * 2:pair * 2 + 2, 16:272], ps[:].rearrange("p (a n) -> p a n", a=2), AF.Copy, [pb], [UE])
                else:
                    for i in range(2):
                        m = pair * 2 + i
                        act(u_ext[:, m, :].rearrange("p (s n) -> p s n", s=4)[:, :, 16:80],
                            ps[:, i * 256:(i + 1) * 256].rearrange("p (s n) -> p s n", s=4), AF.Copy, [pb], [UE])
            QSQ = XSQ
            qps = []
            for pair in range(2):
                ps, pb = psum(pin=True)
                for i in range(2):
                    m = pair * 2 + i
                    for kc in range(8):
                        mm(ps[:, i * 256:(i + 1) * 256], wmi[:, kc, 512 + m * 128:512 + (m + 1) * 128], hs[:, kc, :], kc == 0, kc == 7,
                           [WMI] + ha, [pb])
                act(qsq[:, pair * 2:pair * 2 + 2, 0:256], ps[:].rearrange("p (a n) -> p a n", a=2), AF.Square, [pb], [QSQ])
                qps.append((ps, pb))
            psk, pbk = psum(pin=True)
            for kc in range(8):
                mm(psk[:, 0:256], wmi[:, kc, 1024:1152], hs[:, kc, :], kc == 0, kc == 7, [WMI] + ha, [pbk])
            act(qsq[:, 4, 0:256], psk[:, 0:256], AF.Square, [pbk], [QSQ])
            psv, pbv = psum()
            if not sample:
                for t_ in range(2):
                    for kc in range(8):
                        mm(psv[:, t_ * 128:(t_ + 1) * 128], h_all[:, kc, c0 + t_ * 128:c0 + (t_ + 1) * 128], wmi[:, kc, 1152:1280],
                           kc == 0, kc == 7, [WMI] + ha, [pbv])
                act(V_blk[:, 1:3, :], psv[:, 0:256].rearrange("p (a n) -> p a n", a=2), AF.Copy, [pbv], [VB])
                if lastp and (_LMASK & 1):
                    sv_, sbv_, iv_ = stage()
                    act(sv_[:, 0:128], psv[:, 128:256], AF.Copy, [pbv], [sbv_])
                    dma("sp", v_o[0:128, :], sv_[:, 0:128], ch_out[iv_], [sbv_], [OUT])
            else:
                for s in range(4):
                    for kc in range(8):
                        mm(psv[0:64, s * 128:(s + 1) * 128], h_all[:, kc, c0 + s * 64:c0 + (s + 1) * 64], wmi[:, kc, 1152:1280],
                           kc == 0, kc == 7, [WMI] + ha, [pbv])
                act(V_blk[0:64, :, :], psv[0:64, :].rearrange("p (a n) -> p a n", a=4), AF.Copy, [pbv], [VB])
                sv_, sbv_, iv_ = stage()
                act(sv_[0:64, 0:512], psv[0:64, :], AF.Copy, [pbv], [sbv_])
                for s in range(4 if (_SMASK & 1) else 0):
                    dma("sp", v_o[(1 + s) * 128 + 64:(2 + s) * 128, :], sv_[0:64, s * 128:(s + 1) * 128], ch_out[iv_], [sbv_], [OUT])
            for pair in range(3):
                RQ = bf("rq%d" % (pair % 2))
                rq = rq2[pair % 2]
                ps, pb = psum()
                n_ = 2 if pair < 2 else 1
                for i in range(n_):
                    m = pair * 2 + i
                    mm(ps[:, i * 256:(i + 1) * 256], bd_bf[:], qsq[:, m, 0:256], True, True, [bf("bd_bf"), QSQ], [pb])
                act(rq[:, 0:n_, :], ps[:, 0:n_ * 256].rearrange("p (a n) -> p a n", a=n_), AF.Ln, [pb], [RQ], bias=1e-6, scale=1.0 / 64)
                act(rq[:, 0:n_, :], rq[:, 0:n_, :], AF.Exp, [RQ], [RQ], scale=-0.5)
                if pair < 2:
                    qp, qb = qps[pair]
                    for i in range(2):
                        m = pair * 2 + i
                        stt(qn[:, m, :], qp[:, i * 256:(i + 1) * 256], gq8[:, 0:1], rq[:, i, :], ALU.mult, ALU.mult,
                            [qb, RQ, bf("gq8")], [QN])
                    unpin(qb)
                else:
                    stt(kn[:], psk[:, 0:256], col(C_GK), rq[:, 0, :], ALU.mult, ALU.mult, [pbk, RQ, COLS], [KN])
                    act(kTz[0][0:64, 128:384], kn[0:64, :], AF.Copy, [KN], [KTB])
                    act(kTz[1][64:128, 128:384], kn[64:128, :], AF.Copy, [KN], [KTB])
                    unpin(pbk)
            if lastp and (_LMASK & 4):
                s_, sb_, i_ = stage()
                ps, pb = psum()
                tr(ps[:, 0:128], kn[:, 128:256], ident_f, [KN, CONST], [pb])
                act(s_[:, 0:128], ps[:, 0:128], AF.Copy, [pb], [sb_])
                dma("sp", k_o[0:128, :], s_[:, 0:128], ch_out[i_], [sb_], [OUT])
            if sample and (_SMASK & 4):
                s_, sb_, i_ = stage()
                ps, pb = psum()
                for t_ in range(2):
                    tr(ps[:, t_ * 128:(t_ + 1) * 128], kn[:, t_ * 128:(t_ + 1) * 128], ident_f, [KN, CONST], [pb])
                act(s_[:, 0:256], ps[:, 0:256], AF.Copy, [pb], [sb_])
                for s in range(4):
                    dma("sp", k_o[(1 + s) * 128 + 64:(2 + s) * 128, :],
                        s_[(s % 2) * 64:(s % 2) * 64 + 64, (s // 2) * 128:(s // 2) * 128 + 128], ch_out[i_], [sb_], [OUT])
            SM = bf("att_sm")
            units = []
            if not sample:
                for t_i in range(2):
                    segs = [(t_i * 128, None, V_blk[:, t_i, :], 128, KTB, VB),
                            ((t_i + 1) * 128, None, V_blk[:, t_i + 1, :], 128, KTB, VB)]
                    units.append((128, t_i * 128, segs, first and t_i == 0, t_i))
            elif _SMASK & 8:
                for s in range(4):
                    segs = [(None, s, Vst[:, s, :], 128, bf("kTst"), bf("Vst")),
                            (128 + s * 64, None, V_blk[0:64, s, :], 64, KTB, VB)]
                    units.append((64, s * 64, segs, False, s // 2))
            waves = [(u, slot) for u in units for slot in range(2)]
            nW = len(waves)
            SMW = [Buf("smw%d" % wi) for wi in range(nW)]
            mset(att_sm[:], 0.0, SMW + [SM])
            W = 320 if sample else 272
            PT0, PT1 = bf("ptmp0"), bf("ptmp1")
            A_, B_ = ptmp[0], ptmp[1]

            def dview(ap2):
                if sample:
                    return ap2.rearrange("p (s n) -> p s n", s=4)[:, :, 16:80]
                return ap2[:, 16:272]

            def oview(ap2):
                if sample:
                    return ap2.rearrange("p (s n) -> p s n", s=4)
                return ap2

            def ptt(out, in0, in1, r, w):
                return P.op("pool", lambda e: e.tensor_tensor(out, in0, in1, ALU.add), reads=r, writes=w)

            for gI, wnd in enumerate((2, 4, 8, 16)):
                u_ = u_ext[:, gI, :]
                ptt(A_[:, 1:W], u_[:, 1:W], u_[:, 0:W - 1], [UE], [PT0])
                cur, curB = A_, PT0
                if wnd >= 4:
                    ptt(B_[:, 3:W], A_[:, 3:W], A_[:, 1:W - 2], [PT0], [PT1])
                    cur, curB = B_, PT1
                if wnd >= 8:
                    ptt(A_[:, 7:W], B_[:, 7:W], B_[:, 3:W - 4], [PT1], [PT0])
                    cur, curB = A_, PT0
                if wnd >= 16:
                    ptt(B_[:, 15:W], A_[:, 15:W], A_[:, 7:W - 8], [PT0], [PT1])
                    cur, curB = B_, PT1
                stt(oview(pooled[:, gI, :]), dview(cur[:, 0:W]), 1.0 / wnd, dview(u_), ALU.mult, ALU.subtract, [curB, UE], [PL])
                if first:
                    tt(ptiny[:, gI, :], cur[:, 16:32], consts[:, K_ICNT + gI * 16:K_ICNT + (gI + 1) * 16], ALU.mult,
                       [curB, CONST], [bf("ptiny")])
                    tt(pooled[:, gI, 0:16], ptiny[:, gI, :], u_[:, 16:32], ALU.subtract, [bf("ptiny"), UE, PL], [PL])
            if not sample:
                cpy(u_c[:], u_ext[:, :, 256:272], [UE], [bf("u_c")])
            wstate = {}
            po_tiles = {}
            last_wave_of_tile = {}
            for wi, (u, slot) in enumerate(waves):
                last_wave_of_tile[u[4]] = wi

            def kseg(slot, sg):
                c_, st_, _, n, _, _ = sg
                if st_ is not None:
                    return kTstz[slot][:, st_, 0:n]
                return kTz[slot][:, c_:c_ + n]

            def emitS(i):
                (Mq, a, segs, masked, tk), slot = waves[i]
                nk = sum(sg[3] for sg in segs)
                banks = [psum(), psum()]
                wstate[i] = banks
                for j in range(4):
                    h = slot * 4 + j
                    pS, pSb = banks[j // 2]
                    off = (j % 2) * 256
                    mm(pS[0:Mq, off:off + nk], ident_bf[:, 0:Mq], btab[:, h, 0:nk], True, False, [bf("ident_bf"), bf("btab")], [pSb])
                    if masked:
                        mm(pS[0:Mq, off:off + nk], ident_bf[:, 0:Mq], maskf[:, 0:nk], False, False, [bf("ident_bf"), bf("maskf")], [pSb])
                    ko = 0
                    for si, sg in enumerate(segs):
                        n = sg[3]
                        mm(pS[0:Mq, off + ko:off + ko + n], qn[:, j, a:a + Mq], kseg(slot, sg), False, si == len(segs) - 1,
                           [QN, sg[4]], [pSb])
                        ko += n

            def emitSoft(i):
                (Mq, a, segs, masked, tk), slot = waves[i]
                nk = sum(sg[3] for sg in segs)
                set_ = i % 2
                banks = wstate[i]
                PNB = bf("att_pn%d" % set_)
                smw = att_sm[0:Mq, i * 4:(i + 1) * 4, :]
                for j in range(4):
                    h = slot * 4 + j
                    pS, pSb = banks[j // 2]
                    off = (j % 2) * 256
                    hb = set_ * 4 + j
                    act(att_pn[hb][0:Mq, 0:nk], pS[0:Mq, off:off + nk], AF.Exp, [pSb, bf("negsink"), SMW[i]], [PNB, SMW[i]],
                        bias=negsink[0:Mq, h:h + 1], scale=1.0, accum=att_sm[0:Mq, i * 4 + j, 0:1])
                act(smw[:, :, 1:2], smw[:, :, 0:1], AF.Ln, [SMW[i]], [SMW[i]], bias=1.0, scale=1.0)
                act(smw[:, :, 2:3], smw[:, :, 1:2], AF.Exp, [SMW[i]], [SMW[i]], scale=-1.0)
                for j in range(4):
                    hb = set_ * 4 + j
                    ts(att_pn[hb][0:Mq, 0:nk], att_pn[hb][0:Mq, 0:nk], att_sm[0:Mq, i * 4 + j, 2:3], None, ALU.mult, None, [PNB, SMW[i]], [PNB])

            def emitT(i):
                (Mq, a, segs, masked, tk), slot = waves[i]
                set_ = i % 2
                PNB = bf("att_pn%d" % set_)
                PTSB = bf("att_PT%d" % set_)
                for j in range(4):
                    hb = set_ * 4 + j
                    ko = 0
                    for si, sg in enumerate(segs):
                        n = sg[3]
                        tr(psb[0:n, j * 256 + si * 128:j * 256 + si * 128 + Mq], att_pn[hb][0:Mq, ko:ko + n],
                           ident_bf[0:Mq, 0:Mq], [PNB, bf("ident_bf")], [PTB])
                        ko += n
                if Mq == 128 and all(sg[3] == 128 for sg in segs):
                    cpy(att_PT[set_][:, :], psb[:, :], [PTB], [PTSB])
                else:
                    for j in range(4):
                        for si, sg in enumerate(segs):
                            n = sg[3]
                            cpy(att_PT[set_][0:n, j * 256 + si * 128:j * 256 + si * 128 + Mq],
                                psb[0:n, j * 256 + si * 128:j * 256 + si * 128 + Mq], [PTB], [PTSB])

            def emitPV(i):
                (Mq, a, segs, masked, tk), slot = waves[i]
                set_ = i % 2
                PTSB = bf("att_PT%d" % set_)
                if tk not in po_tiles:
                    po_tiles[tk] = psum(pin=True)
                ps_o, pb_o = po_tiles[tk]
                for j in range(4):
                    for si, sg in enumerate(segs):
                        n = sg[3]
                        mm(ps_o[slot * 64:(slot + 1) * 64, j * 128 + (a % 128):j * 128 + (a % 128) + Mq],
                           sg[2][0:n, slot * 64:(slot + 1) * 64], att_PT[set_][0:n, j * 256 + si * 128:j * 256 + si * 128 + Mq],
                           si == 0, si == len(segs) - 1, [sg[5], PTSB], [pb_o])
                if last_wave_of_tile[tk] == i:
                    act(cat[:, 4:8, tk * 128:(tk + 1) * 128], ps_o[:].rearrange("p (j n) -> p j n", j=4), AF.Copy, [pb_o], [CAT])
                    unpin(pb_o)

            for i in range(nW + 2):
                if i < nW:
                    emitS(i)
                if 0 <= i - 1 < nW:
                    emitT(i - 1)
                if 0 <= i - 2 < nW:
                    emitPV(i - 2)
                if i < nW:
                    emitSoft(i)
            flush_norm()
            for pair in range(2):
                ps, pb = psum()
                for i in range(2):
                    gI = pair * 2 + i
                    mm(ps[:, i * 256:(i + 1) * 256], pw[:, gI, :], pooled[:, gI, :], True, True, [bf("pw"), PL], [pb])
                for i in range(2):
                    gI = pair * 2 + i
                    act(cat[:, gI, :], ps[:, i * 256:(i + 1) * 256], AF.Copy, [pb, COLS], [CAT], scale=col(C_PSC + gI))
            if lastp and (_LMASK & 2):
                rows_out([u_ext[:, gI, 144:272] for gI in range(4)], 112, 16, pool_o[0:128, :], [UE])
            if sample and (_SMASK & 2):
                for s in range(4):
                    w0 = min(max(s * 80 + 80 - 128, 0), 320 - 128)
                    rows_out([u_ext[:, gI, w0:w0 + 128] for gI in range(4)], s * 80 + 64 - w0, 16, pool_o[(1 + s) * 128:(2 + s) * 128, :], [UE])
            wout_residual(cat, CAT, c0)
            if not sample:
                act(kT_c[0:64, :], kTz[0][0:64, 256:384], AF.Copy, [KTB], [bf("kT_c")])
                act(kT_c[64:128, :], kTz[1][64:128, 256:384], AF.Copy, [KTB], [bf("kT_c")])
                act(V_c[:], V_blk[:, 2, :], AF.Copy, [VB], [bf("V_c")])
            deferred_norm.append(lambda c0=c0: norm(c0, c0 + 256, C_GFFN + 0))

        PTAPS = []
        NPT = [0]

        def tap_regions():
            regs = [(wgA[:].rearrange("p a (b c) -> p (a b) c", c=128), [SLA]),
                    (wuA[:].rearrange("p a (b c) -> p (a b) c", c=128), [SLA]),
                    (wdA[:].rearrange("p a (b c) -> p (a b) c", c=128), [SLAd]),
                    (hid[0][:].rearrange("p a (b c) -> p (a b) c", c=128), [bf("hid0")]),
                    (hid[1][:].rearrange("p a (b c) -> p (a b) c", c=128), [bf("hid1")])]
            return regs

        def build_resident_taps():
            del PTAPS[:]
            t0 = 0
            for view, bufs in tap_regions():
                n = min(view.shape[1], 124 - t0)
                if n <= 0:
                    break
                wsrc = cols[:, C_CW + t0:C_CW + t0 + n]
                wbc = bass.AP(tensor=wsrc.tensor, offset=wsrc.offset, ap=[list(x) for x in wsrc.ap] + [[0, 128]])
                tt(view[:, 0:n, :], bcast_mid(ident_bf[:], n), wbc, ALU.mult, [bf("ident_bf"), COLS], bufs)
                for j in range(n):
                    PTAPS.append((view[:, j, :], bufs))
                t0 += n
            NPT[0] = t0

        def mixer1(g, b, kind):
            c0 = b * 256
            tl = [2 * b, 2 * b + 1]
            ha = [HA[t] for t in tl]
            lastp = (g == 1 and b == 3)
            sample = kind == "sample"
            halo = kind == "halo"
            GE, GBF, SIG, CAT1 = bf("glu_ext"), bf("glu_bf"), bf("zg"), bf("cat1")
            hs = h_all[:, :, c0:c0 + 256]
            W = 384 if sample else 288
            if sample:
                mset(glu_ext[:], 0.0, [GE])
                for ci in range(4):
                    act(glu_ext[:, ci, :].rearrange("p (s n) -> p s n", s=4)[:, :, 2:32], clead_s[:, ci, :, :], AF.Copy, [bf("clead_s")], [GE])
            else:
                cpy(glu_ext[:, :, 0:32], glu_c[:], [bf("glu_c")], [GE])
            for pair in range(2):
                psa, pba = psum()
                psg, pbg = psum()
                for (ps_, pb_, base) in ((psa, pba, 0), (psg, pbg, 512)):
                    for i in range(2):
                        m = pair * 2 + i
                        for kc in range(8):
                            mm(ps_[:, i * 256:(i + 1) * 256], wmi[:, kc, base + m * 128:base + (m + 1) * 128], hs[:, kc, :], kc == 0, kc == 7,
                               [WMI] + ha, [pb_])
                act(sig, psg[:].rearrange("p (a n) -> p a n", a=2), AF.Sigmoid, [pbg], [SIG])
                if not sample:
                    tt(glu_ext[:, pair * 2:pair * 2 + 2, 32:288], psa[:].rearrange("p (a n) -> p a n", a=2), sig, ALU.mult,
                       [pba, SIG], [GE])
                else:
                    for i in range(2):
                        m = pair * 2 + i
                        tt(glu_ext[:, m, :].rearrange("p (s n) -> p s n", s=4)[:, :, 32:96],
                           psa[:, i * 256:(i + 1) * 256].rearrange("p (s n) -> p s n", s=4),
                           sig[:, i, :].rearrange("p (s n) -> p s n", s=4), ALU.mult, [pba, SIG], [GE])
            flush_norm()
            if not sample:
                cpy(glu_c[:], glu_ext[:, :, 256:288], [GE], [bf("glu_c")])
            if lastp:
                rows_out([glu_ext[:, ci, 160:288] for ci in range(4)], 96, 32, conv_o[0:128, :], [GE])
            if sample and (_SMASK & 32):
                for s in range(4):
                    w0 = min(max(s * 96 + 96 - 128, 0), 384 - 128)
                    rows_out([glu_ext[:, ci, w0:w0 + 128] for ci in range(4)], s * 96 + 64 - w0, 32, conv_o[(1 + s) * 128:(2 + s) * 128, :], [GE])
            if halo:
                return
            act(glu_bf[:, :, 0:W], glu_ext[:, :, 0:W], AF.Copy, [GE], [GBF])
            CSQ = XSQ
            csq = xsq[:, 0:4, 0:256]
            c_bf = xsq[:, 4:8, 0:256]
            dgc = [0]
            conv_ps = []
            built = {}

            def tap_operand(t):
                if t < NPT[0]:
                    return PTAPS[t]
                t0 = NPT[0] + ((t - NPT[0]) // 16) * 16
                if t0 not in built:
                    nt = min(16, 124 - t0)
                    r_ = dgc[0] % 2
                    dgc[0] += 1
                    DGB = bf("dgb%d" % r_)
                    wsrc = cols[:, C_CW + t0:C_CW + t0 + nt]
                    wbc = bass.AP(tensor=wsrc.tensor, offset=wsrc.offset, ap=[list(x) for x in wsrc.ap] + [[0, 128]])
                    tt(dgb[r_][:, 0:nt, :], bcast_mid(ident_bf[:], nt), wbc, ALU.mult, [bf("ident_bf"), COLS], [DGB])
                    built[t0] = (r_, DGB)
                r_, DGB = built[t0]
                return dgb[r_][:, t - t0, :], [DGB]

            for t_ in range(NPT[0], 124, 16):
                tap_operand(t_)
            ZU = bf("zu")
            for pair in range(2):
                ps, pb = psum()
                for i in range(2):
                    m = pair * 2 + i
                    for kc in range(8):
                        mm(ps[:, i * 256:(i + 1) * 256], wmi[:, kc, 1024 + m * 128:1024 + (m + 1) * 128], hs[:, kc, :], kc == 0, kc == 7,
                           [WMI] + ha, [pb])
                act(zu[:, pair * 2:pair * 2 + 2, :], ps[:].rearrange("p (a n) -> p a n", a=2), AF.Gelu, [pb], [ZU])
            VNB, MVZ = bf("vn_bf"), bf("mvz")
            zgt = [zg[:], ctmp2[:].rearrange("p a n -> p (a n)")]
            ZGt = [[bf("zg")], [bf("ctmp0"), bf("ctmp1")]]
            for t_i in range(2):
                ps, pb = psum()
                for kc in range(8):
                    mm(ps[:], h_all[:, kc, c0 + t_i * 128:c0 + (t_i + 1) * 128], wmi[:, kc, 1536:2048], kc == 0, kc == 7, [WMI] + ha, [pb])
                act(zgt[t_i], ps[:], AF.Gelu, [pb], ZGt[t_i])
            for t_i in range(2):
                dve(lambda e, t_i=t_i: e.bn_stats(st6[:, t_i, 0:6], zgt[t_i]), ZGt[t_i], [bf("st6")])
                dve(lambda e, t_i=t_i: e.bn_aggr(mvz[:, t_i, 0:2], st6[:, t_i, 0:6]), [bf("st6")], [MVZ])
            act(mvz[:, :, 2:3], mvz[:, :, 1:2], AF.Ln, [MVZ], [MVZ], bias=1e-5, scale=1.0)
            act(mvz[:, :, 3:4], mvz[:, :, 2:3], AF.Exp, [MVZ], [MVZ], scale=-0.5)
            for t_i in range(2):
                z_ = zgt[t_i]
                ts(z_, z_, mvz[:, t_i, 0:1], mvz[:, t_i, 3:4], ALU.subtract, ALU.mult, ZGt[t_i] + [MVZ], ZGt[t_i])
                tt(z_, z_, consts[:, K_LNG:K_LNG + 512], ALU.mult, ZGt[t_i] + [CONST], ZGt[t_i])
                tt(z_, z_, consts[:, K_LNB:K_LNB + 512], ALU.add, ZGt[t_i] + [CONST], ZGt[t_i])
                act(vn_bf[:, t_i, :], z_, AF.Copy, ZGt[t_i], [VNB])
                if sample:
                    for half in range(2):
                        s = t_i * 2 + half
                        if _SMASK & 16:
                            dma_scr("sp", gv_o[s * 64:(s + 1) * 64, :], z_[half * 64:(half + 1) * 64, :], ch_misc, ZGt[t_i], [OUT])
            for pair in range(2):
                ps, pb = psum(pin=True)
                for i in range(2):
                    ci = pair * 2 + i
                    for k in range(31):
                        lw, lwb = tap_operand(ci * 31 + k)
                        if not sample:
                            mm(ps[:, i * 256:(i + 1) * 256], lw, glu_bf[:, ci, 2 + k:2 + k + 256], k == 0, k == 30, lwb + [GBF], [pb])
                        else:
                            for s in range(4):
                                P.op("pe", lambda e, o_=ps[:, i * 256 + s * 64:i * 256 + (s + 1) * 64], l_=lw,
                                     r2_=glu_bf[:, ci, s * 96 + 2 + k:s * 96 + 2 + k + 64], st_=(k == 0 and s == 0), sp_=(k == 30 and s == 3):
                                     e.matmul(o_, l_, r2_, start=st_, stop=sp_, skip_group_check=True), reads=lwb + [GBF], writes=[pb])
                for i in range(2):
                    ci = pair * 2 + i
                    act(c_bf[:, ci, :], ps[:, i * 256:(i + 1) * 256], AF.Identity, [pb, COLS], [CSQ], bias=col(C_CB + ci))
                    act(csq[:, ci, :], ps[:, i * 256:(i + 1) * 256], AF.Square, [pb, COLS], [CSQ], bias=col(C_CB + ci))
                conv_ps.append((ps, pb))
            if not sample:
                for pair in range(2):
                    ps, pb = psum()
                    for i in range(2):
                        gg = pair * 2 + i
                        for t_i in range(2):
                            mm(ps[:, i * 256 + t_i * 128:i * 256 + (t_i + 1) * 128], vn_bf[:, t_i, gg * 128:(gg + 1) * 128], wm[:, gg, :],
                               True, True, [VNB, bf("wm")], [pb])
                    for i in range(2):
                        gg = pair * 2 + i
                        GT = bf("ctmp%d" % i)
                        tt(gtmp[i][:].rearrange("p (a n) -> p a n", a=2), ps[:, i * 256:(i + 1) * 256].rearrange("p (a n) -> p a n", a=2),
                           bcast_mid(consts[:, K_GB + gg * 128:K_GB + (gg + 1) * 128], 2), ALU.add, [pb, CONST], [GT])
                        tt(cat1[:, 4 + gg, :], gtmp[i][:], zu[:, gg, :], ALU.mult, [GT, ZU], [CAT1])
            else:
                psH = [psum(pin=True), psum(pin=True)]
                for half in range(2):
                    ps, pb = psH[half]
                    for gg in range(4):
                        for t_i in range(2):
                            o_ = gg * 128 + t_i * 64
                            mm(ps[:, o_:o_ + 64], vn_bf[half * 64:(half + 1) * 64, t_i, gg * 128:(gg + 1) * 128],
                               wms[half * 64:(half + 1) * 64, gg, :], True, True, [VNB, bf("wms")], [pb])
                for gg in range(4):
                    i = gg % 2
                    GT = bf("ctmp%d" % i)
                    for half in range(2):
                        ps, pb = psH[half]
                        tt(gtmp[i][:].rearrange("p (t h n) -> p t h n", t=2, h=2)[:, :, half, :],
                           ps[:, gg * 128:(gg + 1) * 128].rearrange("p (t n) -> p t n", t=2),
                           bcast_mid(consts[:, K_GB + gg * 128:K_GB + gg * 128 + 64], 2), ALU.add, [pb, CONST], [GT])
                    tt(cat1[:, 4 + gg, :], gtmp[i][:], zu[:, gg, :], ALU.mult, [GT, ZU], [CAT1])
                unpin(psH[0][1])
                unpin(psH[1][1])
            MV = bf("mv")
            pst, pbt = psum()
            for ci in range(4):
                mm(pst[:, 0:256], ones_bf[:], c_bf[:, ci, :], ci == 0, ci == 3, [CSQ, bf("ones_bf")], [pbt])
            for ci in range(4):
                mm(pst[:, 256:512], ones_bf[:], csq[:, ci, :], ci == 0, ci == 3, [CSQ, bf("ones_bf")], [pbt])
            act(mv[:, 0, :], pst[:, 0:256], AF.Copy, [pbt], [MV], scale=1.0 / 512)
            tt(mv[:, 2, :], mv[:, 0, :], mv[:, 0, :], ALU.mult, [MV], [MV])
            stt(mv[:, 1, :], pst[:, 256:512], 1.0 / 512, mv[:, 2, :], ALU.mult, ALU.subtract, [pbt, MV], [MV])
            act(mv[:, 1, :], mv[:, 1, :], AF.Ln, [MV], [MV], bias=1e-5, scale=1.0)
            act(mv[:, 1, :], mv[:, 1, :], AF.Exp, [MV], [MV], scale=-0.5)
            for ci in range(4):
                ps, pb = conv_ps[ci // 2]
                i = ci % 2
                CT = bf("ctmp%d" % (ci % 2))
                stt(ctmp[ci % 2][:], ps[:, i * 256:(i + 1) * 256], col(C_CB + ci), mv[:, 0, :], ALU.add, ALU.subtract, [pb, MV, COLS], [CT])
                tt(ctmp[ci % 2][:], ctmp[ci % 2][:], mv[:, 1, :], ALU.mult, [CT, MV], [CT])
                act(cat1[:, ci, :], ctmp[ci % 2][:], AF.Silu, [CT, COLS], [CAT1], bias=col(C_LB + ci), scale=col(C_LG + ci))
                if i == 1:
                    unpin(pb)
            wout_residual(cat1, CAT1, c0)
            deferred_norm.append(lambda c0=c0: norm(c0, c0 + 256, C_GFFN + 8))

        def ffn(layer, blocks, after_block, next_chunk0_loader, mixer_end_ops, bg_loads=()):
            bg = list(bg_loads)
            pending = [None]
            HID = [bf("hid0"), bf("hid1")]
            SSB = [bf("ssb0"), bf("ssb1")]
            it = [0]

            def down(c, blk, hb):
                wg, wu, wd, sb_, ch = chunk_slot(c)
                nf = FCH[c]
                c0_, c1_ = blk
                W = c1_ - c0_
                xr = [XR[t] for t in tiles_of(c0_, c1_)]
                for m in range(8):
                    ps, pb = psum()
                    for fi in range(nf):
                        mm(ps[:, 0:W], wd[:, fi, m * 128:(m + 1) * 128], hid[hb][:, fi, 0:W], fi == 0, fi == nf - 1, [chunk_slot_d(c)[0], HID[hb]], [pb])
                    tt(x_res[:, m, c0_:c1_], ps[:, 0:W], x_res[:, m, c0_:c1_], ALU.add, [pb] + xr, xr)
                if c == len(FCH) - 1:
                    after_block(blk)

            load_chunk(layer, 1, extra=mixer_end_ops)
            for c in range(len(FCH)):
                wg, wu, wd, sb_, ch = chunk_slot(c)
                nf = FCH[c]
                for blk in blocks:
                    c0_, c1_ = blk
                    W = c1_ - c0_
                    ha = [HA[t] for t in tiles_of(c0_, c1_)]
                    hb = it[0] % 2
                    it[0] += 1
                    for fi in range(nf):
                        psg, pbg = psum()
                        for kc in range(8):
                            mm(psg[:, 0:W], wg[:, kc, fi * 128:(fi + 1) * 128], h_all[:, kc, c0_:c1_], kc == 0, kc == 7, [sb_] + ha, [pbg])
                        psu, pbu = psum()
                        for kc in range(8):
                            mm(psu[:, 0:W], wu[:, kc, fi * 128:(fi + 1) * 128], h_all[:, kc, c0_:c1_], kc == 0, kc == 7, [sb_] + ha, [pbu])
                        k_ = fi % 2
                        act(s_sb[k_][:, 0:W], psg[:, 0:W], AF.Silu, [pbg], [SSB[k_]])
                        tt(hid[hb][:, fi, 0:W], psu[:, 0:W], s_sb[k_][:, 0:W], ALU.mult, [pbu, SSB[k_]], [HID[hb]])
                    if pending[0] is not None:
                        down(*pending[0])
                    pending[0] = (c, blk, hb)
                if c + 2 < len(FCH):
                    if pending[0] is not None:
                        down(*pending[0])
                        pending[0] = None
                    load_chunk(layer, c + 2)
                    for _ in range(4):
                        if bg:
                            bg.pop(0)()
            if pending[0] is not None:
                down(*pending[0])
                pending[0] = None
            while bg:
                bg.pop(0)()
            if next_chunk0_loader is not None:
                next_chunk0_loader()

        setup()
        if stg >= 1:
            load_mixer_weights(0)
            load_chunk(0, 0)

        kinds = [["halo", "prompt", "prompt", "prompt", "prompt"], ["prompt", "prompt", "prompt", "prompt", "sample"]]
        ffn_blocks = [
            [[(128, 256), (256, 768), (768, 1280)], [(256, 768), (768, 1280)]],
            [[(0, 512), (512, 1024), (1024, 1280)], [(0, 512), (512, 1024), (1024, 1280)]],
        ]

        for g in range(0 if stg < 1 else (2 if stg >= 6 else 1)):
            gstg = stg if (g == 0 or stg >= 99) else {6: 2, 7: 2, 8: 4, 9: 4, 10: 1, 11: 2}[stg]
            nb0 = 4 if (g == 1 and stg == 6) else (3 if (g == 1 and stg == 11) else 5)
            nb1 = 4 if (g == 1 and stg == 8) else 5
            for b in range(5):
                for t in (2 * b, 2 * b + 1):
                    load_tile(g, t)
                norm(b * 256, b * 256 + 256, C_GMIX + 0)
            if gstg <= 1:
                for t in range(2, 10):
                    store_tile(g, t)
                continue
            for e_ in ("act", "dve", "pe"):
                fence(e_, list(SLB.rs.values()) + list(SLB.ws.values()) + list(SLBd.rs.values()) + list(SLBd.ws.values()))
            for b in range(nb0):
                mixer0(g, b, kinds[g][b])
            flush_norm()
            mix_end = [P.last.get(e_) for e_ in ("pe", "act", "dve")] + scr_dma[-1:]
            if gstg <= 2:
                for t in range(2, 10):
                    store_tile(g, t)
                continue
            load_mixer_weights(1)

            def after_l0(blk):
                norm(blk[0], blk[1], C_GMIX + 8)

            ffn(0, ffn_blocks[g][0], after_l0, None, mix_end)
            if gstg <= 3:
                for t in range(2, 10):
                    store_tile(g, t)
                continue
            for e_ in ("act", "dve", "pe"):
                fence(e_, list(SLB.rs.values()) + list(SLB.ws.values()) + list(SLBd.rs.values()) + list(SLBd.ws.values()))
            build_resident_taps()
            for b in range(nb1):
                mixer1(g, b, kinds[g][b])
            flush_norm()
            load_chunk(1, 0)
            mix_end = [P.last.get(e_) for e_ in ("pe", "act", "dve")] + scr_dma[-1:]
            if gstg <= 4:
                for t in range(2, 10):
                    store_tile(g, t)
                continue
            bg_jobs = load_mixer_weights(0, defer=True) if g == 0 else []

            def after_l1(blk, g=g):
                for t in tiles_of(blk[0], blk[1]):
                    store_tile(g, t)

            ffn(1, ffn_blocks[g][1], after_l1, (lambda: load_chunk(0, 0)) if g == 0 else None, mix_end, bg_jobs)

        P.op("sp", None, reads=[OUT], extra=list(chan_last.values()))
        P.lower(block, sems)
    return nc


def _host_consts(core):
    c = np.zeros((128, NCONST), np.float32)
    c[:, K_ID:K_ID + 128] = np.eye(128, dtype=np.float32)
    j = np.arange(128)[:, None]
    i = np.arange(128)[None, :]
    c[:, K_TRIU:K_TRIU + 128] = (j <= i).astype(np.float32)
    q = np.arange(128)[:, None]
    s = np.arange(256)[None, :]
    dist = np.abs(128 + q - s).astype(np.float32)
    qc = q // 64
    sc = s // 64
    allowed = (sc >= qc) & (sc <= qc + 2)
    nd = np.where(allowed, -dist, -1.0e6).astype(np.float32)
    c[:, K_ND:K_ND + 256] = nd
    mk = np.zeros((128, 256), np.float32)
    if core == 0:
        mk[:, 0:128] = -30000.0
    c[:, K_NDF:K_NDF + 256] = mk
    for gI, w in enumerate((2, 4, 8, 16)):
        pos = np.arange(16)
        cntv = np.minimum(pos + 1, w) if core == 0 else np.full(16, w)
        c[:, K_ICNT + gI * 16:K_ICNT + (gI + 1) * 16] = (1.0 / cntv.astype(np.float32))[None, :]
    bd = np.zeros((128, 128), np.float32)
    bd[0:64, 0:64] = 1.0
    bd[64:128, 64:128] = 1.0
    c[:, K_BD:K_BD + 128] = bd
    j64 = (np.arange(128) % 64)[:, None]
    i64 = np.arange(64)[None, :]
    c[:, K_TRIU64:K_TRIU64 + 64] = (j64 <= i64).astype(np.float32)
    return c


def kernel(x_prompt, x_sample, state_pool, state_swa_k, state_swa_v, state_conv,
           norm_mix, norm_ffn, w_in_even, q_norm, k_norm, attn_sinks, pool_w, pool_scale,
           w_out_even, w_in_odd, conv_w, conv_b, conv_ln_g, conv_ln_b, gmlp_ln_g, gmlp_ln_b,
           gmlp_w, gmlp_b, w_out_odd, ffn_gate, ffn_up, ffn_down):
    f = lambda a: np.ascontiguousarray(np.asarray(a, dtype=np.float32))
    xp = f(x_prompt)[0]
    xsm = f(x_sample).reshape(32 * 64, 1024)
    cols = np.zeros((128, NCOL), np.float32)
    nm, nf_ = f(norm_mix), f(norm_ffn)
    for l in range(2):
        cols[:, C_GMIX + 8 * l:C_GMIX + 8 * l + 8] = nm[l].reshape(8, 128).T
        cols[:, C_GFFN + 8 * l:C_GFFN + 8 * l + 8] = nf_[l].reshape(8, 128).T
    cols[:, C_PSC:C_PSC + 4] = f(pool_scale)[0].reshape(4, 128).T
    cols[:, C_GQ] = np.tile(f(q_norm)[0], 2)
    cols[:, C_GK] = np.tile(f(k_norm)[0], 2)
    cols[:, C_SINK:C_SINK + 8] = np.broadcast_to(f(attn_sinks)[0][None, :], (128, 8))
    cols[:, C_CB:C_CB + 4] = f(conv_b)[0].reshape(4, 128).T
    cols[:, C_LG:C_LG + 4] = f(conv_ln_g)[0].reshape(4, 128).T
    cols[:, C_LB:C_LB + 4] = f(conv_ln_b)[0].reshape(4, 128).T
    cw = f(conv_w)[0]
    cols[:, C_CW:C_CW + 124] = cw.reshape(31, 4, 128).transpose(2, 1, 0).reshape(128, 124)
    lng = np.broadcast_to(f(gmlp_ln_g)[0][None, :], (128, 512))
    lnb = np.broadcast_to(f(gmlp_ln_b)[0][None, :], (128, 512))
    gb = np.broadcast_to(f(gmlp_b)[0].reshape(1, 512), (128, 512))
    gwT = np.ascontiguousarray(f(gmlp_w)[0].transpose(0, 2, 1)).reshape(512, 128)
    shared = {
        "cols": cols,
        "w_in0": f(w_in_even)[0], "w_out0": f(w_out_even)[0], "pool_w": f(pool_w)[0].reshape(512, 128),
        "w_in1": f(w_in_odd)[0], "w_out1": f(w_out_odd)[0], "gwT": gwT,
        "fgate": f(ffn_gate).reshape(2048, 2816), "fup": f(ffn_up).reshape(2048, 2816), "fdown": f(ffn_down).reshape(5632, 1024),
    }
    sp, sk, sv, scv = f(state_pool)[0], f(state_swa_k)[0], f(state_swa_v)[0], f(state_conv)[0]
    in_maps = []
    for c in range(N_CORES):
        xin = np.zeros((2560, 1024), np.float32)
        if c > 0:
            xin[0:256] = xp[2048 * c - 256:2048 * c]
        xin[256:2304] = xp[2048 * c:2048 * (c + 1)]
        xin[2304:2560] = xsm[256 * c:256 * (c + 1)]
        cst = _host_consts(c)
        cst[:, K_LNG:K_LNG + 512] = lng
        cst[:, K_LNB:K_LNB + 512] = lnb
        cst[:, K_GB:K_GB + 512] = gb
        m = dict(shared)
        m.update({
            "xin": xin, "consts": cst,
            "st_pool": np.ascontiguousarray(sp[4 * c:4 * c + 4].reshape(60, 512)),
            "st_k": np.ascontiguousarray(sk[4 * c:4 * c + 4].reshape(512, 128)),
            "st_v": np.ascontiguousarray(sv[4 * c:4 * c + 4].reshape(512, 128)),
            "st_conv": np.ascontiguousarray(scv[4 * c:4 * c + 4].reshape(120, 512)),
        })
        in_maps.append(m)
    nc = build_nc(_STAGE)
    res = run_bass_kernel_spmd(nc, in_maps, core_ids=list(range(N_CORES)))
    if _STAGE < 99:
        return res.results
    R = res.results
    y_prompt = np.concatenate([R[c]["yout"][0:2048] for c in range(N_CORES)], 0)[None]
    y_sample = np.concatenate([R[c]["yout"][2048:2304] for c in range(N_CORES)], 0).reshape(32, 64, 1024)
    def _rows(a, n):
        a = a.reshape(5, 128, 512)
        out = [a[0, 128 - n:128]]
        for s_ in range(4):
            w = 320 if n == 16 else 384
            per = 80 if n == 16 else 96
            w0 = min(max(s_ * per + per - 128, 0), w - 128)
            r0 = s_ * per + 64 - w0
            out.append(a[1 + s_, r0:r0 + n])
        return np.stack(out, 0)

    po = [_rows(R[c]["pool_o"], 16) for c in range(N_CORES)]
    ko = [R[c]["k_o"].reshape(5, 128, 2, 64) for c in range(N_CORES)]
    vo = [R[c]["v_o"].reshape(5, 128, 2, 64) for c in range(N_CORES)]
    co = [_rows(R[c]["conv_o"], 32) for c in range(N_CORES)]
    gvo = [R[c]["gv_o"].reshape(4, 64, 512) for c in range(N_CORES)]
    pool_prompt = po[7][0:1, 1:16][None]
    pool_sample = np.concatenate([p[1:5, 1:16] for p in po], 0)[None]
    k_prompt = ko[7][0:1][None]
    k_sample = np.concatenate([k[1:5] for k in ko], 0)[None]
    v_prompt = vo[7][0:1][None]
    v_sample = np.concatenate([v[1:5] for v in vo], 0)[None]
    conv_prompt = co[7][0:1, 2:32][None]
    conv_sample = np.concatenate([x[1:5, 2:32] for x in co], 0)[None]
    gv_sample = np.concatenate(gvo, 0)[None]
    outs = (y_prompt, y_sample, pool_prompt, pool_sample, k_prompt, k_sample, v_prompt, v_sample,
            conv_prompt, conv_sample, gv_sample)
    return tuple(np.ascontiguousarray(o, dtype=np.float32) for o in outs)
```

```python
import numpy as np
from contextlib import ExitStack
import concourse.bass as bass
import concourse.mybir as mybir
from concourse.bass_utils import run_bass_kernel_spmd

F32 = mybir.dt.float32
BF16 = mybir.dt.bfloat16
ALU = mybir.AluOpType
AF = mybir.ActivationFunctionType
AX = mybir.AxisListType

SAME_ENGINE_SYNC = True
N_CORES = 8
SBUF_BASE = 16512
SBUF_LIMIT = 229376

C_GMIX = 0
C_GFFN = 16
C_PSC = 32
C_GQ = 36
C_GK = 37
C_SINK = 38
C_CB = 46
C_LG = 50
C_LB = 54
C_CW = 58
NCOL = 58 + 124
K_ID = 0
K_TRIU = 128
K_ND = 256
K_NDF = 512
K_ICNT = 768
K_LNG = 832
K_LNB = 1344
K_GB = 1856
K_BD = 2368
K_TRIU64 = 2496
NCONST = 2560

_STAGE = 99
_LMASK = 7
_SMASK = 255
FCH = [3, 3, 3, 3, 3, 3, 3, 1]


class Buf:
    __slots__ = ("name", "ws", "rs")

    def __init__(self, name=""):
        self.name = name
        self.ws = {}
        self.rs = {}


class Chan:
    def __init__(self, sem, wait_all=False):
        self.sem = sem
        self.n = 0
        self.wait_all = wait_all


class Op:
    __slots__ = ("eng", "fn", "deps", "signal", "cnt", "chan", "chan_cnt", "idx")


def _key(o):
    return o.eng if o.chan is None else ("c", id(o.chan))


class Prog:
    ENGS = ("pe", "act", "dve", "pool", "sp")

    def __init__(self):
        self.ops = {e: [] for e in self.ENGS}
        self.n = 0
        self.last = {}

    def op(self, eng, fn, reads=(), writes=(), chan=None, extra=()):
        o = Op()
        o.eng = eng
        o.fn = fn
        o.signal = False
        o.cnt = 0
        o.chan = chan
        o.idx = self.n
        self.n += 1
        if chan is not None:
            chan.n += 1
            o.chan_cnt = chan.n
        else:
            o.chan_cnt = 0
        deps = {}

        def add(p, raw):
            if p is o:
                return
            if p.chan is None and o.chan is None and p.eng == eng:
                if eng == "pe" or not SAME_ENGINE_SYNC or (not raw and eng != "pool"):
                    return
            k = _key(p)
            q = deps.get(k)
            if q is None or q.idx < p.idx:
                deps[k] = p

        for b in reads:
            for w in b.ws.values():
                add(w, True)
        for b in writes:
            for r in b.rs.values():
                add(r, False)
            for w in b.ws.values():
                add(w, False)
        for p in extra:
            if p is not None:
                add(p, True)
        k = _key(o)
        for b in reads:
            b.rs[k] = o
        for b in writes:
            if b.rs:
                b.ws = {}
                b.rs = {}
            b.ws[k] = o
        o.deps = list(deps.values())
        for p in o.deps:
            if p.chan is None:
                p.signal = True
        self.ops[eng].append(o)
        if chan is None and fn is not None:
            self.last[eng] = o
        return o

    def lower(self, block, sems):
        for e in self.ENGS:
            c = 0
            for o in self.ops[e]:
                if o.chan is None and o.signal:
                    c += 1
                    o.cnt = c

        def run(ename):
            def body(eng):
                waited = {}
                for o in self.ops[ename]:
                    need = {}
                    for p in o.deps:
                        if p.chan is not None:
                            s = p.chan.sem
                            v = 16 * (p.chan.n if p.chan.wait_all else p.chan_cnt)
                        else:
                            s, v = sems[p.eng], p.cnt
                        k = id(s)
                        if need.get(k, (None, 0))[1] < v:
                            need[k] = (s, v)
                    for k, (s, v) in need.items():
                        if waited.get(k, 0) >= v:
                            continue
                        waited[k] = v
                        eng.wait_ge(s, v)
                    if o.fn is None:
                        continue
                    ins = o.fn(eng)
                    if o.chan is not None:
                        ins.then_inc(o.chan.sem, 16)
                    elif o.signal:
                        ins.then_inc(sems[ename], 1)
            return body

        block.tensor(run("pe"))
        block.scalar(run("act"))
        block.vector(run("dve"))
        block.gpsimd(run("pool"))
        block.sync(run("sp"))


def bcast_mid(ap, n):
    a = ap.ap
    return bass.AP(tensor=ap.tensor, offset=ap.offset, ap=[list(a[0]), [0, n]] + [list(x) for x in a[1:]])


def build_nc(stg=99):
    nc = bass.Bass("TRN2", target_bir_lowering=False)

    def din(name, shape):
        return nc.dram_tensor(name, shape, F32, kind="ExternalInput").ap()

    def dout(name, shape):
        return nc.dram_tensor(name, shape, F32, kind="ExternalOutput").ap()

    xin = din("xin", [2560, 1024])
    cols_d = din("cols", [128, NCOL])
    consts_d = din("consts", [128, NCONST])
    w_in0_d = din("w_in0", [1024, 1280])
    w_out0_d = din("w_out0", [1024, 1024])
    pool_w_d = din("pool_w", [512, 128])
    w_in1_d = din("w_in1", [1024, 2048])
    w_out1_d = din("w_out1", [1024, 1024])
    gwT_d = din("gwT", [512, 128])
    fgate_d = din("fgate", [2048, 2816])
    fup_d = din("fup", [2048, 2816])
    fdown_d = din("fdown", [5632, 1024])
    st_pool_d = din("st_pool", [60, 512])
    st_k_d = din("st_k", [512, 128])
    st_v_d = din("st_v", [512, 128])
    st_conv_d = din("st_conv", [120, 512])
    yout = dout("yout", [2304, 1024])
    pool_o = dout("pool_o", [5 * 128, 512])
    k_o = dout("k_o", [5 * 128, 128])
    v_o = dout("v_o", [5 * 128, 128])
    conv_o = dout("conv_o", [5 * 128, 512])
    gv_o = dout("gv_o", [4 * 64, 512])

    cnt = [0]

    class Region:
        def __init__(self, start, limit):
            self.off = start
            self.limit = limit

        def alloc(self, shape, dt):
            size = 1
            for s in shape[1:]:
                size *= s
            size *= 4 if dt == F32 else 2
            size = (size + 63) // 64 * 64
            at = self.off
            self.off += size
            assert self.off <= self.limit, ("SBUF overflow", self.off, self.limit)
            cnt[0] += 1
            return nc.alloc_sbuf_tensor_at("sb%d" % cnt[0], list(shape), dt, offset=at)

    perm = Region(SBUF_BASE, SBUF_LIMIT)
    x_res = perm.alloc([128, 8, 1280], F32)
    h_all = perm.alloc([128, 8, 1280], BF16)
    wmi = perm.alloc([128, 8, 2048], BF16)
    wmo = perm.alloc([128, 8, 1024], BF16)
    wgA = perm.alloc([128, 8, 384], BF16)
    wuA = perm.alloc([128, 8, 384], BF16)
    wdA = perm.alloc([128, 3, 1024], BF16)
    xs = [perm.alloc([128, 512], F32) for _ in range(2)]
    cols = perm.alloc([128, NCOL], F32)
    consts = perm.alloc([128, NCONST], F32)
    pw = perm.alloc([128, 4, 128], BF16)
    wm = perm.alloc([128, 4, 128], BF16)
    wms = perm.alloc([128, 4, 64], BF16)
    ident_bf = perm.alloc([128, 128], BF16)
    ones_bf = perm.alloc([128, 128], BF16)
    bd_bf = perm.alloc([128, 128], BF16)
    kTstz = [perm.alloc([128, 4, 128], BF16) for _ in range(2)]
    btab = perm.alloc([128, 8, 256], BF16)
    maskf = perm.alloc([128, 256], BF16)
    negsink = perm.alloc([128, 8], F32)
    Vst = perm.alloc([128, 4, 128], BF16)
    ulead_s = perm.alloc([128, 4, 4, 15], F32)
    clead_s = perm.alloc([128, 4, 4, 30], F32)
    gq8 = perm.alloc([128, 2], F32)
    kT_c = perm.alloc([128, 128], BF16)
    V_c = perm.alloc([128, 128], BF16)
    u_c = perm.alloc([128, 4, 16], F32)
    glu_c = perm.alloc([128, 4, 32], F32)
    xsq = perm.alloc([128, 8, 512], BF16)
    rs = perm.alloc([128, 512], F32)
    s_sb = [perm.alloc([128, 512], F32) for _ in range(2)]
    hid = [perm.alloc([128, 3, 512], BF16) for _ in range(2)]
    SCR = perm.off
    rb = Region(SCR, SBUF_LIMIT)
    wgB = rb.alloc([128, 8, 384], BF16)
    wuB = rb.alloc([128, 8, 384], BF16)
    wdB = rb.alloc([128, 3, 1024], BF16)
    r0 = Region(SCR, SBUF_LIMIT)
    u_ext = r0.alloc([128, 4, 320], F32)
    ptmp = [r0.alloc([128, 320], F32) for _ in range(2)]
    pooled = r0.alloc([128, 4, 256], BF16)
    cat = r0.alloc([128, 8, 256], BF16)
    rq2 = [r0.alloc([128, 2, 256], F32) for _ in range(2)]
    qn = r0.alloc([128, 4, 256], BF16)
    kn = r0.alloc([128, 256], F32)
    kTz = [r0.alloc([128, 384], BF16) for _ in range(2)]
    V_blk = r0.alloc([128, 4, 128], BF16)
    att_pn = [r0.alloc([128, 256], BF16) for _ in range(8)]
    att_PT = [r0.alloc([128, 1024], BF16) for _ in range(2)]
    att_sm = r0.alloc([128, 32, 4], F32)
    ptiny = r0.alloc([128, 4, 16], F32)
    r1 = Region(SCR, SBUF_LIMIT)
    glu_ext = r1.alloc([128, 4, 384], F32)
    glu_bf = r1.alloc([128, 4, 384], BF16)
    dgb = [r1.alloc([128, 16, 128], BF16) for _ in range(2)]
    mv = r1.alloc([128, 3, 256], F32)
    ctmp2 = r1.alloc([128, 2, 256], F32)
    ctmp = [ctmp2[:, 0, :], ctmp2[:, 1, :]]
    zu = r1.alloc([128, 4, 256], BF16)
    zg = r1.alloc([128, 512], F32)
    sig = zg[:].rearrange("p (a n) -> p a n", a=2)
    vn_bf = r1.alloc([128, 2, 512], BF16)
    cat1 = r1.alloc([128, 8, 256], BF16)
    st6 = r1.alloc([128, 2, 8], F32)
    mvz = r1.alloc([128, 2, 4], F32)
    gtmp = ctmp
    qsq = xsq

    psf = [nc.alloc_psum_tensor("psf%d" % i, [128, 512], F32) for i in range(7)]
    psb = nc.alloc_psum_tensor("psb", [128, 1024], BF16)

    with ExitStack() as es:
        def sem(name):
            return es.enter_context(nc.semaphore(name))

        sems = {e: sem("s_" + e) for e in ("pe", "act", "dve", "pool")}
        ch_setup = Chan(sem("c_setup"), wait_all=True)
        ch_setup2 = Chan(sem("c_setup2"), wait_all=True)
        ch_wmi = Chan(sem("c_wmi"))
        ch_wmo = Chan(sem("c_wmo"))
        ch_fA = Chan(sem("c_fA"))
        ch_fB = Chan(sem("c_fB"))
        ch_fAd = Chan(sem("c_fAd"))
        ch_fBd = Chan(sem("c_fBd"))
        ch_xin = [Chan(sem("c_xin%d" % i)) for i in range(2)]
        ch_out = [Chan(sem("c_out%d" % i)) for i in range(2)]
        ch_misc = Chan(sem("c_misc"))
        block = es.enter_context(nc.Block())
        P = Prog()

        XR = [Buf("xr%d" % t) for t in range(10)]
        HA = [Buf("ha%d" % t) for t in range(10)]
        B = {}

        def bf(name):
            if name not in B:
                B[name] = Buf(name)
            return B[name]

        PSB = [Buf("ps%d" % i) for i in range(7)]
        PTB = Buf("ptb")
        XS = [Buf("xs0"), Buf("xs1")]
        OUT = Buf("out")
        ps_rr = [0]
        xs_rr = [0]

        pinned = set()

        def psum(pin=False):
            while True:
                i = ps_rr[0] % 7
                ps_rr[0] += 1
                if i not in pinned:
                    break
            if pin:
                pinned.add(i)
            return psf[i], PSB[i]

        def unpin(pb):
            pinned.discard(PSB.index(pb))

        def stage():
            i = xs_rr[0] % 2
            xs_rr[0] += 1
            return xs[i], XS[i], i

        def mm(out, lhsT, rhs, start, stop, r, w):
            return P.op("pe", lambda e: e.matmul(out, lhsT, rhs, start=start, stop=stop), reads=r, writes=w)

        def tr(out, in_, ident, r, w):
            return P.op("pe", lambda e: e.transpose(out, in_, ident), reads=r, writes=w)

        def act(out, in_, func, r, w, bias=None, scale=None, accum=None):
            kw = {}
            if bias is not None:
                kw["bias"] = bias
            if scale is not None:
                kw["scale"] = scale
            if accum is not None:
                kw["accum_out"] = accum
            return P.op("act", lambda e: e.activation(out=out, in_=in_, func=func, **kw), reads=r, writes=w)

        def dve(fn, r, w):
            return P.op("dve", fn, reads=r, writes=w)

        def tt(out, in0, in1, op, r, w):
            return dve(lambda e: e.tensor_tensor(out, in0, in1, op), r, w)

        def stt(out, in0, scalar, in1, op0, op1, r, w):
            return dve(lambda e: e.scalar_tensor_tensor(out, in0, scalar, in1, op0, op1), r, w)

        def ts(out, in0, s1, s2, op0, op1, r, w):
            if op1 is None:
                return dve(lambda e: e.tensor_scalar(out, in0, s1, s2, op0), r, w)
            return dve(lambda e: e.tensor_scalar(out, in0, s1, s2, op0, op1), r, w)

        def cpy(out, in_, r, w):
            return dve(lambda e: e.tensor_copy(out, in_), r, w)

        def recip(out, in_, r, w):
            return dve(lambda e: e.reciprocal(out, in_), r, w)

        def mset(ap, v, w):
            return dve(lambda e: e.memset(ap, v), [], w)

        chan_last = {}

        def dma(q, out, in_, chan, r, w):
            o = P.op(q, lambda e: e.dma_start(out=out, in_=in_), reads=r, writes=w, chan=chan)
            chan_last[id(chan)] = o
            return o

        def fence(eng, deps):
            P.op(eng, None, extra=deps)

        scr_dma = []

        def dma_scr(q, out, in_, chan, r, w):
            scr_dma.append(dma(q, out, in_, chan, r, w))

        CONST = bf("consts")
        COLS = bf("cols")

        def col(i):
            return cols[:, i:i + 1]

        ident_f = consts[:, K_ID:K_ID + 128]

        def setup():
            dma("sp", cols[:], cols_d[:, :], ch_setup, [], [COLS])
            dma("sp", consts[:], consts_d[:, :], ch_setup, [], [CONST])
            act(ident_bf[:], ident_f, AF.Copy, [CONST], [bf("ident_bf")])
            act(bd_bf[:], consts[:, K_BD:K_BD + 128], AF.Copy, [CONST], [bf("bd_bf")])
            mset(ones_bf[:], 1.0, [bf("ones_bf")])
            ts(gq8[:, 0:1], col(C_GQ), 0.125, None, ALU.mult, None, [COLS], [bf("gq8")])
            mset(kT_c[:], 0.0, [bf("kT_c")])
            for h in range(8):
                ts(btab[:, h, :], consts[:, K_ND:K_ND + 256], 2.0 ** (-(h + 1)), None, ALU.mult, None, [CONST], [bf("btab")])
            act(maskf[:], consts[:, K_NDF:K_NDF + 256], AF.Copy, [CONST], [bf("maskf")])
            ts(negsink[:], cols[:, C_SINK:C_SINK + 8], -1.0, None, ALU.mult, None, [COLS], [bf("negsink")])
            mset(V_c[:], 0.0, [bf("V_c")])
            mset(u_c[:], 0.0, [bf("u_c")])
            mset(glu_c[:], 0.0, [bf("glu_c")])
            if stg == -1:
                return
            dma("pool", pw[:], pool_w_d.rearrange("(g c) d -> c g d", c=128), ch_setup2, [], [bf("pw")])
            dma("pool", Vst[:], st_v_d.rearrange("(s t) c -> t s c", t=128), ch_setup2, [], [bf("Vst")])
            if stg == -2:
                return
            s0, sb0, i0 = stage()
            dma("sp", s0[:, 0:512].rearrange("p (g i) -> p g i", g=4), gwT_d.rearrange("(g j) i -> j g i", j=128), ch_xin[i0], [], [sb0])
            tt(wm[:], s0[:, 0:512].rearrange("p (g i) -> p g i", g=4), bcast_mid(consts[:, K_TRIU:K_TRIU + 128], 4), ALU.mult,
               [sb0, CONST], [bf("wm")])
            gv = gwT_d.rearrange("(g j) i -> j g i", j=128)
            s0b, sb0b, i0b = stage()
            dma("sp", s0b[0:64, 0:256].rearrange("p (g i) -> p g i", g=4), gv[0:64, :, 0:64], ch_xin[i0b], [], [sb0b])
            dma("sp", s0b[64:128, 0:256].rearrange("p (g i) -> p g i", g=4), gv[0:64, :, 0:64], ch_xin[i0b], [], [sb0b])
            tt(wms[:], s0b[:, 0:256].rearrange("p (g i) -> p g i", g=4), bcast_mid(consts[:, K_TRIU64:K_TRIU64 + 64], 4), ALU.mult,
               [sb0b, CONST], [bf("wms")])
            if stg == -3:
                return
            s1, sb1, i1 = stage()
            dma("sp", s1[:, 0:512].rearrange("p (s c) -> p s c", s=4), st_k_d.rearrange("(s t) c -> t s c", t=128), ch_xin[i1], [], [sb1])
            ps, pb = psum()
            for s in range(4):
                tr(ps[:, s * 128:(s + 1) * 128], s1[:, s * 128:(s + 1) * 128], ident_f, [sb1, CONST], [pb])
            mset(kTstz[0][:], 0.0, [bf("kTst")])
            mset(kTstz[1][:], 0.0, [bf("kTst")])
            act(kTstz[0][0:64, :, :], ps[0:64, :].rearrange("p (s t) -> p s t", s=4), AF.Copy, [pb], [bf("kTst")])
            act(kTstz[1][64:128, :, :], ps[64:128, :].rearrange("p (s t) -> p s t", s=4), AF.Copy, [pb], [bf("kTst")])
            for s in range(4):
                dma("sp", k_o[(1 + s) * 128:(1 + s) * 128 + 64, :], s1[64:128, s * 128:(s + 1) * 128], ch_out[i1], [sb1], [OUT])
            if stg == -4:
                return
            s2, sb2, i2 = stage()
            dma("sp", s2[0:60, 0:512], st_pool_d[:, :], ch_xin[i2], [], [sb2])
            ps, pb = psum()
            for g in range(4):
                tr(ps[:, g * 64:g * 64 + 60], s2[0:60, g * 128:(g + 1) * 128], consts[0:60, K_ID:K_ID + 60], [sb2, CONST], [pb])
            act(ulead_s[:], ps[:, 0:256].rearrange("p (g n) -> p g n", g=4)[:, :, 0:60].rearrange("p g (s r) -> p g s r", s=4),
                AF.Copy, [pb], [bf("ulead_s")])
            if stg == -5:
                return
            s3, sb3, i3 = stage()
            dma("sp", s3[0:120, 0:512], st_conv_d[:, :], ch_xin[i3], [], [sb3])
            ps, pb = psum()
            for g in range(4):
                tr(ps[:, g * 128:g * 128 + 120], s3[0:120, g * 128:(g + 1) * 128], consts[0:120, K_ID:K_ID + 120], [sb3, CONST], [pb])
            act(clead_s[:], ps[:].rearrange("p (g n) -> p g n", g=4)[:, :, 0:120].rearrange("p g (s r) -> p g s r", s=4),
                AF.Copy, [pb], [bf("clead_s")])
            if stg == -6:
                return
            s4, sb4, i4 = stage()
            dma("sp", s4[:, 0:512].rearrange("p (s c) -> p s c", s=4), st_v_d.rearrange("(s t) c -> t s c", t=128), ch_xin[i4], [], [sb4])
            for s in range(4):
                dma("sp", v_o[(1 + s) * 128:(1 + s) * 128 + 64, :], s4[64:128, s * 128:(s + 1) * 128], ch_out[i4], [sb4], [OUT])

        WMI = bf("wmi")
        WMO = bf("wmo")
        SLA = bf("slotA")
        SLB = bf("slotB")
        SLAd = bf("slotAd")
        SLBd = bf("slotBd")

        def load_mixer_weights(layer, defer=False):
            jobs = []

            def dq(*args):
                jobs.append(lambda: dma(*args))

            if layer == 0:
                v = w_in0_d.rearrange("(kc p) n -> p kc n", p=128)
                dq("pool", wmi[:, :, 0:512], v[:, :, 0:512], ch_wmi, [], [WMI])
                for j in range(4):
                    for slot in range(2):
                        h = slot * 4 + j
                        dq("pool", wmi[:, :, 512 + j * 128 + slot * 64:512 + j * 128 + slot * 64 + 64],
                            v[:, :, 512 + h * 64:512 + h * 64 + 64], ch_wmi, [], [WMI])
                dq("pool", wmi[:, :, 1024:1280], v[:, :, 1024:1280], ch_wmi, [], [WMI])
                vo = w_out0_d
                dq("pool", wmo[:, 0:4, :], vo[0:512, :].rearrange("(kc p) n -> p kc n", p=128), ch_wmo, [], [WMO])
                for j in range(4):
                    for slot in range(2):
                        h = slot * 4 + j
                        dq("pool", wmo[slot * 64:(slot + 1) * 64, 4 + j, :], vo[512 + h * 64:512 + h * 64 + 64, :], ch_wmo, [], [WMO])
            else:
                v = w_in1_d.rearrange("(kc p) n -> p kc n", p=128)
                dq("pool", wmi[:, :, 0:1024], v[:, :, 0:1024], ch_wmi, [], [WMI])
                dq("pool", wmi[:, :, 1024:2048], v[:, :, 1024:2048], ch_wmi, [], [WMI])
                dq("pool", wmo[:], w_out1_d.rearrange("(kc p) n -> p kc n", p=128), ch_wmo, [], [WMO])
            if defer:
                return jobs
            for j_ in jobs:
                j_()
            return []

        def chunk_slot(c):
            if c % 2 == 0:
                return wgA, wuA, wdA, SLA, ch_fA
            return wgB, wuB, wdB, SLB, ch_fB

        def chunk_slot_d(c):
            if c % 2 == 0:
                return SLAd, ch_fAd
            return SLBd, ch_fBd

        def load_chunk(layer, c, extra=()):
            wg, wu, wd, sb_, ch = chunk_slot(c)
            nf = FCH[c]
            f0 = sum(FCH[:c])
            gv = fgate_d[layer * 1024:(layer + 1) * 1024, :].rearrange("(kc p) n -> p kc n", p=128)
            uv = fup_d[layer * 1024:(layer + 1) * 1024, :].rearrange("(kc p) n -> p kc n", p=128)
            dvw = fdown_d[layer * 2816 + f0 * 128:layer * 2816 + (f0 + nf) * 128, :].rearrange("(f p) n -> p f n", p=128)
            if extra:
                fence("pool", extra)
            dma("pool", wg[:, :, 0:nf * 128], gv[:, :, f0 * 128:(f0 + nf) * 128], ch, [], [sb_])
            dma("pool", wu[:, :, 0:nf * 128], uv[:, :, f0 * 128:(f0 + nf) * 128], ch, [], [sb_])
            sbd_, chd_ = chunk_slot_d(c)
            dma("pool", wd[:, 0:nf, :], dvw, chd_, [], [sbd_])

        def load_tile(g, t):
            row = (g * 10 + t) * 128
            for half in range(2):
                s, sb_, i = stage()
                dma("sp", s[:], xin[row:row + 128, half * 512:(half + 1) * 512], ch_xin[i], [], [sb_])
                ps, pb = psum()
                for k in range(4):
                    tr(ps[:, k * 128:(k + 1) * 128], s[:, k * 128:(k + 1) * 128], ident_f, [sb_, CONST], [pb])
                act(x_res[:, half * 4:half * 4 + 4, t * 128:(t + 1) * 128], ps[:].rearrange("p (k n) -> p k n", k=4), AF.Copy,
                    [pb], [XR[t]])

        def store_tile(g, t):
            row = (g * 10 + t - 2) * 128
            for half in range(2):
                s, sb_, i = stage()
                ps, pb = psum()
                for k in range(4):
                    kc = half * 4 + k
                    tr(ps[:, k * 128:(k + 1) * 128], x_res[:, kc, t * 128:(t + 1) * 128], ident_f, [XR[t], CONST], [pb])
                act(s[:], ps[:], AF.Copy, [pb], [sb_])
                dma("sp", yout[row:row + 128, half * 512:(half + 1) * 512], s[:], ch_out[i], [sb_], [OUT])

        def rows_out(wins, r0, nrows, dst, r):
            s, sb_, i = stage()
            ps, pb = psum()
            for g_, a in enumerate(wins):
                tr(ps[:, g_ * 128:(g_ + 1) * 128], a, ident_f, r + [CONST], [pb])
            act(s[:, 0:512], ps[:, :], AF.Copy, [pb], [sb_])
            dma("sp", dst, s[:, 0:512], ch_out[i], [sb_], [OUT])

        XSQ = bf("xsq")
        RS = bf("rs")

        def tiles_of(c0, c1):
            return list(range(c0 // 128, (c1 + 127) // 128))

        def norm(c0, c1, gcol):
            W = c1 - c0
            tl = tiles_of(c0, c1)
            xr = [XR[t] for t in tl]
            ha = [HA[t] for t in tl]
            for half in range(2):
                act(xsq[:, half * 4:half * 4 + 4, 0:W], x_res[:, half * 4:half * 4 + 4, c0:c1], AF.Square, xr, [XSQ])
            ps, pb = psum()
            for kc in range(8):
                mm(ps[:, 0:W], ones_bf[:], xsq[:, kc, 0:W], kc == 0, kc == 7, [XSQ, bf("ones_bf")], [pb])
            act(rs[:, 0:W], ps[:, 0:W], AF.Ln, [pb], [RS], bias=1e-6, scale=1.0 / 1024)
            act(rs[:, 0:W], rs[:, 0:W], AF.Exp, [RS], [RS], scale=-0.5)
            for kc in range(8):
                stt(h_all[:, kc, c0:c1], x_res[:, kc, c0:c1], col(gcol + kc), rs[:, 0:W], ALU.mult, ALU.mult,
                    xr + [RS, COLS], ha)

        def wout_residual(catt, CATB, c0):
            tl = tiles_of(c0, c0 + 256)
            xr = [XR[t] for t in tl]
            for pair in range(4):
                ps, pb = psum()
                for i in range(2):
                    m = pair * 2 + i
                    for kc in range(8):
                        mm(ps[:, i * 256:(i + 1) * 256], wmo[:, kc, m * 128:(m + 1) * 128], catt[:, kc, :], kc == 0, kc == 7,
                           [WMO, CATB], [pb])
                xv = x_res[:, pair * 2:pair * 2 + 2, c0:c0 + 256]
                tt(xv, ps[:].rearrange("p (a n) -> p a n", a=2), xv, ALU.add, [pb] + xr, xr)

        deferred_norm = []

        def flush_norm():
            while deferred_norm:
                deferred_norm.pop(0)()

        def mixer0(g, b, kind):
            c0 = b * 256
            tl = [2 * b, 2 * b + 1]
            ha = [HA[t] for t in tl]
            first = (g == 0 and b == 1)
            lastp = (g == 1 and b == 3)
            sample = kind == "sample"
            UE, PL, CAT, QN, KN, KTB, VB = bf("u_ext"), bf("pooled"), bf("cat"), bf("qn"), bf("kn"), bf("kTz"), bf("V_blk")
            hs = h_all[:, :, c0:c0 + 256]
            if sample:
                mset(u_ext[:], 0.0, [UE])
                mset(kTz[0][64:128, :], 0.0, [KTB])
                mset(kTz[1][0:64, :], 0.0, [KTB])
                for gI in range(4):
                    act(u_ext[:, gI, :].rearrange("p (s n) -> p s n", s=4)[:, :, 1:16], ulead_s[:, gI, :, :], AF.Copy, [bf("ulead_s")], [UE])
            else:
                cpy(u_ext[:, :, 0:16], u_c[:], [bf("u_c")], [UE])
                mset(kTz[0][64:128, :], 0.0, [KTB])
                mset(kTz[1][0:64, :], 0.0, [KTB])
                act(kTz[0][0:64, 0:128], kT_c[0:64, :], AF.Copy, [bf("kT_c")], [KTB])
                act(kTz[1][64:128, 0:128], kT_c[64:128, :], AF.Copy, [bf("kT_c")], [KTB])
                act(V_blk[:, 0, :], V_c[:], AF.Copy, [bf("V_c")], [VB])
            for pair in range(2):
                ps, pb = psum()
                for i in range(2):
                    m = pair * 2 + i
                    for kc in range(8):
                        mm(ps[:, i * 256:(i + 1) * 256], wmi[:, kc, m * 128:(m + 1) * 128], hs[:, kc, :], kc == 0, kc == 7,
                           [WMI] + ha, [pb])
                if not sample:
                    act(u_ext[:, pair * 2:pair * 2 + 2, 16:272], ps[:].rearrange("p (a n) -> p a n", a=2), AF.Copy, [pb], [UE])
                else:
                    for i in range(2):
                        m = pair * 2 + i
                        act(u_ext[:, m, :].rearrange("p (s n) -> p s n", s=4)[:, :, 16:80],
                            ps[:, i * 256:(i + 1) * 256].rearrange("p (s n) -> p s n", s=4), AF.Copy, [pb], [UE])
            QSQ = XSQ
            qps = []
            for pair in range(2):
                ps, pb = psum(pin=True)
                for i in range(2):
                    m = pair * 2 + i
                    for kc in range(8):
                        mm(ps[:, i * 256:(i + 1) * 256], wmi[:, kc, 512 + m * 128:512 + (m + 1) * 128], hs[:, kc, :], kc == 0, kc == 7,
                           [WMI] + ha, [pb])
                act(qsq[:, pair * 2:pair * 2 + 2, 0:256], ps[:].rearrange("p (a n) -> p a n", a=2), AF.Square, [pb], [QSQ])
                qps.append((ps, pb))
            psk, pbk = psum(pin=True)
            for kc in range(8):
                mm(psk[:, 0:256], wmi[:, kc, 1024:1152], hs[:, kc, :], kc == 0, kc == 7, [WMI] + ha, [pbk])
            act(qsq[:, 4, 0:256], psk[:, 0:256], AF.Square, [pbk], [QSQ])
            psv, pbv = psum()
            if not sample:
                for t_ in range(2):
                    for kc in range(8):
                        mm(psv[:, t_ * 128:(t_ + 1) * 128], h_all[:, kc, c0 + t_ * 128:c0 + (t_ + 1) * 128], wmi[:, kc, 1152:1280],
                           kc == 0, kc == 7, [WMI] + ha, [pbv])
                act(V_blk[:, 1:3, :], psv[:, 0:256].rearrange("p (a n) -> p a n", a=2), AF.Copy, [pbv], [VB])
                if lastp and (_LMASK & 1):
                    sv_, sbv_, iv_ = stage()
                    act(sv_[:, 0:128], psv[:, 128:256], AF.Copy, [pbv], [sbv_])
                    dma("sp", v_o[0:128, :], sv_[:, 0:128], ch_out[iv_], [sbv_], [OUT])
            else:
                for s in range(4):
                    for kc in range(8):
                        mm(psv[0:64, s * 128:(s + 1) * 128], h_all[:, kc, c0 + s * 64:c0 + (s + 1) * 64], wmi[:, kc, 1152:1280],
                           kc == 0, kc == 7, [WMI] + ha, [pbv])
                act(V_blk[0:64, :, :], psv[0:64, :].rearrange("p (a n) -> p a n", a=4), AF.Copy, [pbv], [VB])
                sv_, sbv_, iv_ = stage()
                act(sv_[0:64, 0:512], psv[0:64, :], AF.Copy, [pbv], [sbv_])
                for s in range(4 if (_SMASK & 1) else 0):
                    dma("sp", v_o[(1 + s) * 128 + 64:(2 + s) * 128, :], sv_[0:64, s * 128:(s + 1) * 128], ch_out[iv_], [sbv_], [OUT])
            for pair in range(3):
                RQ = bf("rq%d" % (pair % 2))
                rq = rq2[pair % 2]
                ps, pb = psum()
                n_ = 2 if pair < 2 else 1
                for i in range(n_):
                    m = pair * 2 + i
                    mm(ps[:, i * 256:(i + 1) * 256], bd_bf[:], qsq[:, m, 0:256], True, True, [bf("bd_bf"), QSQ], [pb])
                act(rq[:, 0:n_, :], ps[:, 0:n_ * 256].rearrange("p (a n) -> p a n", a=n_), AF.Ln, [pb], [RQ], bias=1e-6, scale=1.0 / 64)
                act(rq[:, 0:n_, :], rq[:, 0:n_, :], AF.Exp, [RQ], [RQ], scale=-0.5)
                if pair < 2:
                    qp, qb = qps[pair]
                    for i in range(2):
                        m = pair * 2 + i
                        stt(qn[:, m, :], qp[:, i * 256:(i + 1) * 256], gq8[:, 0:1], rq[:, i, :], ALU.mult, ALU.mult,
                            [qb, RQ, bf("gq8")], [QN])
                    unpin(qb)
                else:
                    stt(kn[:], psk[:, 0:256], col(C_GK), rq[:, 0, :], ALU.mult, ALU.mult, [pbk, RQ, COLS], [KN])
                    act(kTz[0][0:64, 128:384], kn[0:64, :], AF.Copy, [KN], [KTB])
                    act(kTz[1][64:128, 128:384], kn[64:128, :], AF.Copy, [KN], [KTB])
                    unpin(pbk)
            if lastp and (_LMASK & 4):
                s_, sb_, i_ = stage()
                ps, pb = psum()
                tr(ps[:, 0:128], kn[:, 128:256], ident_f, [KN, CONST], [pb])
                act(s_[:, 0:128], ps[:, 0:128], AF.Copy, [pb], [sb_])
                dma("sp", k_o[0:128, :], s_[:, 0:128], ch_out[i_], [sb_], [OUT])
            if sample and (_SMASK & 4):
                s_, sb_, i_ = stage()
                ps, pb = psum()
                for t_ in range(2):
                    tr(ps[:, t_ * 128:(t_ + 1) * 128], kn[:, t_ * 128:(t_ + 1) * 128], ident_f, [KN, CONST], [pb])
                act(s_[:, 0:256], ps[:, 0:256], AF.Copy, [pb], [sb_])
                for s in range(4):
                    dma("sp", k_o[(1 + s) * 128 + 64:(2 + s) * 128, :],
                        s_[(s % 2) * 64:(s % 2) * 64 + 64, (s // 2) * 128:(s // 2) * 128 + 128], ch_out[i_], [sb_], [OUT])
            SM = bf("att_sm")
            units = []
            if not sample:
                for t_i in range(2):
                    segs = [(t_i * 128, None, V_blk[:, t_i, :], 128, KTB, VB),
                            ((t_i + 1) * 128, None, V_blk[:, t_i + 1, :], 128, KTB, VB)]
                    units.append((128, t_i * 128, segs, first and t_i == 0, t_i))
            elif _SMASK & 8:
                for s in range(4):
                    segs = [(None, s, Vst[:, s, :], 128, bf("kTst"), bf("Vst")),
                            (128 + s * 64, None, V_blk[0:64, s, :], 64, KTB, VB)]
                    units.append((64, s * 64, segs, False, s // 2))
            waves = [(u, slot) for u in units for slot in range(2)]
            nW = len(waves)
            SMW = [Buf("smw%d" % wi) for wi in range(nW)]
            mset(att_sm[:], 0.0, SMW + [SM])
            W = 320 if sample else 272
            PT0, PT1 = bf("ptmp0"), bf("ptmp1")
            A_, B_ = ptmp[0], ptmp[1]

            def dview(ap2):
                if sample:
                    return ap2.rearrange("p (s n) -> p s n", s=4)[:, :, 16:80]
                return ap2[:, 16:272]

            def oview(ap2):
                if sample:
                    return ap2.rearrange("p (s n) -> p s n", s=4)
                return ap2

            def ptt(out, in0, in1, r, w):
                return P.op("pool", lambda e: e.tensor_tensor(out, in0, in1, ALU.add), reads=r, writes=w)

            def pool_group(gI, wnd):
                u_ = u_ext[:, gI, :]
                ptt(A_[:, 1:W], u_[:, 1:W], u_[:, 0:W - 1], [UE], [PT0])
                cur, curB = A_, PT0
                if wnd >= 4:
                    ptt(B_[:, 3:W], A_[:, 3:W], A_[:, 1:W - 2], [PT0], [PT1])
                    cur, curB = B_, PT1
                if wnd >= 8:
                    ptt(A_[:, 7:W], B_[:, 7:W], B_[:, 3:W - 4], [PT1], [PT0])
                    cur, curB = A_, PT0
                if wnd >= 16:
                    ptt(B_[:, 15:W], A_[:, 15:W], A_[:, 7:W - 8], [PT0], [PT1])
                    cur, curB = B_, PT1
                stt(oview(pooled[:, gI, :]), dview(cur[:, 0:W]), 1.0 / wnd, dview(u_), ALU.mult, ALU.subtract, [curB, UE], [PL])
                if first:
                    tt(ptiny[:, gI, :], cur[:, 16:32], consts[:, K_ICNT + gI * 16:K_ICNT + (gI + 1) * 16], ALU.mult,
                       [curB, CONST], [bf("ptiny")])
                    tt(pooled[:, gI, 0:16], ptiny[:, gI, :], u_[:, 16:32], ALU.subtract, [bf("ptiny"), UE, PL], [PL])

            pool_pending = []
            for gI, wnd in enumerate((2, 4, 8, 16)):
                if gI < 2:
                    pool_group(gI, wnd)
                else:
                    pool_pending.append(lambda gI=gI, wnd=wnd: pool_group(gI, wnd))
            wstate = {}
            po_tiles = {}
            last_wave_of_tile = {}
            for wi, (u, slot) in enumerate(waves):
                last_wave_of_tile[u[4]] = wi

            def kseg(slot, sg):
                c_, st_, _, n, _, _ = sg
                if st_ is not None:
                    return kTstz[slot][:, st_, 0:n]
                return kTz[slot][:, c_:c_ + n]

            def emitS(i):
                (Mq, a, segs, masked, tk), slot = waves[i]
                nk = sum(sg[3] for sg in segs)
                banks = [psum(), psum()]
                wstate[i] = banks
                for j in range(4):
                    h = slot * 4 + j
                    pS, pSb = banks[j // 2]
                    off = (j % 2) * 256
                    mm(pS[0:Mq, off:off + nk], ident_bf[:, 0:Mq], btab[:, h, 0:nk], True, False, [bf("ident_bf"), bf("btab")], [pSb])
                    if masked:
                        mm(pS[0:Mq, off:off + nk], ident_bf[:, 0:Mq], maskf[:, 0:nk], False, False, [bf("ident_bf"), bf("maskf")], [pSb])
                    ko = 0
                    for si, sg in enumerate(segs):
                        n = sg[3]
                        mm(pS[0:Mq, off + ko:off + ko + n], qn[:, j, a:a + Mq], kseg(slot, sg), False, si == len(segs) - 1,
                           [QN, sg[4]], [pSb])
                        ko += n

            def emitSoft(i):
                (Mq, a, segs, masked, tk), slot = waves[i]
                nk = sum(sg[3] for sg in segs)
                set_ = i % 2
                banks = wstate[i]
                PNB = bf("att_pn%d" % set_)
                smw = att_sm[0:Mq, i * 4:(i + 1) * 4, :]
                for j in range(4):
                    h = slot * 4 + j
                    pS, pSb = banks[j // 2]
                    off = (j % 2) * 256
                    hb = set_ * 4 + j
                    act(att_pn[hb][0:Mq, 0:nk], pS[0:Mq, off:off + nk], AF.Exp, [pSb, bf("negsink"), SMW[i]], [PNB, SMW[i]],
                        bias=negsink[0:Mq, h:h + 1], scale=1.0, accum=att_sm[0:Mq, i * 4 + j, 0:1])
                act(smw[:, :, 1:2], smw[:, :, 0:1], AF.Ln, [SMW[i]], [SMW[i]], bias=1.0, scale=1.0)
                act(smw[:, :, 2:3], smw[:, :, 1:2], AF.Exp, [SMW[i]], [SMW[i]], scale=-1.0)
                for j in range(4):
                    hb = set_ * 4 + j
                    ts(att_pn[hb][0:Mq, 0:nk], att_pn[hb][0:Mq, 0:nk], att_sm[0:Mq, i * 4 + j, 2:3], None, ALU.mult, None, [PNB, SMW[i]], [PNB])

            def emitT(i):
                (Mq, a, segs, masked, tk), slot = waves[i]
                set_ = i % 2
                PNB = bf("att_pn%d" % set_)
                PTSB = bf("att_PT%d" % set_)
                for j in range(4):
                    hb = set_ * 4 + j
                    ko = 0
                    for si, sg in enumerate(segs):
                        n = sg[3]
                        tr(psb[0:n, j * 256 + si * 128:j * 256 + si * 128 + Mq], att_pn[hb][0:Mq, ko:ko + n],
                           ident_bf[0:Mq, 0:Mq], [PNB, bf("ident_bf")], [PTB])
                        ko += n
                if Mq == 128 and all(sg[3] == 128 for sg in segs):
                    cpy(att_PT[set_][:, :], psb[:, :], [PTB], [PTSB])
                else:
                    for j in range(4):
                        for si, sg in enumerate(segs):
                            n = sg[3]
                            cpy(att_PT[set_][0:n, j * 256 + si * 128:j * 256 + si * 128 + Mq],
                                psb[0:n, j * 256 + si * 128:j * 256 + si * 128 + Mq], [PTB], [PTSB])

            def emitPV(i):
                (Mq, a, segs, masked, tk), slot = waves[i]
                set_ = i % 2
                PTSB = bf("att_PT%d" % set_)
                if tk not in po_tiles:
                    po_tiles[tk] = psum(pin=True)
                ps_o, pb_o = po_tiles[tk]
                for j in range(4):
                    for si, sg in enumerate(segs):
                        n = sg[3]
                        mm(ps_o[slot * 64:(slot + 1) * 64, j * 128 + (a % 128):j * 128 + (a % 128) + Mq],
                           sg[2][0:n, slot * 64:(slot + 1) * 64], att_PT[set_][0:n, j * 256 + si * 128:j * 256 + si * 128 + Mq],
                           si == 0, si == len(segs) - 1, [sg[5], PTSB], [pb_o])
                if last_wave_of_tile[tk] == i:
                    act(cat[:, 4:8, tk * 128:(tk + 1) * 128], ps_o[:].rearrange("p (j n) -> p j n", j=4), AF.Copy, [pb_o], [CAT])
                    unpin(pb_o)

            for i in range(nW + 2):
                if i < nW:
                    emitS(i)
                if 0 <= i - 1 < nW:
                    emitT(i - 1)
                if 0 <= i - 2 < nW:
                    emitPV(i - 2)
                if i < nW:
                    emitSoft(i)
                if pool_pending and i in (0, 1):
                    pool_pending.pop(0)()
            while pool_pending:
                pool_pending.pop(0)()
            if not sample:
                cpy(u_c[:], u_ext[:, :, 256:272], [UE], [bf("u_c")])
            flush_norm()
            for pair in range(2):
                ps, pb = psum()
                for i in range(2):
                    gI = pair * 2 + i
                    mm(ps[:, i * 256:(i + 1) * 256], pw[:, gI, :], pooled[:, gI, :], True, True, [bf("pw"), PL], [pb])
                for i in range(2):
                    gI = pair * 2 + i
                    act(cat[:, gI, :], ps[:, i * 256:(i + 1) * 256], AF.Copy, [pb, COLS], [CAT], scale=col(C_PSC + gI))
            if lastp and (_LMASK & 2):
                rows_out([u_ext[:, gI, 144:272] for gI in range(4)], 112, 16, pool_o[0:128, :], [UE])
            if sample and (_SMASK & 2):
                for s in range(4):
                    w0 = min(max(s * 80 + 80 - 128, 0), 320 - 128)
                    rows_out([u_ext[:, gI, w0:w0 + 128] for gI in range(4)], s * 80 + 64 - w0, 16, pool_o[(1 + s) * 128:(2 + s) * 128, :], [UE])
            wout_residual(cat, CAT, c0)
            if not sample:
                act(kT_c[0:64, :], kTz[0][0:64, 256:384], AF.Copy, [KTB], [bf("kT_c")])
                act(kT_c[64:128, :], kTz[1][64:128, 256:384], AF.Copy, [KTB], [bf("kT_c")])
                act(V_c[:], V_blk[:, 2, :], AF.Copy, [VB], [bf("V_c")])
            deferred_norm.append(lambda c0=c0: norm(c0, c0 + 256, C_GFFN + 0))

        PTAPS = []
        NPT = [0]

        def tap_regions():
            regs = [(wgA[:].rearrange("p a (b c) -> p (a b) c", c=128), [SLA]),
                    (wuA[:].rearrange("p a (b c) -> p (a b) c", c=128), [SLA]),
                    (wdA[:].rearrange("p a (b c) -> p (a b) c", c=128), [SLAd]),
                    (hid[0][:].rearrange("p a (b c) -> p (a b) c", c=128), [bf("hid0")]),
                    (hid[1][:].rearrange("p a (b c) -> p (a b) c", c=128), [bf("hid1")])]
            return regs

        def build_resident_taps():
            del PTAPS[:]
            t0 = 0
            for view, bufs in tap_regions():
                n = min(view.shape[1], 124 - t0)
                if n <= 0:
                    break
                wsrc = cols[:, C_CW + t0:C_CW + t0 + n]
                wbc = bass.AP(tensor=wsrc.tensor, offset=wsrc.offset, ap=[list(x) for x in wsrc.ap] + [[0, 128]])
                tt(view[:, 0:n, :], bcast_mid(ident_bf[:], n), wbc, ALU.mult, [bf("ident_bf"), COLS], bufs)
                for j in range(n):
                    PTAPS.append((view[:, j, :], bufs))
                t0 += n
            NPT[0] = t0

        def mixer1(g, b, kind):
            c0 = b * 256
            tl = [2 * b, 2 * b + 1]
            ha = [HA[t] for t in tl]
            lastp = (g == 1 and b == 3)
            sample = kind == "sample"
            halo = kind == "halo"
            GE, GBF, SIG, CAT1 = bf("glu_ext"), bf("glu_bf"), bf("zg"), bf("cat1")
            hs = h_all[:, :, c0:c0 + 256]
            W = 384 if sample else 288
            if sample:
                mset(glu_ext[:], 0.0, [GE])
                for ci in range(4):
                    act(glu_ext[:, ci, :].rearrange("p (s n) -> p s n", s=4)[:, :, 2:32], clead_s[:, ci, :, :], AF.Copy, [bf("clead_s")], [GE])
            else:
                cpy(glu_ext[:, :, 0:32], glu_c[:], [bf("glu_c")], [GE])
            for pair in range(2):
                psa, pba = psum()
                psg, pbg = psum()
                for (ps_, pb_, base) in ((psa, pba, 0), (psg, pbg, 512)):
                    for i in range(2):
                        m = pair * 2 + i
                        for kc in range(8):
                            mm(ps_[:, i * 256:(i + 1) * 256], wmi[:, kc, base + m * 128:base + (m + 1) * 128], hs[:, kc, :], kc == 0, kc == 7,
                               [WMI] + ha, [pb_])
                act(sig, psg[:].rearrange("p (a n) -> p a n", a=2), AF.Sigmoid, [pbg], [SIG])
                if not sample:
                    tt(glu_ext[:, pair * 2:pair * 2 + 2, 32:288], psa[:].rearrange("p (a n) -> p a n", a=2), sig, ALU.mult,
                       [pba, SIG], [GE])
                else:
                    for i in range(2):
                        m = pair * 2 + i
                        tt(glu_ext[:, m, :].rearrange("p (s n) -> p s n", s=4)[:, :, 32:96],
                           psa[:, i * 256:(i + 1) * 256].rearrange("p (s n) -> p s n", s=4),
                           sig[:, i, :].rearrange("p (s n) -> p s n", s=4), ALU.mult, [pba, SIG], [GE])
            flush_norm()
            if not sample:
                cpy(glu_c[:], glu_ext[:, :, 256:288], [GE], [bf("glu_c")])
            if lastp:
                rows_out([glu_ext[:, ci, 160:288] for ci in range(4)], 96, 32, conv_o[0:128, :], [GE])
            if sample and (_SMASK & 32):
                for s in range(4):
                    w0 = min(max(s * 96 + 96 - 128, 0), 384 - 128)
                    rows_out([glu_ext[:, ci, w0:w0 + 128] for ci in range(4)], s * 96 + 64 - w0, 32, conv_o[(1 + s) * 128:(2 + s) * 128, :], [GE])
            if halo:
                return
            act(glu_bf[:, :, 0:W], glu_ext[:, :, 0:W], AF.Copy, [GE], [GBF])
            CSQ = XSQ
            csq = xsq[:, 0:4, 0:256]
            c_bf = xsq[:, 4:8, 0:256]
            dgc = [0]
            conv_ps = []
            built = {}

            def tap_operand(t):
                if t < NPT[0]:
                    return PTAPS[t]
                t0 = NPT[0] + ((t - NPT[0]) // 16) * 16
                if t0 not in built:
                    nt = min(16, 124 - t0)
                    r_ = dgc[0] % 2
                    dgc[0] += 1
                    DGB = bf("dgb%d" % r_)
                    wsrc = cols[:, C_CW + t0:C_CW + t0 + nt]
                    wbc = bass.AP(tensor=wsrc.tensor, offset=wsrc.offset, ap=[list(x) for x in wsrc.ap] + [[0, 128]])
                    tt(dgb[r_][:, 0:nt, :], bcast_mid(ident_bf[:], nt), wbc, ALU.mult, [bf("ident_bf"), COLS], [DGB])
                    built[t0] = (r_, DGB)
                r_, DGB = built[t0]
                return dgb[r_][:, t - t0, :], [DGB]

            for t_ in range(NPT[0], 124, 16):
                tap_operand(t_)
            ZU = bf("zu")
            for pair in range(2):
                ps, pb = psum()
                for i in range(2):
                    m = pair * 2 + i
                    for kc in range(8):
                        mm(ps[:, i * 256:(i + 1) * 256], wmi[:, kc, 1024 + m * 128:1024 + (m + 1) * 128], hs[:, kc, :], kc == 0, kc == 7,
                           [WMI] + ha, [pb])
                act(zu[:, pair * 2:pair * 2 + 2, :], ps[:].rearrange("p (a n) -> p a n", a=2), AF.Gelu, [pb], [ZU])
            VNB, MVZ = bf("vn_bf"), bf("mvz")
            zgt = [zg[:], ctmp2[:].rearrange("p a n -> p (a n)")]
            ZGt = [[bf("zg")], [bf("ctmp0"), bf("ctmp1")]]
            for t_i in range(2):
                ps, pb = psum()
                for kc in range(8):
                    mm(ps[:], h_all[:, kc, c0 + t_i * 128:c0 + (t_i + 1) * 128], wmi[:, kc, 1536:2048], kc == 0, kc == 7, [WMI] + ha, [pb])
                act(zgt[t_i], ps[:], AF.Gelu, [pb], ZGt[t_i])
            for t_i in range(2):
                dve(lambda e, t_i=t_i: e.bn_stats(st6[:, t_i, 0:6], zgt[t_i]), ZGt[t_i], [bf("st6")])
                dve(lambda e, t_i=t_i: e.bn_aggr(mvz[:, t_i, 0:2], st6[:, t_i, 0:6]), [bf("st6")], [MVZ])
            act(mvz[:, :, 2:3], mvz[:, :, 1:2], AF.Ln, [MVZ], [MVZ], bias=1e-5, scale=1.0)
            act(mvz[:, :, 3:4], mvz[:, :, 2:3], AF.Exp, [MVZ], [MVZ], scale=-0.5)
            for t_i in range(2):
                z_ = zgt[t_i]
                ts(z_, z_, mvz[:, t_i, 0:1], mvz[:, t_i, 3:4], ALU.subtract, ALU.mult, ZGt[t_i] + [MVZ], ZGt[t_i])
                tt(z_, z_, consts[:, K_LNG:K_LNG + 512], ALU.mult, ZGt[t_i] + [CONST], ZGt[t_i])
                tt(z_, z_, consts[:, K_LNB:K_LNB + 512], ALU.add, ZGt[t_i] + [CONST], ZGt[t_i])
                act(vn_bf[:, t_i, :], z_, AF.Copy, ZGt[t_i], [VNB])
                if sample:
                    for half in range(2):
                        s = t_i * 2 + half
                        if _SMASK & 16:
                            dma_scr("sp", gv_o[s * 64:(s + 1) * 64, :], z_[half * 64:(half + 1) * 64, :], ch_misc, ZGt[t_i], [OUT])
            for pair in range(2):
                ps, pb = psum(pin=True)
                for i in range(2):
                    ci = pair * 2 + i
                    for k in range(31):
                        lw, lwb = tap_operand(ci * 31 + k)
                        if not sample:
                            mm(ps[:, i * 256:(i + 1) * 256], lw, glu_bf[:, ci, 2 + k:2 + k + 256], k == 0, k == 30, lwb + [GBF], [pb])
                        else:
                            for s in range(4):
                                P.op("pe", lambda e, o_=ps[:, i * 256 + s * 64:i * 256 + (s + 1) * 64], l_=lw,
                                     r2_=glu_bf[:, ci, s * 96 + 2 + k:s * 96 + 2 + k + 64], st_=(k == 0 and s == 0), sp_=(k == 30 and s == 3):
                                     e.matmul(o_, l_, r2_, start=st_, stop=sp_, skip_group_check=True), reads=lwb + [GBF], writes=[pb])
                for i in range(2):
                    ci = pair * 2 + i
                    act(c_bf[:, ci, :], ps[:, i * 256:(i + 1) * 256], AF.Identity, [pb, COLS], [CSQ], bias=col(C_CB + ci))
                    act(csq[:, ci, :], ps[:, i * 256:(i + 1) * 256], AF.Square, [pb, COLS], [CSQ], bias=col(C_CB + ci))
                conv_ps.append((ps, pb))
            if not sample:
                for pair in range(2):
                    ps, pb = psum()
                    for i in range(2):
                        gg = pair * 2 + i
                        for t_i in range(2):
                            mm(ps[:, i * 256 + t_i * 128:i * 256 + (t_i + 1) * 128], vn_bf[:, t_i, gg * 128:(gg + 1) * 128], wm[:, gg, :],
                               True, True, [VNB, bf("wm")], [pb])
                    for i in range(2):
                        gg = pair * 2 + i
                        GT = bf("ctmp%d" % i)
                        tt(gtmp[i][:].rearrange("p (a n) -> p a n", a=2), ps[:, i * 256:(i + 1) * 256].rearrange("p (a n) -> p a n", a=2),
                           bcast_mid(consts[:, K_GB + gg * 128:K_GB + (gg + 1) * 128], 2), ALU.add, [pb, CONST], [GT])
                        tt(cat1[:, 4 + gg, :], gtmp[i][:], zu[:, gg, :], ALU.mult, [GT, ZU], [CAT1])
            else:
                psH = [psum(pin=True), psum(pin=True)]
                for half in range(2):
                    ps, pb = psH[half]
                    for gg in range(4):
                        for t_i in range(2):
                            o_ = gg * 128 + t_i * 64
                            mm(ps[:, o_:o_ + 64], vn_bf[half * 64:(half + 1) * 64, t_i, gg * 128:(gg + 1) * 128],
                               wms[half * 64:(half + 1) * 64, gg, :], True, True, [VNB, bf("wms")], [pb])
                for gg in range(4):
                    i = gg % 2
                    GT = bf("ctmp%d" % i)
                    for half in range(2):
                        ps, pb = psH[half]
                        tt(gtmp[i][:].rearrange("p (t h n) -> p t h n", t=2, h=2)[:, :, half, :],
                           ps[:, gg * 128:(gg + 1) * 128].rearrange("p (t n) -> p t n", t=2),
                           bcast_mid(consts[:, K_GB + gg * 128:K_GB + gg * 128 + 64], 2), ALU.add, [pb, CONST], [GT])
                    tt(cat1[:, 4 + gg, :], gtmp[i][:], zu[:, gg, :], ALU.mult, [GT, ZU], [CAT1])
                unpin(psH[0][1])
                unpin(psH[1][1])
            MV = bf("mv")
            pst, pbt = psum()
            for ci in range(4):
                mm(pst[:, 0:256], ones_bf[:], c_bf[:, ci, :], ci == 0, ci == 3, [CSQ, bf("ones_bf")], [pbt])
            for ci in range(4):
                mm(pst[:, 256:512], ones_bf[:], csq[:, ci, :], ci == 0, ci == 3, [CSQ, bf("ones_bf")], [pbt])
            act(mv[:, 0, :], pst[:, 0:256], AF.Copy, [pbt], [MV], scale=1.0 / 512)
            tt(mv[:, 2, :], mv[:, 0, :], mv[:, 0, :], ALU.mult, [MV], [MV])
            stt(mv[:, 1, :], pst[:, 256:512], 1.0 / 512, mv[:, 2, :], ALU.mult, ALU.subtract, [pbt, MV], [MV])
            act(mv[:, 1, :], mv[:, 1, :], AF.Ln, [MV], [MV], bias=1e-5, scale=1.0)
            act(mv[:, 1, :], mv[:, 1, :], AF.Exp, [MV], [MV], scale=-0.5)
            for ci in range(4):
                ps, pb = conv_ps[ci // 2]
                i = ci % 2
                CT = bf("ctmp%d" % (ci % 2))
                stt(ctmp[ci % 2][:], ps[:, i * 256:(i + 1) * 256], col(C_CB + ci), mv[:, 0, :], ALU.add, ALU.subtract, [pb, MV, COLS], [CT])
                tt(ctmp[ci % 2][:], ctmp[ci % 2][:], mv[:, 1, :], ALU.mult, [CT, MV], [CT])
                act(cat1[:, ci, :], ctmp[ci % 2][:], AF.Silu, [CT, COLS], [CAT1], bias=col(C_LB + ci), scale=col(C_LG + ci))
                if i == 1:
                    unpin(pb)
            wout_residual(cat1, CAT1, c0)
            deferred_norm.append(lambda c0=c0: norm(c0, c0 + 256, C_GFFN + 8))

        def ffn(layer, blocks, after_block, next_chunk0_loader, mixer_end_ops, bg_loads=()):
            bg = list(bg_loads)
            pending = [None]
            HID = [bf("hid0"), bf("hid1")]
            SSB = [bf("ssb0"), bf("ssb1")]
            it = [0]

            def down(c, blk, hb):
                wg, wu, wd, sb_, ch = chunk_slot(c)
                nf = FCH[c]
                c0_, c1_ = blk
                W = c1_ - c0_
                xr = [XR[t] for t in tiles_of(c0_, c1_)]
                for m in range(8):
                    ps, pb = psum()
                    for fi in range(nf):
                        mm(ps[:, 0:W], wd[:, fi, m * 128:(m + 1) * 128], hid[hb][:, fi, 0:W], fi == 0, fi == nf - 1, [chunk_slot_d(c)[0], HID[hb]], [pb])
                    tt(x_res[:, m, c0_:c1_], ps[:, 0:W], x_res[:, m, c0_:c1_], ALU.add, [pb] + xr, xr)
                if c == len(FCH) - 1:
                    after_block(blk)

            load_chunk(layer, 1, extra=mixer_end_ops)
            for c in range(len(FCH)):
                wg, wu, wd, sb_, ch = chunk_slot(c)
                nf = FCH[c]
                for blk in blocks:
                    c0_, c1_ = blk
                    W = c1_ - c0_
                    ha = [HA[t] for t in tiles_of(c0_, c1_)]
                    hb = it[0] % 2
                    it[0] += 1
                    for fi in range(nf):
                        psg, pbg = psum()
                        for kc in range(8):
                            mm(psg[:, 0:W], wg[:, kc, fi * 128:(fi + 1) * 128], h_all[:, kc, c0_:c1_], kc == 0, kc == 7, [sb_] + ha, [pbg])
                        psu, pbu = psum()
                        for kc in range(8):
                            mm(psu[:, 0:W], wu[:, kc, fi * 128:(fi + 1) * 128], h_all[:, kc, c0_:c1_], kc == 0, kc == 7, [sb_] + ha, [pbu])
                        k_ = fi % 2
                        act(s_sb[k_][:, 0:W], psg[:, 0:W], AF.Silu, [pbg], [SSB[k_]])
                        tt(hid[hb][:, fi, 0:W], psu[:, 0:W], s_sb[k_][:, 0:W], ALU.mult, [pbu, SSB[k_]], [HID[hb]])
                    if pending[0] is not None:
                        down(*pending[0])
                    pending[0] = (c, blk, hb)
                if c + 2 < len(FCH):
                    if pending[0] is not None:
                        down(*pending[0])
                        pending[0] = None
                    load_chunk(layer, c + 2)
                    for _ in range(4):
                        if bg:
                            bg.pop(0)()
            if pending[0] is not None:
                down(*pending[0])
                pending[0] = None
            while bg:
                bg.pop(0)()
            if next_chunk0_loader is not None:
                next_chunk0_loader()

        setup()
        if stg >= 1:
            load_mixer_weights(0)
            load_chunk(0, 0)

        kinds = [["halo", "prompt", "prompt", "prompt", "prompt"], ["prompt", "prompt", "prompt", "prompt", "sample"]]
        ffn_blocks = [
            [[(128, 256), (256, 768), (768, 1280)], [(256, 768), (768, 1280)]],
            [[(0, 512), (512, 1024), (1024, 1280)], [(0, 512), (512, 1024), (1024, 1280)]],
        ]

        for g in range(0 if stg < 1 else (2 if stg >= 6 else 1)):
            gstg = stg if (g == 0 or stg >= 99) else {6: 2, 7: 2, 8: 4, 9: 4, 10: 1, 11: 2}[stg]
            nb0 = 4 if (g == 1 and stg == 6) else (3 if (g == 1 and stg == 11) else 5)
            nb1 = 4 if (g == 1 and stg == 8) else 5
            for b in range(5):
                for t in (2 * b, 2 * b + 1):
                    load_tile(g, t)
                norm(b * 256, b * 256 + 256, C_GMIX + 0)
            if gstg <= 1:
                for t in range(2, 10):
                    store_tile(g, t)
                continue
            for e_ in ("act", "dve", "pe"):
                fence(e_, list(SLB.rs.values()) + list(SLB.ws.values()) + list(SLBd.rs.values()) + list(SLBd.ws.values()))
            for b in range(nb0):
                mixer0(g, b, kinds[g][b])
            flush_norm()
            mix_end = [P.last.get(e_) for e_ in ("pe", "act", "dve")] + scr_dma[-1:]
            if gstg <= 2:
                for t in range(2, 10):
                    store_tile(g, t)
                continue
            load_mixer_weights(1)

            def after_l0(blk):
                norm(blk[0], blk[1], C_GMIX + 8)

            ffn(0, ffn_blocks[g][0], after_l0, None, mix_end)
            if gstg <= 3:
                for t in range(2, 10):
                    store_tile(g, t)
                continue
            for e_ in ("act", "dve", "pe"):
                fence(e_, list(SLB.rs.values()) + list(SLB.ws.values()) + list(SLBd.rs.values()) + list(SLBd.ws.values()))
            build_resident_taps()
            for b in range(nb1):
                mixer1(g, b, kinds[g][b])
            flush_norm()
            load_chunk(1, 0)
            mix_end = [P.last.get(e_) for e_ in ("pe", "act", "dve")] + scr_dma[-1:]
            if gstg <= 4:
                for t in range(2, 10):
                    store_tile(g, t)
                continue
            bg_jobs = load_mixer_weights(0, defer=True) if g == 0 else []

            def after_l1(blk, g=g):
                for t in tiles_of(blk[0], blk[1]):
                    store_tile(g, t)

            ffn(1, ffn_blocks[g][1], after_l1, (lambda: load_chunk(0, 0)) if g == 0 else None, mix_end, bg_jobs)

        P.op("sp", None, reads=[OUT], extra=list(chan_last.values()))
        P.lower(block, sems)
    return nc


def _host_consts(core):
    c = np.zeros((128, NCONST), np.float32)
    c[:, K_ID:K_ID + 128] = np.eye(128, dtype=np.float32)
    j = np.arange(128)[:, None]
    i = np.arange(128)[None, :]
    c[:, K_TRIU:K_TRIU + 128] = (j <= i).astype(np.float32)
    q = np.arange(128)[:, None]
    s = np.arange(256)[None, :]
    dist = np.abs(128 + q - s).astype(np.float32)
    qc = q // 64
    sc = s // 64
    allowed = (sc >= qc) & (sc <= qc + 2)
    nd = np.where(allowed, -dist, -1.0e6).astype(np.float32)
    c[:, K_ND:K_ND + 256] = nd
    mk = np.zeros((128, 256), np.float32)
    if core == 0:
        mk[:, 0:128] = -30000.0
    c[:, K_NDF:K_NDF + 256] = mk
    for gI, w in enumerate((2, 4, 8, 16)):
        pos = np.arange(16)
        cntv = np.minimum(pos + 1, w) if core == 0 else np.full(16, w)
        c[:, K_ICNT + gI * 16:K_ICNT + (gI + 1) * 16] = (1.0 / cntv.astype(np.float32))[None, :]
    bd = np.zeros((128, 128), np.float32)
    bd[0:64, 0:64] = 1.0
    bd[64:128, 64:128] = 1.0
    c[:, K_BD:K_BD + 128] = bd
    j64 = (np.arange(128) % 64)[:, None]
    i64 = np.arange(64)[None, :]
    c[:, K_TRIU64:K_TRIU64 + 64] = (j64 <= i64).astype(np.float32)
    return c


def kernel(x_prompt, x_sample, state_pool, state_swa_k, state_swa_v, state_conv,
           norm_mix, norm_ffn, w_in_even, q_norm, k_norm, attn_sinks, pool_w, pool_scale,
           w_out_even, w_in_odd, conv_w, conv_b, conv_ln_g, conv_ln_b, gmlp_ln_g, gmlp_ln_b,
           gmlp_w, gmlp_b, w_out_odd, ffn_gate, ffn_up, ffn_down):
    f = lambda a: np.ascontiguousarray(np.asarray(a, dtype=np.float32))
    xp = f(x_prompt)[0]
    xsm = f(x_sample).reshape(32 * 64, 1024)
    cols = np.zeros((128, NCOL), np.float32)
    nm, nf_ = f(norm_mix), f(norm_ffn)
    for l in range(2):
        cols[:, C_GMIX + 8 * l:C_GMIX + 8 * l + 8] = nm[l].reshape(8, 128).T
        cols[:, C_GFFN + 8 * l:C_GFFN + 8 * l + 8] = nf_[l].reshape(8, 128).T
    cols[:, C_PSC:C_PSC + 4] = f(pool_scale)[0].reshape(4, 128).T
    cols[:, C_GQ] = np.tile(f(q_norm)[0], 2)
    cols[:, C_GK] = np.tile(f(k_norm)[0], 2)
    cols[:, C_SINK:C_SINK + 8] = np.broadcast_to(f(attn_sinks)[0][None, :], (128, 8))
    cols[:, C_CB:C_CB + 4] = f(conv_b)[0].reshape(4, 128).T
    cols[:, C_LG:C_LG + 4] = f(conv_ln_g)[0].reshape(4, 128).T
    cols[:, C_LB:C_LB + 4] = f(conv_ln_b)[0].reshape(4, 128).T
    cw = f(conv_w)[0]
    cols[:, C_CW:C_CW + 124] = cw.reshape(31, 4, 128).transpose(2, 1, 0).reshape(128, 124)
    lng = np.broadcast_to(f(gmlp_ln_g)[0][None, :], (128, 512))
    lnb = np.broadcast_to(f(gmlp_ln_b)[0][None, :], (128, 512))
    gb = np.broadcast_to(f(gmlp_b)[0].reshape(1, 512), (128, 512))
    gwT = np.ascontiguousarray(f(gmlp_w)[0].transpose(0, 2, 1)).reshape(512, 128)
    shared = {
        "cols": cols,
        "w_in0": f(w_in_even)[0], "w_out0": f(w_out_even)[0], "pool_w": f(pool_w)[0].reshape(512, 128),
        "w_in1": f(w_in_odd)[0], "w_out1": f(w_out_odd)[0], "gwT": gwT,
        "fgate": f(ffn_gate).reshape(2048, 2816), "fup": f(ffn_up).reshape(2048, 2816), "fdown": f(ffn_down).reshape(5632, 1024),
    }
    sp, sk, sv, scv = f(state_pool)[0], f(state_swa_k)[0], f(state_swa_v)[0], f(state_conv)[0]
    in_maps = []
    for c in range(N_CORES):
        xin = np.zeros((2560, 1024), np.float32)
        if c > 0:
            xin[0:256] = xp[2048 * c - 256:2048 * c]
        xin[256:2304] = xp[2048 * c:2048 * (c + 1)]
        xin[2304:2560] = xsm[256 * c:256 * (c + 1)]
        cst = _host_consts(c)
        cst[:, K_LNG:K_LNG + 512] = lng
        cst[:, K_LNB:K_LNB + 512] = lnb
        cst[:, K_GB:K_GB + 512] = gb
        m = dict(shared)
        m.update({
            "xin": xin, "consts": cst,
            "st_pool": np.ascontiguousarray(sp[4 * c:4 * c + 4].reshape(60, 512)),
            "st_k": np.ascontiguousarray(sk[4 * c:4 * c + 4].reshape(512, 128)),
            "st_v": np.ascontiguousarray(sv[4 * c:4 * c + 4].reshape(512, 128)),
            "st_conv": np.ascontiguousarray(scv[4 * c:4 * c + 4].reshape(120, 512)),
        })
        in_maps.append(m)
    nc = build_nc(_STAGE)
    res = run_bass_kernel_spmd(nc, in_maps, core_ids=list(range(N_CORES)))
    if _STAGE < 99:
        return res.results
    R = res.results
    y_prompt = np.concatenate([R[c]["yout"][0:2048] for c in range(N_CORES)], 0)[None]
    y_sample = np.concatenate([R[c]["yout"][2048:2304] for c in range(N_CORES)], 0).reshape(32, 64, 1024)
    def _rows(a, n):
        a = a.reshape(5, 128, 512)
        out = [a[0, 128 - n:128]]
        for s_ in range(4):
            w = 320 if n == 16 else 384
            per = 80 if n == 16 else 96
            w0 = min(max(s_ * per + per - 128, 0), w - 128)
            r0 = s_ * per + 64 - w0
            out.append(a[1 + s_, r0:r0 + n])
        return np.stack(out, 0)

    po = [_rows(R[c]["pool_o"], 16) for c in range(N_CORES)]
    ko = [R[c]["k_o"].reshape(5, 128, 2, 64) for c in range(N_CORES)]
    vo = [R[c]["v_o"].reshape(5, 128, 2, 64) for c in range(N_CORES)]
    co = [_rows(R[c]["conv_o"], 32) for c in range(N_CORES)]
    gvo = [R[c]["gv_o"].reshape(4, 64, 512) for c in range(N_CORES)]
    pool_prompt = po[7][0:1, 1:16][None]
    pool_sample = np.concatenate([p[1:5, 1:16] for p in po], 0)[None]
    k_prompt = ko[7][0:1][None]
    k_sample = np.concatenate([k[1:5] for k in ko], 0)[None]
    v_prompt = vo[7][0:1][None]
    v_sample = np.concatenate([v[1:5] for v in vo], 0)[None]
    conv_prompt = co[7][0:1, 2:32][None]
    conv_sample = np.concatenate([x[1:5, 2:32] for x in co], 0)[None]
    gv_sample = np.concatenate(gvo, 0)[None]
    outs = (y_prompt, y_sample, pool_prompt, pool_sample, k_prompt, k_sample, v_prompt, v_sample,
            conv_prompt, conv_sample, gv_sample)
    return tuple(np.ascontiguousarray(o, dtype=np.float32) for o in outs)
```

```python
import numpy as np
from contextlib import ExitStack
import concourse.bass as bass
import concourse.mybir as mybir
from concourse.bass_utils import run_bass_kernel_spmd

F32 = mybir.dt.float32
BF16 = mybir.dt.bfloat16
ALU = mybir.AluOpType
AF = mybir.ActivationFunctionType
AX = mybir.AxisListType

SAME_ENGINE_SYNC = True
N_CORES = 8
SBUF_BASE = 16512
SBUF_LIMIT = 229376

C_GMIX = 0
C_GFFN = 16
C_PSC = 32
C_GQ = 36
C_GK = 37
C_SINK = 38
C_CB = 46
C_LG = 50
C_LB = 54
C_CW = 58
NCOL = 58 + 124
K_ID = 0
K_TRIU = 128
K_ND = 256
K_NDF = 512
K_ICNT = 768
K_LNG = 832
K_LNB = 1344
K_GB = 1856
K_BD = 2368
K_TRIU64 = 2496
NCONST = 2560

_STAGE = 99
_LMASK = 7
_SMASK = 255
FCH = [3, 3, 3, 3, 3, 3, 3, 1]


class Buf:
    __slots__ = ("name", "ws", "rs")

    def __init__(self, name=""):
        self.name = name
        self.ws = {}
        self.rs = {}


class Chan:
    def __init__(self, sem, wait_all=False):
        self.sem = sem
        self.n = 0
        self.wait_all = wait_all


class Op:
    __slots__ = ("eng", "fn", "deps", "signal", "cnt", "chan", "chan_cnt", "idx")


def _key(o):
    return o.eng if o.chan is None else ("c", id(o.chan))


class Prog:
    ENGS = ("pe", "act", "dve", "pool", "sp")

    def __init__(self):
        self.ops = {e: [] for e in self.ENGS}
        self.n = 0
        self.last = {}

    def op(self, eng, fn, reads=(), writes=(), chan=None, extra=()):
        o = Op()
        o.eng = eng
        o.fn = fn
        o.signal = False
        o.cnt = 0
        o.chan = chan
        o.idx = self.n
        self.n += 1
        if chan is not None:
            chan.n += 1
            o.chan_cnt = chan.n
        else:
            o.chan_cnt = 0
        deps = {}

        def add(p, raw):
            if p is o:
                return
            if p.chan is None and o.chan is None and p.eng == eng:
                if eng == "pe" or not SAME_ENGINE_SYNC or (not raw and eng != "pool"):
                    return
            k = _key(p)
            q = deps.get(k)
            if q is None or q.idx < p.idx:
                deps[k] = p

        for b in reads:
            for w in b.ws.values():
                add(w, True)
        for b in writes:
            for r in b.rs.values():
                add(r, False)
            for w in b.ws.values():
                add(w, False)
        for p in extra:
            if p is not None:
                add(p, True)
        k = _key(o)
        for b in reads:
            b.rs[k] = o
        for b in writes:
            if b.rs:
                b.ws = {}
                b.rs = {}
            b.ws[k] = o
        o.deps = list(deps.values())
        for p in o.deps:
            if p.chan is None:
                p.signal = True
        self.ops[eng].append(o)
        if chan is None and fn is not None:
            self.last[eng] = o
        return o

    def lower(self, block, sems):
        for e in self.ENGS:
            c = 0
            for o in self.ops[e]:
                if o.chan is None and o.signal:
                    c += 1
                    o.cnt = c

        def run(ename):
            def body(eng):
                waited = {}
                for o in self.ops[ename]:
                    need = {}
                    for p in o.deps:
                        if p.chan is not None:
                            s = p.chan.sem
                            v = 16 * (p.chan.n if p.chan.wait_all else p.chan_cnt)
                        else:
                            s, v = sems[p.eng], p.cnt
                        k = id(s)
                        if need.get(k, (None, 0))[1] < v:
                            need[k] = (s, v)
                    for k, (s, v) in need.items():
                        if waited.get(k, 0) >= v:
                            continue
                        waited[k] = v
                        eng.wait_ge(s, v)
                    if o.fn is None:
                        continue
                    ins = o.fn(eng)
                    if o.chan is not None:
                        ins.then_inc(o.chan.sem, 16)
                    elif o.signal:
                        ins.then_inc(sems[ename], 1)
            return body

        block.tensor(run("pe"))
        block.scalar(run("act"))
        block.vector(run("dve"))
        block.gpsimd(run("pool"))
        block.sync(run("sp"))


def bcast_mid(ap, n):
    a = ap.ap
    return bass.AP(tensor=ap.tensor, offset=ap.offset, ap=[list(a[0]), [0, n]] + [list(x) for x in a[1:]])


def build_nc(stg=99):
    nc = bass.Bass("TRN2", target_bir_lowering=False)

    def din(name, shape):
        return nc.dram_tensor(name, shape, F32, kind="ExternalInput").ap()

    def dout(name, shape):
        return nc.dram_tensor(name, shape, F32, kind="ExternalOutput").ap()

    xin = din("xin", [2560, 1024])
    cols_d = din("cols", [128, NCOL])
    consts_d = din("consts", [128, NCONST])
    w_in0_d = din("w_in0", [1024, 1280])
    w_out0_d = din("w_out0", [1024, 1024])
    pool_w_d = din("pool_w", [512, 128])
    w_in1_d = din("w_in1", [1024, 2048])
    w_out1_d = din("w_out1", [1024, 1024])
    gwT_d = din("gwT", [512, 128])
    fgate_d = din("fgate", [2048, 2816])
    fup_d = din("fup", [2048, 2816])
    fdown_d = din("fdown", [5632, 1024])
    st_pool_d = din("st_pool", [60, 512])
    st_k_d = din("st_k", [512, 128])
    st_v_d = din("st_v", [512, 128])
    st_conv_d = din("st_conv", [120, 512])
    yout = dout("yout", [2304, 1024])
    pool_o = dout("pool_o", [5 * 128, 512])
    k_o = dout("k_o", [5 * 128, 128])
    v_o = dout("v_o", [5 * 128, 128])
    conv_o = dout("conv_o", [5 * 128, 512])
    gv_o = dout("gv_o", [4 * 64, 512])

    cnt = [0]

    class Region:
        def __init__(self, start, limit):
            self.off = start
            self.limit = limit

        def alloc(self, shape, dt):
            size = 1
            for s in shape[1:]:
                size *= s
            size *= 4 if dt == F32 else 2
            size = (size + 63) // 64 * 64
            at = self.off
            self.off += size
            assert self.off <= self.limit, ("SBUF overflow", self.off, self.limit)
            cnt[0] += 1
            return nc.alloc_sbuf_tensor_at("sb%d" % cnt[0], list(shape), dt, offset=at)

    perm = Region(SBUF_BASE, SBUF_LIMIT)
    x_res = perm.alloc([128, 8, 1280], F32)
    h_all = perm.alloc([128, 8, 1280], BF16)
    wmi = perm.alloc([128, 8, 2048], BF16)
    wmo = perm.alloc([128, 8, 1024], BF16)
    wgA = perm.alloc([128, 8, 384], BF16)
    wuA = perm.alloc([128, 8, 384], BF16)
    wdA = perm.alloc([128, 3, 1024], BF16)
    xs = [perm.alloc([128, 512], F32) for _ in range(2)]
    cols = perm.alloc([128, NCOL], F32)
    consts = perm.alloc([128, NCONST], F32)
    pw = perm.alloc([128, 4, 128], BF16)
    wm = perm.alloc([128, 4, 128], BF16)
    wms = perm.alloc([128, 4, 64], BF16)
    ident_bf = perm.alloc([128, 128], BF16)
    ones_bf = perm.alloc([128, 128], BF16)
    bd_bf = perm.alloc([128, 128], BF16)
    kTstz = [perm.alloc([128, 4, 128], BF16) for _ in range(2)]
    btab = perm.alloc([128, 8, 256], BF16)
    maskf = perm.alloc([128, 256], BF16)
    negsink = perm.alloc([128, 8], F32)
    Vst = perm.alloc([128, 4, 128], BF16)
    ulead_s = perm.alloc([128, 4, 4, 15], F32)
    clead_s = perm.alloc([128, 4, 4, 30], F32)
    gq8 = perm.alloc([128, 2], F32)
    kT_c = perm.alloc([128, 128], BF16)
    V_c = perm.alloc([128, 128], BF16)
    u_c = perm.alloc([128, 4, 16], F32)
    glu_c = perm.alloc([128, 4, 32], F32)
    xsq = perm.alloc([128, 8, 512], BF16)
    rs = perm.alloc([128, 512], F32)
    s_sb = [perm.alloc([128, 512], F32) for _ in range(2)]
    hid = [perm.alloc([128, 3, 512], BF16) for _ in range(2)]
    SCR = perm.off
    rb = Region(SCR, SBUF_LIMIT)
    wgB = rb.alloc([128, 8, 384], BF16)
    wuB = rb.alloc([128, 8, 384], BF16)
    wdB = rb.alloc([128, 3, 1024], BF16)
    r0 = Region(SCR, SBUF_LIMIT)
    u_ext = r0.alloc([128, 4, 320], F32)
    ptmp = [r0.alloc([128, 320], F32) for _ in range(2)]
    pooled = r0.alloc([128, 4, 256], BF16)
    cat = r0.alloc([128, 8, 256], BF16)
    rq2 = [r0.alloc([128, 2, 256], F32) for _ in range(2)]
    qn = r0.alloc([128, 4, 256], BF16)
    kn = r0.alloc([128, 256], F32)
    kTz = [r0.alloc([128, 384], BF16) for _ in range(2)]
    V_blk = r0.alloc([128, 4, 128], BF16)
    att_pn = [r0.alloc([128, 256], BF16) for _ in range(8)]
    att_PT = [r0.alloc([128, 1024], BF16) for _ in range(2)]
    att_sm = r0.alloc([128, 32, 4], F32)
    ptiny = r0.alloc([128, 4, 16], F32)
    r1 = Region(SCR, SBUF_LIMIT)
    glu_ext = r1.alloc([128, 4, 384], F32)
    glu_bf = r1.alloc([128, 4, 384], BF16)
    dgb = [r1.alloc([128, 16, 128], BF16) for _ in range(2)]
    mv = r1.alloc([128, 3, 256], F32)
    ctmp2 = r1.alloc([128, 2, 256], F32)
    ctmp = [ctmp2[:, 0, :], ctmp2[:, 1, :]]
    zu = r1.alloc([128, 4, 256], BF16)
    zg = r1.alloc([128, 512], F32)
    sig = zg[:].rearrange("p (a n) -> p a n", a=2)
    vn_bf = r1.alloc([128, 2, 512], BF16)
    cat1 = r1.alloc([128, 8, 256], BF16)
    st6 = r1.alloc([128, 2, 8], F32)
    mvz = r1.alloc([128, 2, 4], F32)
    gtmp = ctmp
    qsq = xsq

    psf = [nc.alloc_psum_tensor("psf%d" % i, [128, 512], F32) for i in range(7)]
    psb = nc.alloc_psum_tensor("psb", [128, 1024], BF16)

    with ExitStack() as es:
        def sem(name):
            return es.enter_context(nc.semaphore(name))

        sems = {e: sem("s_" + e) for e in ("pe", "act", "dve", "pool")}
        ch_setup = Chan(sem("c_setup"), wait_all=True)
        ch_setup2 = Chan(sem("c_setup2"), wait_all=True)
        ch_wmi = Chan(sem("c_wmi"))
        ch_wmo = Chan(sem("c_wmo"))
        ch_fA = Chan(sem("c_fA"))
        ch_fB = Chan(sem("c_fB"))
        ch_fAd = Chan(sem("c_fAd"))
        ch_fBd = Chan(sem("c_fBd"))
        ch_xin = [Chan(sem("c_xin%d" % i)) for i in range(2)]
        ch_out = [Chan(sem("c_out%d" % i)) for i in range(2)]
        ch_misc = Chan(sem("c_misc"))
        block = es.enter_context(nc.Block())
        P = Prog()

        XR = [Buf("xr%d" % t) for t in range(10)]
        HA = [Buf("ha%d" % t) for t in range(10)]
        B = {}

        def bf(name):
            if name not in B:
                B[name] = Buf(name)
            return B[name]

        PSB = [Buf("ps%d" % i) for i in range(7)]
        PTB = Buf("ptb")
        XS = [Buf("xs0"), Buf("xs1")]
        OUT = Buf("out")
        ps_rr = [0]
        xs_rr = [0]

        pinned = set()

        def psum(pin=False):
            while True:
                i = ps_rr[0] % 7
                ps_rr[0] += 1
                if i not in pinned:
                    break
            if pin:
                pinned.add(i)
            return psf[i], PSB[i]

        def unpin(pb):
            pinned.discard(PSB.index(pb))

        def stage():
            i = xs_rr[0] % 2
            xs_rr[0] += 1
            return xs[i], XS[i], i

        def mm(out, lhsT, rhs, start, stop, r, w):
            return P.op("pe", lambda e: e.matmul(out, lhsT, rhs, start=start, stop=stop), reads=r, writes=w)

        def tr(out, in_, ident, r, w):
            return P.op("pe", lambda e: e.transpose(out, in_, ident), reads=r, writes=w)

        def act(out, in_, func, r, w, bias=None, scale=None, accum=None):
            kw = {}
            if bias is not None:
                kw["bias"] = bias
            if scale is not None:
                kw["scale"] = scale
            if accum is not None:
                kw["accum_out"] = accum
            return P.op("act", lambda e: e.activation(out=out, in_=in_, func=func, **kw), reads=r, writes=w)

        def dve(fn, r, w):
            return P.op("dve", fn, reads=r, writes=w)

        def tt(out, in0, in1, op, r, w):
            return dve(lambda e: e.tensor_tensor(out, in0, in1, op), r, w)

        def stt(out, in0, scalar, in1, op0, op1, r, w):
            return dve(lambda e: e.scalar_tensor_tensor(out, in0, scalar, in1, op0, op1), r, w)

        def ts(out, in0, s1, s2, op0, op1, r, w):
            if op1 is None:
                return dve(lambda e: e.tensor_scalar(out, in0, s1, s2, op0), r, w)
            return dve(lambda e: e.tensor_scalar(out, in0, s1, s2, op0, op1), r, w)

        def cpy(out, in_, r, w):
            return dve(lambda e: e.tensor_copy(out, in_), r, w)

        def recip(out, in_, r, w):
            return dve(lambda e: e.reciprocal(out, in_), r, w)

        def mset(ap, v, w):
            return dve(lambda e: e.memset(ap, v), [], w)

        chan_last = {}

        def dma(q, out, in_, chan, r, w):
            o = P.op(q, lambda e: e.dma_start(out=out, in_=in_), reads=r, writes=w, chan=chan)
            chan_last[id(chan)] = o
            return o

        def fence(eng, deps):
            P.op(eng, None, extra=deps)

        scr_dma = []

        def dma_scr(q, out, in_, chan, r, w):
            scr_dma.append(dma(q, out, in_, chan, r, w))

        CONST = bf("consts")
        COLS = bf("cols")

        def col(i):
            return cols[:, i:i + 1]

        ident_f = consts[:, K_ID:K_ID + 128]

        def setup():
            dma("sp", cols[:], cols_d[:, :], ch_setup, [], [COLS])
            dma("sp", consts[:], consts_d[:, :], ch_setup, [], [CONST])
            act(ident_bf[:], ident_f, AF.Copy, [CONST], [bf("ident_bf")])
            act(bd_bf[:], consts[:, K_BD:K_BD + 128], AF.Copy, [CONST], [bf("bd_bf")])
            mset(ones_bf[:], 1.0, [bf("ones_bf")])
            ts(gq8[:, 0:1], col(C_GQ), 0.125, None, ALU.mult, None, [COLS], [bf("gq8")])
            mset(kT_c[:], 0.0, [bf("kT_c")])
            for h in range(8):
                ts(btab[:, h, :], consts[:, K_ND:K_ND + 256], 2.0 ** (-(h + 1)), None, ALU.mult, None, [CONST], [bf("btab")])
            act(maskf[:], consts[:, K_NDF:K_NDF + 256], AF.Copy, [CONST], [bf("maskf")])
            ts(negsink[:], cols[:, C_SINK:C_SINK + 8], -1.0, None, ALU.mult, None, [COLS], [bf("negsink")])
            mset(V_c[:], 0.0, [bf("V_c")])
            mset(u_c[:], 0.0, [bf("u_c")])
            mset(glu_c[:], 0.0, [bf("glu_c")])
            if stg == -1:
                return
            dma("pool", pw[:], pool_w_d.rearrange("(g c) d -> c g d", c=128), ch_setup2, [], [bf("pw")])
            dma("pool", Vst[:], st_v_d.rearrange("(s t) c -> t s c", t=128), ch_setup2, [], [bf("Vst")])
            if stg == -2:
                return
            s0, sb0, i0 = stage()
            dma("sp", s0[:, 0:512].rearrange("p (g i) -> p g i", g=4), gwT_d.rearrange("(g j) i -> j g i", j=128), ch_xin[i0], [], [sb0])
            tt(wm[:], s0[:, 0:512].rearrange("p (g i) -> p g i", g=4), bcast_mid(consts[:, K_TRIU:K_TRIU + 128], 4), ALU.mult,
               [sb0, CONST], [bf("wm")])
            gv = gwT_d.rearrange("(g j) i -> j g i", j=128)
            s0b, sb0b, i0b = stage()
            dma("sp", s0b[0:64, 0:256].rearrange("p (g i) -> p g i", g=4), gv[0:64, :, 0:64], ch_xin[i0b], [], [sb0b])
            dma("sp", s0b[64:128, 0:256].rearrange("p (g i) -> p g i", g=4), gv[0:64, :, 0:64], ch_xin[i0b], [], [sb0b])
            tt(wms[:], s0b[:, 0:256].rearrange("p (g i) -> p g i", g=4), bcast_mid(consts[:, K_TRIU64:K_TRIU64 + 64], 4), ALU.mult,
               [sb0b, CONST], [bf("wms")])
            if stg == -3:
                return
            s1, sb1, i1 = stage()
            dma("sp", s1[:, 0:512].rearrange("p (s c) -> p s c", s=4), st_k_d.rearrange("(s t) c -> t s c", t=128), ch_xin[i1], [], [sb1])
            ps, pb = psum()
            for s in range(4):
                tr(ps[:, s * 128:(s + 1) * 128], s1[:, s * 128:(s + 1) * 128], ident_f, [sb1, CONST], [pb])
            mset(kTstz[0][:], 0.0, [bf("kTst")])
            mset(kTstz[1][:], 0.0, [bf("kTst")])
            act(kTstz[0][0:64, :, :], ps[0:64, :].rearrange("p (s t) -> p s t", s=4), AF.Copy, [pb], [bf("kTst")])
            act(kTstz[1][64:128, :, :], ps[64:128, :].rearrange("p (s t) -> p s t", s=4), AF.Copy, [pb], [bf("kTst")])
            for s in range(4):
                dma("sp", k_o[(1 + s) * 128:(1 + s) * 128 + 64, :], s1[64:128, s * 128:(s + 1) * 128], ch_out[i1], [sb1], [OUT])
            if stg == -4:
                return
            s2, sb2, i2 = stage()
            dma("sp", s2[0:60, 0:512], st_pool_d[:, :], ch_xin[i2], [], [sb2])
            ps, pb = psum()
            for g in range(4):
                tr(ps[:, g * 64:g * 64 + 60], s2[0:60, g * 128:(g + 1) * 128], consts[0:60, K_ID:K_ID + 60], [sb2, CONST], [pb])
            act(ulead_s[:], ps[:, 0:256].rearrange("p (g n) -> p g n", g=4)[:, :, 0:60].rearrange("p g (s r) -> p g s r", s=4),
                AF.Copy, [pb], [bf("ulead_s")])
            if stg == -5:
                return
            s3, sb3, i3 = stage()
            dma("sp", s3[0:120, 0:512], st_conv_d[:, :], ch_xin[i3], [], [sb3])
            ps, pb = psum()
            for g in range(4):
                tr(ps[:, g * 128:g * 128 + 120], s3[0:120, g * 128:(g + 1) * 128], consts[0:120, K_ID:K_ID + 120], [sb3, CONST], [pb])
            act(clead_s[:], ps[:].rearrange("p (g n) -> p g n", g=4)[:, :, 0:120].rearrange("p g (s r) -> p g s r", s=4),
                AF.Copy, [pb], [bf("clead_s")])
            if stg == -6:
                return
            s4, sb4, i4 = stage()
            dma("sp", s4[:, 0:512].rearrange("p (s c) -> p s c", s=4), st_v_d.rearrange("(s t) c -> t s c", t=128), ch_xin[i4], [], [sb4])
            for s in range(4):
                dma("sp", v_o[(1 + s) * 128:(1 + s) * 128 + 64, :], s4[64:128, s * 128:(s + 1) * 128], ch_out[i4], [sb4], [OUT])

        WMI = bf("wmi")
        WMO = bf("wmo")
        SLA = bf("slotA")
        SLB = bf("slotB")
        SLAd = bf("slotAd")
        SLBd = bf("slotBd")

        def load_mixer_weights(layer, defer=False):
            jobs = []

            def dq(*args):
                jobs.append(lambda: dma(*args))

            if layer == 0:
                v = w_in0_d.rearrange("(kc p) n -> p kc n", p=128)
                dq("pool", wmi[:, :, 0:512], v[:, :, 0:512], ch_wmi, [], [WMI])
                for j in range(4):
                    for slot in range(2):
                        h = slot * 4 + j
                        dq("pool", wmi[:, :, 512 + j * 128 + slot * 64:512 + j * 128 + slot * 64 + 64],
                            v[:, :, 512 + h * 64:512 + h * 64 + 64], ch_wmi, [], [WMI])
                dq("pool", wmi[:, :, 1024:1280], v[:, :, 1024:1280], ch_wmi, [], [WMI])
                vo = w_out0_d
                dq("pool", wmo[:, 0:4, :], vo[0:512, :].rearrange("(kc p) n -> p kc n", p=128), ch_wmo, [], [WMO])
                for j in range(4):
                    for slot in range(2):
                        h = slot * 4 + j
                        dq("pool", wmo[slot * 64:(slot + 1) * 64, 4 + j, :], vo[512 + h * 64:512 + h * 64 + 64, :], ch_wmo, [], [WMO])
            else:
                v = w_in1_d.rearrange("(kc p) n -> p kc n", p=128)
                dq("pool", wmi[:, :, 0:1024], v[:, :, 0:1024], ch_wmi, [], [WMI])
                dq("pool", wmi[:, :, 1024:2048], v[:, :, 1024:2048], ch_wmi, [], [WMI])
                dq("pool", wmo[:], w_out1_d.rearrange("(kc p) n -> p kc n", p=128), ch_wmo, [], [WMO])
            if defer:
                return jobs
            for j_ in jobs:
                j_()
            return []

        def chunk_slot(c):
            if c % 2 == 0:
                return wgA, wuA, wdA, SLA, ch_fA
            return wgB, wuB, wdB, SLB, ch_fB

        def chunk_slot_d(c):
            if c % 2 == 0:
                return SLAd, ch_fAd
            return SLBd, ch_fBd

        def load_chunk(layer, c, extra=()):
            wg, wu, wd, sb_, ch = chunk_slot(c)
            nf = FCH[c]
            f0 = sum(FCH[:c])
            gv = fgate_d[layer * 1024:(layer + 1) * 1024, :].rearrange("(kc p) n -> p kc n", p=128)
            uv = fup_d[layer * 1024:(layer + 1) * 1024, :].rearrange("(kc p) n -> p kc n", p=128)
            dvw = fdown_d[layer * 2816 + f0 * 128:layer * 2816 + (f0 + nf) * 128, :].rearrange("(f p) n -> p f n", p=128)
            if extra:
                fence("pool", extra)
            dma("pool", wg[:, :, 0:nf * 128], gv[:, :, f0 * 128:(f0 + nf) * 128], ch, [], [sb_])
            dma("pool", wu[:, :, 0:nf * 128], uv[:, :, f0 * 128:(f0 + nf) * 128], ch, [], [sb_])
            sbd_, chd_ = chunk_slot_d(c)
            dma("pool", wd[:, 0:nf, :], dvw, chd_, [], [sbd_])

        def load_tile(g, t):
            row = (g * 10 + t) * 128
            for half in range(2):
                s, sb_, i = stage()
                dma("sp", s[:], xin[row:row + 128, half * 512:(half + 1) * 512], ch_xin[i], [], [sb_])
                ps, pb = psum()
                for k in range(4):
                    tr(ps[:, k * 128:(k + 1) * 128], s[:, k * 128:(k + 1) * 128], ident_f, [sb_, CONST], [pb])
                act(x_res[:, half * 4:half * 4 + 4, t * 128:(t + 1) * 128], ps[:].rearrange("p (k n) -> p k n", k=4), AF.Copy,
                    [pb], [XR[t]])

        def store_tile(g, t):
            row = (g * 10 + t - 2) * 128
            for half in range(2):
                s, sb_, i = stage()
                ps, pb = psum()
                for k in range(4):
                    kc = half * 4 + k
                    tr(ps[:, k * 128:(k + 1) * 128], x_res[:, kc, t * 128:(t + 1) * 128], ident_f, [XR[t], CONST], [pb])
                act(s[:], ps[:], AF.Copy, [pb], [sb_])
                dma("sp", yout[row:row + 128, half * 512:(half + 1) * 512], s[:], ch_out[i], [sb_], [OUT])

        def rows_out(wins, r0, nrows, dst, r):
            s, sb_, i = stage()
            ps, pb = psum()
            for g_, a in enumerate(wins):
                tr(ps[:, g_ * 128:(g_ + 1) * 128], a, ident_f, r + [CONST], [pb])
            act(s[:, 0:512], ps[:, :], AF.Copy, [pb], [sb_])
            dma("sp", dst, s[:, 0:512], ch_out[i], [sb_], [OUT])

        XSQ = bf("xsq")
        RS = bf("rs")

        def tiles_of(c0, c1):
            return list(range(c0 // 128, (c1 + 127) // 128))

        def norm(c0, c1, gcol):
            W = c1 - c0
            tl = tiles_of(c0, c1)
            xr = [XR[t] for t in tl]
            ha = [HA[t] for t in tl]
            for half in range(2):
                act(xsq[:, half * 4:half * 4 + 4, 0:W], x_res[:, half * 4:half * 4 + 4, c0:c1], AF.Square, xr, [XSQ])
            ps, pb = psum()
            for kc in range(8):
                mm(ps[:, 0:W], ones_bf[:], xsq[:, kc, 0:W], kc == 0, kc == 7, [XSQ, bf("ones_bf")], [pb])
            act(rs[:, 0:W], ps[:, 0:W], AF.Ln, [pb], [RS], bias=1e-6, scale=1.0 / 1024)
            act(rs[:, 0:W], rs[:, 0:W], AF.Exp, [RS], [RS], scale=-0.5)
            for kc in range(8):
                stt(h_all[:, kc, c0:c1], x_res[:, kc, c0:c1], col(gcol + kc), rs[:, 0:W], ALU.mult, ALU.mult,
                    xr + [RS, COLS], ha)

        def wout_residual(catt, CATB, c0):
            tl = tiles_of(c0, c0 + 256)
            xr = [XR[t] for t in tl]
            for pair in range(4):
                ps, pb = psum()
                for i in range(2):
                    m = pair * 2 + i
                    for kc in range(8):
                        mm(ps[:, i * 256:(i + 1) * 256], wmo[:, kc, m * 128:(m + 1) * 128], catt[:, kc, :], kc == 0, kc == 7,
                           [WMO, CATB], [pb])
                xv = x_res[:, pair * 2:pair * 2 + 2, c0:c0 + 256]
                tt(xv, ps[:].rearrange("p (a n) -> p a n", a=2), xv, ALU.add, [pb] + xr, xr)

        deferred_norm = []

        def flush_norm():
            while deferred_norm:
                deferred_norm.pop(0)()

        def mixer0(g, b, kind):
            c0 = b * 256
            tl = [2 * b, 2 * b + 1]
            ha = [HA[t] for t in tl]
            first = (g == 0 and b == 1)
            lastp = (g == 1 and b == 3)
            sample = kind == "sample"
            UE, PL, CAT, QN, KN, KTB, VB = bf("u_ext"), bf("pooled"), bf("cat"), bf("qn"), bf("kn"), bf("kTz"), bf("V_blk")
            hs = h_all[:, :, c0:c0 + 256]
            if sample:
                mset(u_ext[:], 0.0, [UE])
                mset(kTz[0][64:128, :], 0.0, [KTB])
                mset(kTz[1][0:64, :], 0.0, [KTB])
                for gI in range(4):
                    act(u_ext[:, gI, :].rearrange("p (s n) -> p s n", s=4)[:, :, 1:16], ulead_s[:, gI, :, :], AF.Copy, [bf("ulead_s")], [UE])
            else:
                cpy(u_ext[:, :, 0:16], u_c[:], [bf("u_c")], [UE])
                mset(kTz[0][64:128, :], 0.0, [KTB])
                mset(kTz[1][0:64, :], 0.0, [KTB])
                act(kTz[0][0:64, 0:128], kT_c[0:64, :], AF.Copy, [bf("kT_c")], [KTB])
                act(kTz[1][64:128, 0:128], kT_c[64:128, :], AF.Copy, [bf("kT_c")], [KTB])
                act(V_blk[:, 0, :], V_c[:], AF.Copy, [bf("V_c")], [VB])
            for pair in range(2):
                ps, pb = psum()
                for i in range(2):
                    m = pair * 2 + i
                    for kc in range(8):
                        mm(ps[:, i * 256:(i + 1) * 256], wmi[:, kc, m * 128:(m + 1) * 128], hs[:, kc, :], kc == 0, kc == 7,
                           [WMI] + ha, [pb])
                if not sample:
                    act(u_ext[:, pair * 2:pair * 2 + 2, 16:272], ps[:].rearrange("p (a n) -> p a n", a=2), AF.Copy, [pb], [UE])
                else:
                    for i in range(2):
                        m = pair * 2 + i
                        act(u_ext[:, m, :].rearrange("p (s n) -> p s n", s=4)[:, :, 16:80],
                            ps[:, i * 256:(i + 1) * 256].rearrange("p (s n) -> p s n", s=4), AF.Copy, [pb], [UE])
            QSQ = XSQ
            qps = []
            for pair in range(2):
                ps, pb = psum(pin=True)
                for i in range(2):
                    m = pair * 2 + i
                    for kc in range(8):
                        mm(ps[:, i * 256:(i + 1) * 256], wmi[:, kc, 512 + m * 128:512 + (m + 1) * 128], hs[:, kc, :], kc == 0, kc == 7,
                           [WMI] + ha, [pb])
                act(qsq[:, pair * 2:pair * 2 + 2, 0:256], ps[:].rearrange("p (a n) -> p a n", a=2), AF.Square, [pb], [QSQ])
                qps.append((ps, pb))
            psk, pbk = psum(pin=True)
            for kc in range(8):
                mm(psk[:, 0:256], wmi[:, kc, 1024:1152], hs[:, kc, :], kc == 0, kc == 7, [WMI] + ha, [pbk])
            act(qsq[:, 4, 0:256], psk[:, 0:256], AF.Square, [pbk], [QSQ])
            psv, pbv = psum()
            if not sample:
                for t_ in range(2):
                    for kc in range(8):
                        mm(psv[:, t_ * 128:(t_ + 1) * 128], h_all[:, kc, c0 + t_ * 128:c0 + (t_ + 1) * 128], wmi[:, kc, 1152:1280],
                           kc == 0, kc == 7, [WMI] + ha, [pbv])
                act(V_blk[:, 1:3, :], psv[:, 0:256].rearrange("p (a n) -> p a n", a=2), AF.Copy, [pbv], [VB])
                if lastp and (_LMASK & 1):
                    sv_, sbv_, iv_ = stage()
                    act(sv_[:, 0:128], psv[:, 128:256], AF.Copy, [pbv], [sbv_])
                    dma("sp", v_o[0:128, :], sv_[:, 0:128], ch_out[iv_], [sbv_], [OUT])
            else:
                for s in range(4):
                    for kc in range(8):
                        mm(psv[0:64, s * 128:(s + 1) * 128], h_all[:, kc, c0 + s * 64:c0 + (s + 1) * 64], wmi[:, kc, 1152:1280],
                           kc == 0, kc == 7, [WMI] + ha, [pbv])
                act(V_blk[0:64, :, :], psv[0:64, :].rearrange("p (a n) -> p a n", a=4), AF.Copy, [pbv], [VB])
                sv_, sbv_, iv_ = stage()
                act(sv_[0:64, 0:512], psv[0:64, :], AF.Copy, [pbv], [sbv_])
                for s in range(4 if (_SMASK & 1) else 0):
                    dma("sp", v_o[(1 + s) * 128 + 64:(2 + s) * 128, :], sv_[0:64, s * 128:(s + 1) * 128], ch_out[iv_], [sbv_], [OUT])
            for pair in range(3):
                RQ = bf("rq%d" % (pair % 2))
                rq = rq2[pair % 2]
                ps, pb = psum()
                n_ = 2 if pair < 2 else 1
                for i in range(n_):
                    m = pair * 2 + i
                    mm(ps[:, i * 256:(i + 1) * 256], bd_bf[:], qsq[:, m, 0:256], True, True, [bf("bd_bf"), QSQ], [pb])
                act(rq[:, 0:n_, :], ps[:, 0:n_ * 256].rearrange("p (a n) -> p a n", a=n_), AF.Ln, [pb], [RQ], bias=1e-6, scale=1.0 / 64)
                act(rq[:, 0:n_, :], rq[:, 0:n_, :], AF.Exp, [RQ], [RQ], scale=-0.5)
                if pair < 2:
                    qp, qb = qps[pair]
                    for i in range(2):
                        m = pair * 2 + i
                        stt(qn[:, m, :], qp[:, i * 256:(i + 1) * 256], gq8[:, 0:1], rq[:, i, :], ALU.mult, ALU.mult,
                            [qb, RQ, bf("gq8")], [QN])
                    unpin(qb)
                else:
                    stt(kn[:], psk[:, 0:256], col(C_GK), rq[:, 0, :], ALU.mult, ALU.mult, [pbk, RQ, COLS], [KN])
                    act(kTz[0][0:64, 128:384], kn[0:64, :], AF.Copy, [KN], [KTB])
                    act(kTz[1][64:128, 128:384], kn[64:128, :], AF.Copy, [KN], [KTB])
                    unpin(pbk)
            if lastp and (_LMASK & 4):
                s_, sb_, i_ = stage()
                ps, pb = psum()
                tr(ps[:, 0:128], kn[:, 128:256], ident_f, [KN, CONST], [pb])
                act(s_[:, 0:128], ps[:, 0:128], AF.Copy, [pb], [sb_])
                dma("sp", k_o[0:128, :], s_[:, 0:128], ch_out[i_], [sb_], [OUT])
            if sample and (_SMASK & 4):
                s_, sb_, i_ = stage()
                ps, pb = psum()
                for t_ in range(2):
                    tr(ps[:, t_ * 128:(t_ + 1) * 128], kn[:, t_ * 128:(t_ + 1) * 128], ident_f, [KN, CONST], [pb])
                act(s_[:, 0:256], ps[:, 0:256], AF.Copy, [pb], [sb_])
                for s in range(4):
                    dma("sp", k_o[(1 + s) * 128 + 64:(2 + s) * 128, :],
                        s_[(s % 2) * 64:(s % 2) * 64 + 64, (s // 2) * 128:(s // 2) * 128 + 128], ch_out[i_], [sb_], [OUT])
            SM = bf("att_sm")
            units = []
            if not sample:
                for t_i in range(2):
                    segs = [(t_i * 128, None, V_blk[:, t_i, :], 128, KTB, VB),
                            ((t_i + 1) * 128, None, V_blk[:, t_i + 1, :], 128, KTB, VB)]
                    units.append((128, t_i * 128, segs, first and t_i == 0, t_i))
            elif _SMASK & 8:
                for s in range(4):
                    segs = [(None, s, Vst[:, s, :], 128, bf("kTst"), bf("Vst")),
                            (128 + s * 64, None, V_blk[0:64, s, :], 64, KTB, VB)]
                    units.append((64, s * 64, segs, False, s // 2))
            waves = [(u, slot) for u in units for slot in range(2)]
            nW = len(waves)
            SMW = [Buf("smw%d" % wi) for wi in range(nW)]
            mset(att_sm[:], 0.0, SMW + [SM])
            W = 320 if sample else 272
            PT0, PT1 = bf("ptmp0"), bf("ptmp1")
            A_, B_ = ptmp[0], ptmp[1]

            def dview(ap2):
                if sample:
                    return ap2.rearrange("p (s n) -> p s n", s=4)[:, :, 16:80]
                return ap2[:, 16:272]

            def oview(ap2):
                if sample:
                    return ap2.rearrange("p (s n) -> p s n", s=4)
                return ap2

            def ptt(out, in0, in1, r, w):
                return P.op("pool", lambda e: e.tensor_tensor(out, in0, in1, ALU.add), reads=r, writes=w)

            def pool_group(gI, wnd):
                u_ = u_ext[:, gI, :]
                ptt(A_[:, 1:W], u_[:, 1:W], u_[:, 0:W - 1], [UE], [PT0])
                cur, curB = A_, PT0
                if wnd >= 4:
                    ptt(B_[:, 3:W], A_[:, 3:W], A_[:, 1:W - 2], [PT0], [PT1])
                    cur, curB = B_, PT1
                if wnd >= 8:
                    ptt(A_[:, 7:W], B_[:, 7:W], B_[:, 3:W - 4], [PT1], [PT0])
                    cur, curB = A_, PT0
                if wnd >= 16:
                    ptt(B_[:, 15:W], A_[:, 15:W], A_[:, 7:W - 8], [PT0], [PT1])
                    cur, curB = B_, PT1
                stt(oview(pooled[:, gI, :]), dview(cur[:, 0:W]), 1.0 / wnd, dview(u_), ALU.mult, ALU.subtract, [curB, UE], [PL])
                if first:
                    tt(ptiny[:, gI, :], cur[:, 16:32], consts[:, K_ICNT + gI * 16:K_ICNT + (gI + 1) * 16], ALU.mult,
                       [curB, CONST], [bf("ptiny")])
                    tt(pooled[:, gI, 0:16], ptiny[:, gI, :], u_[:, 16:32], ALU.subtract, [bf("ptiny"), UE, PL], [PL])

            pool_pending = []
            for gI, wnd in enumerate((2, 4, 8, 16)):
                if gI < 2:
                    pool_group(gI, wnd)
                else:
                    pool_pending.append(lambda gI=gI, wnd=wnd: pool_group(gI, wnd))
            wstate = {}
            po_tiles = {}
            last_wave_of_tile = {}
            for wi, (u, slot) in enumerate(waves):
                last_wave_of_tile[u[4]] = wi

            def kseg(slot, sg):
                c_, st_, _, n, _, _ = sg
                if st_ is not None:
                    return kTstz[slot][:, st_, 0:n]
                return kTz[slot][:, c_:c_ + n]

            def emitS(i):
                (Mq, a, segs, masked, tk), slot = waves[i]
                nk = sum(sg[3] for sg in segs)
                banks = [psum(), psum()]
                wstate[i] = banks
                for j in range(4):
                    h = slot * 4 + j
                    pS, pSb = banks[j // 2]
                    off = (j % 2) * 256
                    mm(pS[0:Mq, off:off + nk], ident_bf[:, 0:Mq], btab[:, h, 0:nk], True, False, [bf("ident_bf"), bf("btab")], [pSb])
                    if masked:
                        mm(pS[0:Mq, off:off + nk], ident_bf[:, 0:Mq], maskf[:, 0:nk], False, False, [bf("ident_bf"), bf("maskf")], [pSb])
                    ko = 0
                    for si, sg in enumerate(segs):
                        n = sg[3]
                        mm(pS[0:Mq, off + ko:off + ko + n], qn[:, j, a:a + Mq], kseg(slot, sg), False, si == len(segs) - 1,
                           [QN, sg[4]], [pSb])
                        ko += n

            def emitSoft(i):
                (Mq, a, segs, masked, tk), slot = waves[i]
                nk = sum(sg[3] for sg in segs)
                set_ = i % 2
                banks = wstate[i]
                PNB = bf("att_pn%d" % set_)
                smw = att_sm[0:Mq, i * 4:(i + 1) * 4, :]
                for j in range(4):
                    h = slot * 4 + j
                    pS, pSb = banks[j // 2]
                    off = (j % 2) * 256
                    hb = set_ * 4 + j
                    act(att_pn[hb][0:Mq, 0:nk], pS[0:Mq, off:off + nk], AF.Exp, [pSb, bf("negsink"), SMW[i]], [PNB, SMW[i]],
                        bias=negsink[0:Mq, h:h + 1], scale=1.0, accum=att_sm[0:Mq, i * 4 + j, 0:1])
                act(smw[:, :, 1:2], smw[:, :, 0:1], AF.Ln, [SMW[i]], [SMW[i]], bias=1.0, scale=1.0)
                act(smw[:, :, 2:3], smw[:, :, 1:2], AF.Exp, [SMW[i]], [SMW[i]], scale=-1.0)
                for j in range(4):
                    hb = set_ * 4 + j
                    ts(att_pn[hb][0:Mq, 0:nk], att_pn[hb][0:Mq, 0:nk], att_sm[0:Mq, i * 4 + j, 2:3], None, ALU.mult, None, [PNB, SMW[i]], [PNB])

            def emitT(i):
                (Mq, a, segs, masked, tk), slot = waves[i]
                set_ = i % 2
                PNB = bf("att_pn%d" % set_)
                PTSB = bf("att_PT%d" % set_)
                for j in range(4):
                    hb = set_ * 4 + j
                    ko = 0
                    for si, sg in enumerate(segs):
                        n = sg[3]
                        tr(psb[0:n, j * 256 + si * 128:j * 256 + si * 128 + Mq], att_pn[hb][0:Mq, ko:ko + n],
                           ident_bf[0:Mq, 0:Mq], [PNB, bf("ident_bf")], [PTB])
                        ko += n
                if Mq == 128 and all(sg[3] == 128 for sg in segs):
                    cpy(att_PT[set_][:, :], psb[:, :], [PTB], [PTSB])
                else:
                    for j in range(4):
                        for si, sg in enumerate(segs):
                            n = sg[3]
                            cpy(att_PT[set_][0:n, j * 256 + si * 128:j * 256 + si * 128 + Mq],
                                psb[0:n, j * 256 + si * 128:j * 256 + si * 128 + Mq], [PTB], [PTSB])

            def emitPV(i):
                (Mq, a, segs, masked, tk), slot = waves[i]
                set_ = i % 2
                PTSB = bf("att_PT%d" % set_)
                if tk not in po_tiles:
                    po_tiles[tk] = psum(pin=True)
                ps_o, pb_o = po_tiles[tk]
                for j in range(4):
                    for si, sg in enumerate(segs):
                        n = sg[3]
                        mm(ps_o[slot * 64:(slot + 1) * 64, j * 128 + (a % 128):j * 128 + (a % 128) + Mq],
                           sg[2][0:n, slot * 64:(slot + 1) * 64], att_PT[set_][0:n, j * 256 + si * 128:j * 256 + si * 128 + Mq],
                           si == 0, si == len(segs) - 1, [sg[5], PTSB], [pb_o])
                if last_wave_of_tile[tk] == i:
                    act(cat[:, 4:8, tk * 128:(tk + 1) * 128], ps_o[:].rearrange("p (j n) -> p j n", j=4), AF.Copy, [pb_o], [CAT])
                    unpin(pb_o)

            for i in range(nW + 2):
                if i < nW:
                    emitS(i)
                if 0 <= i - 1 < nW:
                    emitT(i - 1)
                if 0 <= i - 2 < nW:
                    emitPV(i - 2)
                if i < nW:
                    emitSoft(i)
                if pool_pending and i in (0, 1):
                    pool_pending.pop(0)()
            while pool_pending:
                pool_pending.pop(0)()
            if not sample:
                cpy(u_c[:], u_ext[:, :, 256:272], [UE], [bf("u_c")])
            flush_norm()
            for pair in range(2):
                ps, pb = psum()
                for i in range(2):
                    gI = pair * 2 + i
                    mm(ps[:, i * 256:(i + 1) * 256], pw[:, gI, :], pooled[:, gI, :], True, True, [bf("pw"), PL], [pb])
                for i in range(2):
                    gI = pair * 2 + i
                    act(cat[:, gI, :], ps[:, i * 256:(i + 1) * 256], AF.Copy, [pb, COLS], [CAT], scale=col(C_PSC + gI))
            if lastp and (_LMASK & 2):
                rows_out([u_ext[:, gI, 144:272] for gI in range(4)], 112, 16, pool_o[0:128, :], [UE])
            if sample and (_SMASK & 2):
                for s in range(4):
                    w0 = min(max(s * 80 + 80 - 128, 0), 320 - 128)
                    rows_out([u_ext[:, gI, w0:w0 + 128] for gI in range(4)], s * 80 + 64 - w0, 16, pool_o[(1 + s) * 128:(2 + s) * 128, :], [UE])
            wout_residual(cat, CAT, c0)
            if not sample:
                act(kT_c[0:64, :], kTz[0][0:64, 256:384], AF.Copy, [KTB], [bf("kT_c")])
                act(kT_c[64:128, :], kTz[1][64:128, 256:384], AF.Copy, [KTB], [bf("kT_c")])
                act(V_c[:], V_blk[:, 2, :], AF.Copy, [VB], [bf("V_c")])
            deferred_norm.append(lambda c0=c0: norm(c0, c0 + 256, C_GFFN + 0))

        PTAPS = []
        NPT = [0]

        def tap_regions():
            regs = [(wgA[:].rearrange("p a (b c) -> p (a b) c", c=128), [SLA]),
                    (wuA[:].rearrange("p a (b c) -> p (a b) c", c=128), [SLA]),
                    (wdA[:].rearrange("p a (b c) -> p (a b) c", c=128), [SLAd]),
                    (hid[0][:].rearrange("p a (b c) -> p (a b) c", c=128), [bf("hid0")]),
                    (hid[1][:].rearrange("p a (b c) -> p (a b) c", c=128), [bf("hid1")])]
            return regs

        def build_resident_taps():
            del PTAPS[:]
            t0 = 0
            for view, bufs in tap_regions():
                n = min(view.shape[1], 124 - t0)
                if n <= 0:
                    break
                wsrc = cols[:, C_CW + t0:C_CW + t0 + n]
                wbc = bass.AP(tensor=wsrc.tensor, offset=wsrc.offset, ap=[list(x) for x in wsrc.ap] + [[0, 128]])
                tt(view[:, 0:n, :], bcast_mid(ident_bf[:], n), wbc, ALU.mult, [bf("ident_bf"), COLS], bufs)
                for j in range(n):
                    PTAPS.append((view[:, j, :], bufs))
                t0 += n
            NPT[0] = t0

        def mixer1(g, b, kind):
            c0 = b * 256
            tl = [2 * b, 2 * b + 1]
            ha = [HA[t] for t in tl]
            lastp = (g == 1 and b == 3)
            sample = kind == "sample"
            halo = kind == "halo"
            GE, GBF, SIG, CAT1 = bf("glu_ext"), bf("glu_bf"), bf("zg"), bf("cat1")
            hs = h_all[:, :, c0:c0 + 256]
            W = 384 if sample else 288
            if sample:
                mset(glu_ext[:], 0.0, [GE])
                for ci in range(4):
                    act(glu_ext[:, ci, :].rearrange("p (s n) -> p s n", s=4)[:, :, 2:32], clead_s[:, ci, :, :], AF.Copy, [bf("clead_s")], [GE])
            else:
                cpy(glu_ext[:, :, 0:32], glu_c[:], [bf("glu_c")], [GE])
            for pair in range(2):
                psa, pba = psum()
                psg, pbg = psum()
                for (ps_, pb_, base) in ((psa, pba, 0), (psg, pbg, 512)):
                    for i in range(2):
                        m = pair * 2 + i
                        for kc in range(8):
                            mm(ps_[:, i * 256:(i + 1) * 256], wmi[:, kc, base + m * 128:base + (m + 1) * 128], hs[:, kc, :], kc == 0, kc == 7,
                               [WMI] + ha, [pb_])
                act(sig, psg[:].rearrange("p (a n) -> p a n", a=2), AF.Sigmoid, [pbg], [SIG])
                if not sample:
                    tt(glu_ext[:, pair * 2:pair * 2 + 2, 32:288], psa[:].rearrange("p (a n) -> p a n", a=2), sig, ALU.mult,
                       [pba, SIG], [GE])
                else:
                    for i in range(2):
                        m = pair * 2 + i
                        tt(glu_ext[:, m, :].rearrange("p (s n) -> p s n", s=4)[:, :, 32:96],
                           psa[:, i * 256:(i + 1) * 256].rearrange("p (s n) -> p s n", s=4),
                           sig[:, i, :].rearrange("p (s n) -> p s n", s=4), ALU.mult, [pba, SIG], [GE])
            flush_norm()
            if not sample:
                cpy(glu_c[:], glu_ext[:, :, 256:288], [GE], [bf("glu_c")])
            if lastp:
                rows_out([glu_ext[:, ci, 160:288] for ci in range(4)], 96, 32, conv_o[0:128, :], [GE])
            if sample and (_SMASK & 32):
                for s in range(4):
                    w0 = min(max(s * 96 + 96 - 128, 0), 384 - 128)
                    rows_out([glu_ext[:, ci, w0:w0 + 128] for ci in range(4)], s * 96 + 64 - w0, 32, conv_o[(1 + s) * 128:(2 + s) * 128, :], [GE])
            if halo:
                return
            act(glu_bf[:, :, 0:W], glu_ext[:, :, 0:W], AF.Copy, [GE], [GBF])
            CSQ = XSQ
            csq = xsq[:, 0:4, 0:256]
            c_bf = xsq[:, 4:8, 0:256]
            dgc = [0]
            conv_ps = []
            built = {}

            def tap_operand(t):
                if t < NPT[0]:
                    return PTAPS[t]
                t0 = NPT[0] + ((t - NPT[0]) // 16) * 16
                if t0 not in built:
                    nt = min(16, 124 - t0)
                    r_ = dgc[0] % 2
                    dgc[0] += 1
                    DGB = bf("dgb%d" % r_)
                    wsrc = cols[:, C_CW + t0:C_CW + t0 + nt]
                    wbc = bass.AP(tensor=wsrc.tensor, offset=wsrc.offset, ap=[list(x) for x in wsrc.ap] + [[0, 128]])
                    tt(dgb[r_][:, 0:nt, :], bcast_mid(ident_bf[:], nt), wbc, ALU.mult, [bf("ident_bf"), COLS], [DGB])
                    built[t0] = (r_, DGB)
                r_, DGB = built[t0]
                return dgb[r_][:, t - t0, :], [DGB]

            for t_ in range(NPT[0], 124, 16):
                tap_operand(t_)
            ZU = bf("zu")
            for pair in range(2):
                ps, pb = psum()
                for i in range(2):
                    m = pair * 2 + i
                    for kc in range(8):
                        mm(ps[:, i * 256:(i + 1) * 256], wmi[:, kc, 1024 + m * 128:1024 + (m + 1) * 128], hs[:, kc, :], kc == 0, kc == 7,
                           [WMI] + ha, [pb])
                act(zu[:, pair * 2:pair * 2 + 2, :], ps[:].rearrange("p (a n) -> p a n", a=2), AF.Gelu, [pb], [ZU])
            VNB, MVZ = bf("vn_bf"), bf("mvz")
            zgt = [zg[:], ctmp2[:].rearrange("p a n -> p (a n)")]
            ZGt = [[bf("zg")], [bf("ctmp0"), bf("ctmp1")]]
            for t_i in range(2):
                ps, pb = psum()
                for kc in range(8):
                    mm(ps[:], h_all[:, kc, c0 + t_i * 128:c0 + (t_i + 1) * 128], wmi[:, kc, 1536:2048], kc == 0, kc == 7, [WMI] + ha, [pb])
                act(zgt[t_i], ps[:], AF.Gelu, [pb], ZGt[t_i])
            for t_i in range(2):
                dve(lambda e, t_i=t_i: e.bn_stats(st6[:, t_i, 0:6], zgt[t_i]), ZGt[t_i], [bf("st6")])
                dve(lambda e, t_i=t_i: e.bn_aggr(mvz[:, t_i, 0:2], st6[:, t_i, 0:6]), [bf("st6")], [MVZ])
            act(mvz[:, :, 2:3], mvz[:, :, 1:2], AF.Ln, [MVZ], [MVZ], bias=1e-5, scale=1.0)
            act(mvz[:, :, 3:4], mvz[:, :, 2:3], AF.Exp, [MVZ], [MVZ], scale=-0.5)
            for t_i in range(2):
                z_ = zgt[t_i]
                ts(z_, z_, mvz[:, t_i, 0:1], mvz[:, t_i, 3:4], ALU.subtract, ALU.mult, ZGt[t_i] + [MVZ], ZGt[t_i])
                tt(z_, z_, consts[:, K_LNG:K_LNG + 512], ALU.mult, ZGt[t_i] + [CONST], ZGt[t_i])
                tt(z_, z_, consts[:, K_LNB:K_LNB + 512], ALU.add, ZGt[t_i] + [CONST], ZGt[t_i])
                act(vn_bf[:, t_i, :], z_, AF.Copy, ZGt[t_i], [VNB])
                if sample:
                    for half in range(2):
                        s = t_i * 2 + half
                        if _SMASK & 16:
                            dma_scr("sp", gv_o[s * 64:(s + 1) * 64, :], z_[half * 64:(half + 1) * 64, :], ch_misc, ZGt[t_i], [OUT])
            for pair in range(2):
                ps, pb = psum(pin=True)
                for i in range(2):
                    ci = pair * 2 + i
                    for k in range(31):
                        lw, lwb = tap_operand(ci * 31 + k)
                        if not sample:
                            mm(ps[:, i * 256:(i + 1) * 256], lw, glu_bf[:, ci, 2 + k:2 + k + 256], k == 0, k == 30, lwb + [GBF], [pb])
                        else:
                            for s in range(4):
                                P.op("pe", lambda e, o_=ps[:, i * 256 + s * 64:i * 256 + (s + 1) * 64], l_=lw,
                                     r2_=glu_bf[:, ci, s * 96 + 2 + k:s * 96 + 2 + k + 64], st_=(k == 0 and s == 0), sp_=(k == 30 and s == 3):
                                     e.matmul(o_, l_, r2_, start=st_, stop=sp_, skip_group_check=True), reads=lwb + [GBF], writes=[pb])
                for i in range(2):
                    ci = pair * 2 + i
                    act(c_bf[:, ci, :], ps[:, i * 256:(i + 1) * 256], AF.Identity, [pb, COLS], [CSQ], bias=col(C_CB + ci))
                    act(csq[:, ci, :], ps[:, i * 256:(i + 1) * 256], AF.Square, [pb, COLS], [CSQ], bias=col(C_CB + ci))
                conv_ps.append((ps, pb))
            if not sample:
                for pair in range(2):
                    ps, pb = psum()
                    for i in range(2):
                        gg = pair * 2 + i
                        for t_i in range(2):
                            mm(ps[:, i * 256 + t_i * 128:i * 256 + (t_i + 1) * 128], vn_bf[:, t_i, gg * 128:(gg + 1) * 128], wm[:, gg, :],
                               True, True, [VNB, bf("wm")], [pb])
                    for i in range(2):
                        gg = pair * 2 + i
                        GT = bf("ctmp%d" % i)
                        tt(gtmp[i][:].rearrange("p (a n) -> p a n", a=2), ps[:, i * 256:(i + 1) * 256].rearrange("p (a n) -> p a n", a=2),
                           bcast_mid(consts[:, K_GB + gg * 128:K_GB + (gg + 1) * 128], 2), ALU.add, [pb, CONST], [GT])
                        tt(cat1[:, 4 + gg, :], gtmp[i][:], zu[:, gg, :], ALU.mult, [GT, ZU], [CAT1])
            else:
                psH = [psum(pin=True), psum(pin=True)]
                for half in range(2):
                    ps, pb = psH[half]
                    for gg in range(4):
                        for t_i in range(2):
                            o_ = gg * 128 + t_i * 64
                            mm(ps[:, o_:o_ + 64], vn_bf[half * 64:(half + 1) * 64, t_i, gg * 128:(gg + 1) * 128],
                               wms[half * 64:(half + 1) * 64, gg, :], True, True, [VNB, bf("wms")], [pb])
                for gg in range(4):
                    i = gg % 2
                    GT = bf("ctmp%d" % i)
                    for half in range(2):
                        ps, pb = psH[half]
                        tt(gtmp[i][:].rearrange("p (t h n) -> p t h n", t=2, h=2)[:, :, half, :],
                           ps[:, gg * 128:(gg + 1) * 128].rearrange("p (t n) -> p t n", t=2),
                           bcast_mid(consts[:, K_GB + gg * 128:K_GB + gg * 128 + 64], 2), ALU.add, [pb, CONST], [GT])
                    tt(cat1[:, 4 + gg, :], gtmp[i][:], zu[:, gg, :], ALU.mult, [GT, ZU], [CAT1])
                unpin(psH[0][1])
                unpin(psH[1][1])
            MV = bf("mv")
            pst, pbt = psum()
            for ci in range(4):
                mm(pst[:, 0:256], ones_bf[:], c_bf[:, ci, :], ci == 0, ci == 3, [CSQ, bf("ones_bf")], [pbt])
            for ci in range(4):
                mm(pst[:, 256:512], ones_bf[:], csq[:, ci, :], ci == 0, ci == 3, [CSQ, bf("ones_bf")], [pbt])
            act(mv[:, 0, :], pst[:, 0:256], AF.Copy, [pbt], [MV], scale=1.0 / 512)
            tt(mv[:, 2, :], mv[:, 0, :], mv[:, 0, :], ALU.mult, [MV], [MV])
            stt(mv[:, 1, :], pst[:, 256:512], 1.0 / 512, mv[:, 2, :], ALU.mult, ALU.subtract, [pbt, MV], [MV])
            act(mv[:, 1, :], mv[:, 1, :], AF.Ln, [MV], [MV], bias=1e-5, scale=1.0)
            act(mv[:, 1, :], mv[:, 1, :], AF.Exp, [MV], [MV], scale=-0.5)
            for ci in range(4):
                ps, pb = conv_ps[ci // 2]
                i = ci % 2
                CT = bf("ctmp%d" % (ci % 2))
                stt(ctmp[ci % 2][:], ps[:, i * 256:(i + 1) * 256], col(C_CB + ci), mv[:, 0, :], ALU.add, ALU.subtract, [pb, MV, COLS], [CT])
                tt(ctmp[ci % 2][:], ctmp[ci % 2][:], mv[:, 1, :], ALU.mult, [CT, MV], [CT])
                act(cat1[:, ci, :], ctmp[ci % 2][:], AF.Silu, [CT, COLS], [CAT1], bias=col(C_LB + ci), scale=col(C_LG + ci))
                if i == 1:
                    unpin(pb)
            wout_residual(cat1, CAT1, c0)
            deferred_norm.append(lambda c0=c0: norm(c0, c0 + 256, C_GFFN + 8))

        def ffn(layer, blocks, after_block, next_chunk0_loader, mixer_end_ops, bg_loads=()):
            bg = list(bg_loads)
            pending = [None]
            HID = [bf("hid0"), bf("hid1")]
            SSB = [bf("ssb0"), bf("ssb1")]
            it = [0]

            def down(c, blk, hb):
                wg, wu, wd, sb_, ch = chunk_slot(c)
                nf = FCH[c]
                c0_, c1_ = blk
                W = c1_ - c0_
                xr = [XR[t] for t in tiles_of(c0_, c1_)]
                for m in range(8):
                    ps, pb = psum()
                    for fi in range(nf):
                        mm(ps[:, 0:W], wd[:, fi, m * 128:(m + 1) * 128], hid[hb][:, fi, 0:W], fi == 0, fi == nf - 1, [chunk_slot_d(c)[0], HID[hb]], [pb])
                    tt(x_res[:, m, c0_:c1_], ps[:, 0:W], x_res[:, m, c0_:c1_], ALU.add, [pb] + xr, xr)
                if c == len(FCH) - 1:
                    after_block(blk)

            load_chunk(layer, 1, extra=mixer_end_ops)
            for c in range(len(FCH)):
                wg, wu, wd, sb_, ch = chunk_slot(c)
                nf = FCH[c]
                for blk in blocks:
                    c0_, c1_ = blk
                    W = c1_ - c0_
                    ha = [HA[t] for t in tiles_of(c0_, c1_)]
                    hb = it[0] % 2
                    it[0] += 1
                    for fi in range(nf):
                        psg, pbg = psum()
                        for kc in range(8):
                            mm(psg[:, 0:W], wg[:, kc, fi * 128:(fi + 1) * 128], h_all[:, kc, c0_:c1_], kc == 0, kc == 7, [sb_] + ha, [pbg])
                        psu, pbu = psum()
                        for kc in range(8):
                            mm(psu[:, 0:W], wu[:, kc, fi * 128:(fi + 1) * 128], h_all[:, kc, c0_:c1_], kc == 0, kc == 7, [sb_] + ha, [pbu])
                        k_ = fi % 2
                        act(s_sb[k_][:, 0:W], psg[:, 0:W], AF.Silu, [pbg], [SSB[k_]])
                        tt(hid[hb][:, fi, 0:W], psu[:, 0:W], s_sb[k_][:, 0:W], ALU.mult, [pbu, SSB[k_]], [HID[hb]])
                    if pending[0] is not None:
                        down(*pending[0])
                    pending[0] = (c, blk, hb)
                if c + 2 < len(FCH):
                    if pending[0] is not None:
                        down(*pending[0])
                        pending[0] = None
                    load_chunk(layer, c + 2)
                    for _ in range(4):
                        if bg:
                            bg.pop(0)()
            if pending[0] is not None:
                down(*pending[0])
                pending[0] = None
            while bg:
                bg.pop(0)()
            if next_chunk0_loader is not None:
                next_chunk0_loader()

        setup()
        if stg >= 1:
            load_mixer_weights(0)
            load_chunk(0, 0)

        kinds = [["halo", "prompt", "prompt", "prompt", "prompt"], ["prompt", "prompt", "prompt", "prompt", "sample"]]
        ffn_blocks = [
            [[(128, 256), (256, 768), (768, 1280)], [(256, 768), (768, 1280)]],
            [[(0, 512), (512, 1024), (1024, 1280)], [(0, 512), (512, 1024), (1024, 1280)]],
        ]

        for g in range(0 if stg < 1 else (2 if stg >= 6 else 1)):
            gstg = stg if (g == 0 or stg >= 99) else {6: 2, 7: 2, 8: 4, 9: 4, 10: 1, 11: 2}[stg]
            nb0 = 4 if (g == 1 and stg == 6) else (3 if (g == 1 and stg == 11) else 5)
            nb1 = 4 if (g == 1 and stg == 8) else 5
            for b in range(5):
                for t in (2 * b, 2 * b + 1):
                    load_tile(g, t)
                norm(b * 256, b * 256 + 256, C_GMIX + 0)
            if gstg <= 1:
                for t in range(2, 10):
                    store_tile(g, t)
                continue
            for e_ in ("act", "dve", "pe"):
                fence(e_, list(SLB.rs.values()) + list(SLB.ws.values()) + list(SLBd.rs.values()) + list(SLBd.ws.values()))
            for b in range(nb0):
                mixer0(g, b, kinds[g][b])
            flush_norm()
            mix_end = [P.last.get(e_) for e_ in ("pe", "act", "dve")] + scr_dma[-1:]
            if gstg <= 2:
                for t in range(2, 10):
                    store_tile(g, t)
                continue
            bg1_jobs = load_mixer_weights(1, defer=True)

            def after_l0(blk):
                norm(blk[0], blk[1], C_GMIX + 8)

            ffn(0, ffn_blocks[g][0], after_l0, None, mix_end, bg1_jobs)
            if gstg <= 3:
                for t in range(2, 10):
                    store_tile(g, t)
                continue
            for e_ in ("act", "dve", "pe"):
                fence(e_, list(SLB.rs.values()) + list(SLB.ws.values()) + list(SLBd.rs.values()) + list(SLBd.ws.values()))
            build_resident_taps()
            for b in range(nb1):
                mixer1(g, b, kinds[g][b])
            flush_norm()
            load_chunk(1, 0)
            mix_end = [P.last.get(e_) for e_ in ("pe", "act", "dve")] + scr_dma[-1:]
            if gstg <= 4:
                for t in range(2, 10):
                    store_tile(g, t)
                continue
            bg_jobs = load_mixer_weights(0, defer=True) if g == 0 else []

            def after_l1(blk, g=g):
                for t in tiles_of(blk[0], blk[1]):
                    store_tile(g, t)

            ffn(1, ffn_blocks[g][1], after_l1, (lambda: load_chunk(0, 0)) if g == 0 else None, mix_end, bg_jobs)

        P.op("sp", None, reads=[OUT], extra=list(chan_last.values()))
        P.lower(block, sems)
    return nc


def _host_consts(core):
    c = np.zeros((128, NCONST), np.float32)
    c[:, K_ID:K_ID + 128] = np.eye(128, dtype=np.float32)
    j = np.arange(128)[:, None]
    i = np.arange(128)[None, :]
    c[:, K_TRIU:K_TRIU + 128] = (j <= i).astype(np.float32)
    q = np.arange(128)[:, None]
    s = np.arange(256)[None, :]
    dist = np.abs(128 + q - s).astype(np.float32)
    qc = q // 64
    sc = s // 64
    allowed = (sc >= qc) & (sc <= qc + 2)
    nd = np.where(allowed, -dist, -1.0e6).astype(np.float32)
    c[:, K_ND:K_ND + 256] = nd
    mk = np.zeros((128, 256), np.float32)
    if core == 0:
        mk[:, 0:128] = -30000.0
    c[:, K_NDF:K_NDF + 256] = mk
    for gI, w in enumerate((2, 4, 8, 16)):
        pos = np.arange(16)
        cntv = np.minimum(pos + 1, w) if core == 0 else np.full(16, w)
        c[:, K_ICNT + gI * 16:K_ICNT + (gI + 1) * 16] = (1.0 / cntv.astype(np.float32))[None, :]
    bd = np.zeros((128, 128), np.float32)
    bd[0:64, 0:64] = 1.0
    bd[64:128, 64:128] = 1.0
    c[:, K_BD:K_BD + 128] = bd
    j64 = (np.arange(128) % 64)[:, None]
    i64 = np.arange(64)[None, :]
    c[:, K_TRIU64:K_TRIU64 + 64] = (j64 <= i64).astype(np.float32)
    return c


def kernel(x_prompt, x_sample, state_pool, state_swa_k, state_swa_v, state_conv,
           norm_mix, norm_ffn, w_in_even, q_norm, k_norm, attn_sinks, pool_w, pool_scale,
           w_out_even, w_in_odd, conv_w, conv_b, conv_ln_g, conv_ln_b, gmlp_ln_g, gmlp_ln_b,
           gmlp_w, gmlp_b, w_out_odd, ffn_gate, ffn_up, ffn_down):
    f = lambda a: np.ascontiguousarray(np.asarray(a, dtype=np.float32))
    xp = f(x_prompt)[0]
    xsm = f(x_sample).reshape(32 * 64, 1024)
    cols = np.zeros((128, NCOL), np.float32)
    nm, nf_ = f(norm_mix), f(norm_ffn)
    for l in range(2):
        cols[:, C_GMIX + 8 * l:C_GMIX + 8 * l + 8] = nm[l].reshape(8, 128).T
        cols[:, C_GFFN + 8 * l:C_GFFN + 8 * l + 8] = nf_[l].reshape(8, 128).T
    cols[:, C_PSC:C_PSC + 4] = f(pool_scale)[0].reshape(4, 128).T
    cols[:, C_GQ] = np.tile(f(q_norm)[0], 2)
    cols[:, C_GK] = np.tile(f(k_norm)[0], 2)
    cols[:, C_SINK:C_SINK + 8] = np.broadcast_to(f(attn_sinks)[0][None, :], (128, 8))
    cols[:, C_CB:C_CB + 4] = f(conv_b)[0].reshape(4, 128).T
    cols[:, C_LG:C_LG + 4] = f(conv_ln_g)[0].reshape(4, 128).T
    cols[:, C_LB:C_LB + 4] = f(conv_ln_b)[0].reshape(4, 128).T
    cw = f(conv_w)[0]
    cols[:, C_CW:C_CW + 124] = cw.reshape(31, 4, 128).transpose(2, 1, 0).reshape(128, 124)
    lng = np.broadcast_to(f(gmlp_ln_g)[0][None, :], (128, 512))
    lnb = np.broadcast_to(f(gmlp_ln_b)[0][None, :], (128, 512))
    gb = np.broadcast_to(f(gmlp_b)[0].reshape(1, 512), (128, 512))
    gwT = np.ascontiguousarray(f(gmlp_w)[0].transpose(0, 2, 1)).reshape(512, 128)
    shared = {
        "cols": cols,
        "w_in0": f(w_in_even)[0], "w_out0": f(w_out_even)[0], "pool_w": f(pool_w)[0].reshape(512, 128),
        "w_in1": f(w_in_odd)[0], "w_out1": f(w_out_odd)[0], "gwT": gwT,
        "fgate": f(ffn_gate).reshape(2048, 2816), "fup": f(ffn_up).reshape(2048, 2816), "fdown": f(ffn_down).reshape(5632, 1024),
    }
    sp, sk, sv, scv = f(state_pool)[0], f(state_swa_k)[0], f(state_swa_v)[0], f(state_conv)[0]
    in_maps = []
    for c in range(N_CORES):
        xin = np.zeros((2560, 1024), np.float32)
        if c > 0:
            xin[0:256] = xp[2048 * c - 256:2048 * c]
        xin[256:2304] = xp[2048 * c:2048 * (c + 1)]
        xin[2304:2560] = xsm[256 * c:256 * (c + 1)]
        cst = _host_consts(c)
        cst[:, K_LNG:K_LNG + 512] = lng
        cst[:, K_LNB:K_LNB + 512] = lnb
        cst[:, K_GB:K_GB + 512] = gb
        m = dict(shared)
        m.update({
            "xin": xin, "consts": cst,
            "st_pool": np.ascontiguousarray(sp[4 * c:4 * c + 4].reshape(60, 512)),
            "st_k": np.ascontiguousarray(sk[4 * c:4 * c + 4].reshape(512, 128)),
            "st_v": np.ascontiguousarray(sv[4 * c:4 * c + 4].reshape(512, 128)),
            "st_conv": np.ascontiguousarray(scv[4 * c:4 * c + 4].reshape(120, 512)),
        })
        in_maps.append(m)
    nc = build_nc(_STAGE)
    res = run_bass_kernel_spmd(nc, in_maps, core_ids=list(range(N_CORES)))
    if _STAGE < 99:
        return res.results
    R = res.results
    y_prompt = np.concatenate([R[c]["yout"][0:2048] for c in range(N_CORES)], 0)[None]
    y_sample = np.concatenate([R[c]["yout"][2048:2304] for c in range(N_CORES)], 0).reshape(32, 64, 1024)
    def _rows(a, n):
        a = a.reshape(5, 128, 512)
        out = [a[0, 128 - n:128]]
        for s_ in range(4):
            w = 320 if n == 16 else 384
            per = 80 if n == 16 else 96
            w0 = min(max(s_ * per + per - 128, 0), w - 128)
            r0 = s_ * per + 64 - w0
            out.append(a[1 + s_, r0:r0 + n])
        return np.stack(out, 0)

    po = [_rows(R[c]["pool_o"], 16) for c in range(N_CORES)]
    ko = [R[c]["k_o"].reshape(5, 128, 2, 64) for c in range(N_CORES)]
    vo = [R[c]["v_o"].reshape(5, 128, 2, 64) for c in range(N_CORES)]
    co = [_rows(R[c]["conv_o"], 32) for c in range(N_CORES)]
    gvo = [R[c]["gv_o"].reshape(4, 64, 512) for c in range(N_CORES)]
    pool_prompt = po[7][0:1, 1:16][None]
    pool_sample = np.concatenate([p[1:5, 1:16] for p in po], 0)[None]
    k_prompt = ko[7][0:1][None]
    k_sample = np.concatenate([k[1:5] for k in ko], 0)[None]
    v_prompt = vo[7][0:1][None]
    v_sample = np.concatenate([v[1:5] for v in vo], 0)[None]
    conv_prompt = co[7][0:1, 2:32][None]
    conv_sample = np.concatenate([x[1:5, 2:32] for x in co], 0)[None]
    gv_sample = np.concatenate(gvo, 0)[None]
    outs = (y_prompt, y_sample, pool_prompt, pool_sample, k_prompt, k_sample, v_prompt, v_sample,
            conv_prompt, conv_sample, gv_sample)
    return tuple(np.ascontiguousarray(o, dtype=np.float32) for o in outs)
```

```python
import numpy as np
from contextlib import ExitStack
import concourse.bass as bass
import concourse.mybir as mybir
from concourse.bass_utils import run_bass_kernel_spmd

F32 = mybir.dt.float32
BF16 = mybir.dt.bfloat16
ALU = mybir.AluOpType
AF = mybir.ActivationFunctionType
AX = mybir.AxisListType

SAME_ENGINE_SYNC = True
N_CORES = 8
SBUF_BASE = 16512
SBUF_LIMIT = 229376

C_GMIX = 0
C_GFFN = 16
C_PSC = 32
C_GQ = 36
C_GK = 37
C_SINK = 38
C_CB = 46
C_LG = 50
C_LB = 54
C_CW = 58
NCOL = 58 + 124
K_ID = 0
K_TRIU = 128
K_ND = 256
K_NDF = 512
K_ICNT = 768
K_LNG = 832
K_LNB = 1344
K_GB = 1856
K_BD = 2368
K_TRIU64 = 2496
NCONST = 2560

_STAGE = 99
_LMASK = 7
_SMASK = 255
FCH = [3, 3, 3, 3, 3, 3, 3, 1]


class Buf:
    __slots__ = ("name", "ws", "rs")

    def __init__(self, name=""):
        self.name = name
        self.ws = {}
        self.rs = {}


class Chan:
    def __init__(self, sem, wait_all=False):
        self.sem = sem
        self.n = 0
        self.wait_all = wait_all


class Op:
    __slots__ = ("eng", "fn", "deps", "signal", "cnt", "chan", "chan_cnt", "idx")


def _key(o):
    return o.eng if o.chan is None else ("c", id(o.chan))


class Prog:
    ENGS = ("pe", "act", "dve", "pool", "sp")

    def __init__(self):
        self.ops = {e: [] for e in self.ENGS}
        self.n = 0
        self.last = {}

    def op(self, eng, fn, reads=(), writes=(), chan=None, extra=()):
        o = Op()
        o.eng = eng
        o.fn = fn
        o.signal = False
        o.cnt = 0
        o.chan = chan
        o.idx = self.n
        self.n += 1
        if chan is not None:
            chan.n += 1
            o.chan_cnt = chan.n
        else:
            o.chan_cnt = 0
        deps = {}

        def add(p, raw):
            if p is o:
                return
            if p.chan is None and o.chan is None and p.eng == eng:
                if eng == "pe" or not SAME_ENGINE_SYNC or (not raw and eng != "pool"):
                    return
            k = _key(p)
            q = deps.get(k)
            if q is None or q.idx < p.idx:
                deps[k] = p

        for b in reads:
            for w in b.ws.values():
                add(w, True)
        for b in writes:
            for r in b.rs.values():
                add(r, False)
            for w in b.ws.values():
                add(w, False)
        for p in extra:
            if p is not None:
                add(p, True)
        k = _key(o)
        for b in reads:
            b.rs[k] = o
        for b in writes:
            if b.rs:
                b.ws = {}
                b.rs = {}
            b.ws[k] = o
        o.deps = list(deps.values())
        for p in o.deps:
            if p.chan is None:
                p.signal = True
        self.ops[eng].append(o)
        if chan is None and fn is not None:
            self.last[eng] = o
        return o

    def lower(self, block, sems):
        for e in self.ENGS:
            c = 0
            for o in self.ops[e]:
                if o.chan is None and o.signal:
                    c += 1
                    o.cnt = c

        def run(ename):
            def body(eng):
                waited = {}
                for o in self.ops[ename]:
                    need = {}
                    for p in o.deps:
                        if p.chan is not None:
                            s = p.chan.sem
                            v = 16 * (p.chan.n if p.chan.wait_all else p.chan_cnt)
                        else:
                            s, v = sems[p.eng], p.cnt
                        k = id(s)
                        if need.get(k, (None, 0))[1] < v:
                            need[k] = (s, v)
                    for k, (s, v) in need.items():
                        if waited.get(k, 0) >= v:
                            continue
                        waited[k] = v
                        eng.wait_ge(s, v)
                    if o.fn is None:
                        continue
                    ins = o.fn(eng)
                    if o.chan is not None:
                        ins.then_inc(o.chan.sem, 16)
                    elif o.signal:
                        ins.then_inc(sems[ename], 1)
            return body

        block.tensor(run("pe"))
        block.scalar(run("act"))
        block.vector(run("dve"))
        block.gpsimd(run("pool"))
        block.sync(run("sp"))


def bcast_mid(ap, n):
    a = ap.ap
    return bass.AP(tensor=ap.tensor, offset=ap.offset, ap=[list(a[0]), [0, n]] + [list(x) for x in a[1:]])


def build_nc(stg=99):
    nc = bass.Bass("TRN2", target_bir_lowering=False)

    def din(name, shape):
        return nc.dram_tensor(name, shape, F32, kind="ExternalInput").ap()

    def dout(name, shape):
        return nc.dram_tensor(name, shape, F32, kind="ExternalOutput").ap()

    xin = din("xin", [2560, 1024])
    cols_d = din("cols", [128, NCOL])
    consts_d = din("consts", [128, NCONST])
    w_in0_d = din("w_in0", [1024, 1280])
    w_out0_d = din("w_out0", [1024, 1024])
    pool_w_d = din("pool_w", [512, 128])
    w_in1_d = din("w_in1", [1024, 2048])
    w_out1_d = din("w_out1", [1024, 1024])
    gwT_d = din("gwT", [512, 128])
    fgate_d = din("fgate", [2048, 2816])
    fup_d = din("fup", [2048, 2816])
    fdown_d = din("fdown", [5632, 1024])
    st_pool_d = din("st_pool", [60, 512])
    st_k_d = din("st_k", [512, 128])
    st_v_d = din("st_v", [512, 128])
    st_conv_d = din("st_conv", [120, 512])
    yout = dout("yout", [2304, 1024])
    pool_o = dout("pool_o", [5 * 128, 512])
    k_o = dout("k_o", [5 * 128, 128])
    v_o = dout("v_o", [5 * 128, 128])
    conv_o = dout("conv_o", [5 * 128, 512])
    gv_o = dout("gv_o", [4 * 64, 512])

    cnt = [0]

    class Region:
        def __init__(self, start, limit):
            self.off = start
            self.limit = limit

        def alloc(self, shape, dt):
            size = 1
            for s in shape[1:]:
                size *= s
            size *= 4 if dt == F32 else 2
            size = (size + 63) // 64 * 64
            at = self.off
            self.off += size
            assert self.off <= self.limit, ("SBUF overflow", self.off, self.limit)
            cnt[0] += 1
            return nc.alloc_sbuf_tensor_at("sb%d" % cnt[0], list(shape), dt, offset=at)

    perm = Region(SBUF_BASE, SBUF_LIMIT)
    x_res = perm.alloc([128, 8, 1280], F32)
    h_all = perm.alloc([128, 8, 1280], BF16)
    wmi = perm.alloc([128, 8, 2048], BF16)
    wmo = perm.alloc([128, 8, 1024], BF16)
    wgA = perm.alloc([128, 8, 384], BF16)
    wuA = perm.alloc([128, 8, 384], BF16)
    wdA = perm.alloc([128, 3, 1024], BF16)
    xs = [perm.alloc([128, 512], F32) for _ in range(2)]
    cols = perm.alloc([128, NCOL], F32)
    consts = perm.alloc([128, NCONST], F32)
    pw = perm.alloc([128, 4, 128], BF16)
    wm = perm.alloc([128, 4, 128], BF16)
    wms = perm.alloc([128, 4, 64], BF16)
    ident_bf = perm.alloc([128, 128], BF16)
    ones_bf = perm.alloc([128, 128], BF16)
    bd_bf = perm.alloc([128, 128], BF16)
    kTstz = [perm.alloc([128, 4, 128], BF16) for _ in range(2)]
    btab = perm.alloc([128, 8, 256], BF16)
    maskf = perm.alloc([128, 256], BF16)
    negsink = perm.alloc([128, 8], F32)
    Vst = perm.alloc([128, 4, 128], BF16)
    ulead_s = perm.alloc([128, 4, 4, 15], F32)
    clead_s = perm.alloc([128, 4, 4, 30], F32)
    gq8 = perm.alloc([128, 2], F32)
    kT_c = perm.alloc([128, 128], BF16)
    V_c = perm.alloc([128, 128], BF16)
    u_c = perm.alloc([128, 4, 16], F32)
    glu_c = perm.alloc([128, 4, 32], F32)
    xsq = perm.alloc([128, 8, 512], BF16)
    rs = perm.alloc([128, 512], F32)
    s_sb = [perm.alloc([128, 512], F32) for _ in range(2)]
    hid = [perm.alloc([128, 3, 512], BF16) for _ in range(2)]
    SCR = perm.off
    rb = Region(SCR, SBUF_LIMIT)
    wgB = rb.alloc([128, 8, 384], BF16)
    wuB = rb.alloc([128, 8, 384], BF16)
    wdB = rb.alloc([128, 3, 1024], BF16)
    r0 = Region(SCR, SBUF_LIMIT)
    u_ext = r0.alloc([128, 4, 320], F32)
    ptmp = [r0.alloc([128, 320], F32) for _ in range(2)]
    pooled = r0.alloc([128, 4, 256], BF16)
    cat = r0.alloc([128, 8, 256], BF16)
    rq2 = [r0.alloc([128, 2, 256], F32) for _ in range(2)]
    qn = r0.alloc([128, 4, 256], BF16)
    kn = r0.alloc([128, 256], F32)
    kTz = [r0.alloc([128, 384], BF16) for _ in range(2)]
    V_blk = r0.alloc([128, 4, 128], BF16)
    att_pn = [r0.alloc([128, 256], BF16) for _ in range(8)]
    att_PT = [r0.alloc([128, 1024], BF16) for _ in range(2)]
    att_sm = r0.alloc([128, 32, 4], F32)
    ptiny = r0.alloc([128, 4, 16], F32)
    r1 = Region(SCR, SBUF_LIMIT)
    glu_ext = r1.alloc([128, 4, 384], F32)
    glu_bf = r1.alloc([128, 4, 384], BF16)
    dgb = [r1.alloc([128, 16, 128], BF16) for _ in range(2)]
    mv = r1.alloc([128, 3, 256], F32)
    ctmp2 = r1.alloc([128, 2, 256], F32)
    ctmp = [ctmp2[:, 0, :], ctmp2[:, 1, :]]
    zu = r1.alloc([128, 4, 256], BF16)
    zg = r1.alloc([128, 512], F32)
    sig = zg[:].rearrange("p (a n) -> p a n", a=2)
    vn_bf = r1.alloc([128, 2, 512], BF16)
    cat1 = r1.alloc([128, 8, 256], BF16)
    st6 = r1.alloc([128, 2, 8], F32)
    mvz = r1.alloc([128, 2, 4], F32)
    gtmp = ctmp
    qsq = xsq

    psf = [nc.alloc_psum_tensor("psf%d" % i, [128, 512], F32) for i in range(7)]
    psb = nc.alloc_psum_tensor("psb", [128, 1024], BF16)

    with ExitStack() as es:
        def sem(name):
            return es.enter_context(nc.semaphore(name))

        sems = {e: sem("s_" + e) for e in ("pe", "act", "dve", "pool")}
        ch_setup = Chan(sem("c_setup"), wait_all=True)
        ch_setup2 = Chan(sem("c_setup2"), wait_all=True)
        ch_wmi = Chan(sem("c_wmi"))
        ch_wmo = Chan(sem("c_wmo"))
        ch_fA = Chan(sem("c_fA"))
        ch_fB = Chan(sem("c_fB"))
        ch_fAd = Chan(sem("c_fAd"))
        ch_fBd = Chan(sem("c_fBd"))
        ch_xin = [Chan(sem("c_xin%d" % i)) for i in range(2)]
        ch_out = [Chan(sem("c_out%d" % i)) for i in range(2)]
        ch_misc = Chan(sem("c_misc"))
        block = es.enter_context(nc.Block())
        P = Prog()

        XR = [Buf("xr%d" % t) for t in range(10)]
        HA = [Buf("ha%d" % t) for t in range(10)]
        B = {}

        def bf(name):
            if name not in B:
                B[name] = Buf(name)
            return B[name]

        PSB = [Buf("ps%d" % i) for i in range(7)]
        PTB = Buf("ptb")
        XS = [Buf("xs0"), Buf("xs1")]
        OUT = Buf("out")
        ps_rr = [0]
        xs_rr = [0]

        pinned = set()

        def psum(pin=False):
            while True:
                i = ps_rr[0] % 7
                ps_rr[0] += 1
                if i not in pinned:
                    break
            if pin:
                pinned.add(i)
            return psf[i], PSB[i]

        def unpin(pb):
            pinned.discard(PSB.index(pb))

        def stage():
            i = xs_rr[0] % 2
            xs_rr[0] += 1
            return xs[i], XS[i], i

        def mm(out, lhsT, rhs, start, stop, r, w):
            return P.op("pe", lambda e: e.matmul(out, lhsT, rhs, start=start, stop=stop), reads=r, writes=w)

        def tr(out, in_, ident, r, w):
            return P.op("pe", lambda e: e.transpose(out, in_, ident), reads=r, writes=w)

        def act(out, in_, func, r, w, bias=None, scale=None, accum=None):
            kw = {}
            if bias is not None:
                kw["bias"] = bias
            if scale is not None:
                kw["scale"] = scale
            if accum is not None:
                kw["accum_out"] = accum
            return P.op("act", lambda e: e.activation(out=out, in_=in_, func=func, **kw), reads=r, writes=w)

        def dve(fn, r, w):
            return P.op("dve", fn, reads=r, writes=w)

        def tt(out, in0, in1, op, r, w):
            return dve(lambda e: e.tensor_tensor(out, in0, in1, op), r, w)

        def stt(out, in0, scalar, in1, op0, op1, r, w):
            return dve(lambda e: e.scalar_tensor_tensor(out, in0, scalar, in1, op0, op1), r, w)

        def ts(out, in0, s1, s2, op0, op1, r, w):
            if op1 is None:
                return dve(lambda e: e.tensor_scalar(out, in0, s1, s2, op0), r, w)
            return dve(lambda e: e.tensor_scalar(out, in0, s1, s2, op0, op1), r, w)

        def cpy(out, in_, r, w):
            return dve(lambda e: e.tensor_copy(out, in_), r, w)

        def recip(out, in_, r, w):
            return dve(lambda e: e.reciprocal(out, in_), r, w)

        def mset(ap, v, w):
            return dve(lambda e: e.memset(ap, v), [], w)

        chan_last = {}

        def dma(q, out, in_, chan, r, w):
            o = P.op(q, lambda e: e.dma_start(out=out, in_=in_), reads=r, writes=w, chan=chan)
            chan_last[id(chan)] = o
            return o

        def fence(eng, deps):
            P.op(eng, None, extra=deps)

        scr_dma = []

        def dma_scr(q, out, in_, chan, r, w):
            scr_dma.append(dma(q, out, in_, chan, r, w))

        CONST = bf("consts")
        COLS = bf("cols")

        def col(i):
            return cols[:, i:i + 1]

        ident_f = consts[:, K_ID:K_ID + 128]

        def setup():
            dma("sp", cols[:], cols_d[:, :], ch_setup, [], [COLS])
            dma("sp", consts[:], consts_d[:, :], ch_setup, [], [CONST])
            act(ident_bf[:], ident_f, AF.Copy, [CONST], [bf("ident_bf")])
            act(bd_bf[:], consts[:, K_BD:K_BD + 128], AF.Copy, [CONST], [bf("bd_bf")])
            mset(ones_bf[:], 1.0, [bf("ones_bf")])
            ts(gq8[:, 0:1], col(C_GQ), 0.125, None, ALU.mult, None, [COLS], [bf("gq8")])
            mset(kT_c[:], 0.0, [bf("kT_c")])
            for h in range(8):
                ts(btab[:, h, :], consts[:, K_ND:K_ND + 256], 2.0 ** (-(h + 1)), None, ALU.mult, None, [CONST], [bf("btab")])
            act(maskf[:], consts[:, K_NDF:K_NDF + 256], AF.Copy, [CONST], [bf("maskf")])
            ts(negsink[:], cols[:, C_SINK:C_SINK + 8], -1.0, None, ALU.mult, None, [COLS], [bf("negsink")])
            mset(V_c[:], 0.0, [bf("V_c")])
            mset(u_c[:], 0.0, [bf("u_c")])
            mset(glu_c[:], 0.0, [bf("glu_c")])
            if stg == -1:
                return
            dma("pool", pw[:], pool_w_d.rearrange("(g c) d -> c g d", c=128), ch_setup2, [], [bf("pw")])
            dma("pool", Vst[:], st_v_d.rearrange("(s t) c -> t s c", t=128), ch_setup2, [], [bf("Vst")])
            if stg == -2:
                return
            s0, sb0, i0 = stage()
            dma("sp", s0[:, 0:512].rearrange("p (g i) -> p g i", g=4), gwT_d.rearrange("(g j) i -> j g i", j=128), ch_xin[i0], [], [sb0])
            tt(wm[:], s0[:, 0:512].rearrange("p (g i) -> p g i", g=4), bcast_mid(consts[:, K_TRIU:K_TRIU + 128], 4), ALU.mult,
               [sb0, CONST], [bf("wm")])
            gv = gwT_d.rearrange("(g j) i -> j g i", j=128)
            s0b, sb0b, i0b = stage()
            dma("sp", s0b[0:64, 0:256].rearrange("p (g i) -> p g i", g=4), gv[0:64, :, 0:64], ch_xin[i0b], [], [sb0b])
            dma("sp", s0b[64:128, 0:256].rearrange("p (g i) -> p g i", g=4), gv[0:64, :, 0:64], ch_xin[i0b], [], [sb0b])
            tt(wms[:], s0b[:, 0:256].rearrange("p (g i) -> p g i", g=4), bcast_mid(consts[:, K_TRIU64:K_TRIU64 + 64], 4), ALU.mult,
               [sb0b, CONST], [bf("wms")])
            if stg == -3:
                return
            s1, sb1, i1 = stage()
            dma("sp", s1[:, 0:512].rearrange("p (s c) -> p s c", s=4), st_k_d.rearrange("(s t) c -> t s c", t=128), ch_xin[i1], [], [sb1])
            ps, pb = psum()
            for s in range(4):
                tr(ps[:, s * 128:(s + 1) * 128], s1[:, s * 128:(s + 1) * 128], ident_f, [sb1, CONST], [pb])
            mset(kTstz[0][:], 0.0, [bf("kTst")])
            mset(kTstz[1][:], 0.0, [bf("kTst")])
            act(kTstz[0][0:64, :, :], ps[0:64, :].rearrange("p (s t) -> p s t", s=4), AF.Copy, [pb], [bf("kTst")])
            act(kTstz[1][64:128, :, :], ps[64:128, :].rearrange("p (s t) -> p s t", s=4), AF.Copy, [pb], [bf("kTst")])
            for s in range(4):
                dma("sp", k_o[(1 + s) * 128:(1 + s) * 128 + 64, :], s1[64:128, s * 128:(s + 1) * 128], ch_out[i1], [sb1], [OUT])
            if stg == -4:
                return
            s2, sb2, i2 = stage()
            dma("sp", s2[0:60, 0:512], st_pool_d[:, :], ch_xin[i2], [], [sb2])
            ps, pb = psum()
            for g in range(4):
                tr(ps[:, g * 64:g * 64 + 60], s2[0:60, g * 128:(g + 1) * 128], consts[0:60, K_ID:K_ID + 60], [sb2, CONST], [pb])
            act(ulead_s[:], ps[:, 0:256].rearrange("p (g n) -> p g n", g=4)[:, :, 0:60].rearrange("p g (s r) -> p g s r", s=4),
                AF.Copy, [pb], [bf("ulead_s")])
            if stg == -5:
                return
            s3, sb3, i3 = stage()
            dma("sp", s3[0:120, 0:512], st_conv_d[:, :], ch_xin[i3], [], [sb3])
            ps, pb = psum()
            for g in range(4):
                tr(ps[:, g * 128:g * 128 + 120], s3[0:120, g * 128:(g + 1) * 128], consts[0:120, K_ID:K_ID + 120], [sb3, CONST], [pb])
            act(clead_s[:], ps[:].rearrange("p (g n) -> p g n", g=4)[:, :, 0:120].rearrange("p g (s r) -> p g s r", s=4),
                AF.Copy, [pb], [bf("clead_s")])
            if stg == -6:
                return
            s4, sb4, i4 = stage()
            dma("sp", s4[:, 0:512].rearrange("p (s c) -> p s c", s=4), st_v_d.rearrange("(s t) c -> t s c", t=128), ch_xin[i4], [], [sb4])
            for s in range(4):
                dma("sp", v_o[(1 + s) * 128:(1 + s) * 128 + 64, :], s4[64:128, s * 128:(s + 1) * 128], ch_out[i4], [sb4], [OUT])

        WMI = bf("wmi")
        WMO = bf("wmo")
        SLA = bf("slotA")
        SLB = bf("slotB")
        SLAd = bf("slotAd")
        SLBd = bf("slotBd")

        def load_mixer_weights(layer, defer=False):
            jobs = []

            def dq(*args):
                jobs.append(lambda: dma(*args))

            if layer == 0:
                v = w_in0_d.rearrange("(kc p) n -> p kc n", p=128)
                dq("pool", wmi[:, :, 0:512], v[:, :, 0:512], ch_wmi, [], [WMI])
                for j in range(4):
                    for slot in range(2):
                        h = slot * 4 + j
                        dq("pool", wmi[:, :, 512 + j * 128 + slot * 64:512 + j * 128 + slot * 64 + 64],
                            v[:, :, 512 + h * 64:512 + h * 64 + 64], ch_wmi, [], [WMI])
                dq("pool", wmi[:, :, 1024:1280], v[:, :, 1024:1280], ch_wmi, [], [WMI])
                vo = w_out0_d
                dq("pool", wmo[:, 0:4, :], vo[0:512, :].rearrange("(kc p) n -> p kc n", p=128), ch_wmo, [], [WMO])
                for j in range(4):
                    for slot in range(2):
                        h = slot * 4 + j
                        dq("pool", wmo[slot * 64:(slot + 1) * 64, 4 + j, :], vo[512 + h * 64:512 + h * 64 + 64, :], ch_wmo, [], [WMO])
            else:
                v = w_in1_d.rearrange("(kc p) n -> p kc n", p=128)
                dq("pool", wmi[:, :, 0:1024], v[:, :, 0:1024], ch_wmi, [], [WMI])
                dq("pool", wmi[:, :, 1024:2048], v[:, :, 1024:2048], ch_wmi, [], [WMI])
                dq("pool", wmo[:], w_out1_d.rearrange("(kc p) n -> p kc n", p=128), ch_wmo, [], [WMO])
            if defer:
                return jobs
            for j_ in jobs:
                j_()
            return []

        def chunk_slot(c):
            if c % 2 == 0:
                return wgA, wuA, wdA, SLA, ch_fA
            return wgB, wuB, wdB, SLB, ch_fB

        def chunk_slot_d(c):
            if c % 2 == 0:
                return SLAd, ch_fAd
            return SLBd, ch_fBd

        def load_chunk(layer, c, extra=()):
            wg, wu, wd, sb_, ch = chunk_slot(c)
            nf = FCH[c]
            f0 = sum(FCH[:c])
            gv = fgate_d[layer * 1024:(layer + 1) * 1024, :].rearrange("(kc p) n -> p kc n", p=128)
            uv = fup_d[layer * 1024:(layer + 1) * 1024, :].rearrange("(kc p) n -> p kc n", p=128)
            dvw = fdown_d[layer * 2816 + f0 * 128:layer * 2816 + (f0 + nf) * 128, :].rearrange("(f p) n -> p f n", p=128)
            if extra:
                fence("pool", extra)
            dma("pool", wg[:, :, 0:nf * 128], gv[:, :, f0 * 128:(f0 + nf) * 128], ch, [], [sb_])
            dma("pool", wu[:, :, 0:nf * 128], uv[:, :, f0 * 128:(f0 + nf) * 128], ch, [], [sb_])
            sbd_, chd_ = chunk_slot_d(c)
            dma("pool", wd[:, 0:nf, :], dvw, chd_, [], [sbd_])

        def load_tile(g, t):
            row = (g * 10 + t) * 128
            for half in range(2):
                s, sb_, i = stage()
                dma("sp", s[:], xin[row:row + 128, half * 512:(half + 1) * 512], ch_xin[i], [], [sb_])
                ps, pb = psum()
                for k in range(4):
                    tr(ps[:, k * 128:(k + 1) * 128], s[:, k * 128:(k + 1) * 128], ident_f, [sb_, CONST], [pb])
                act(x_res[:, half * 4:half * 4 + 4, t * 128:(t + 1) * 128], ps[:].rearrange("p (k n) -> p k n", k=4), AF.Copy,
                    [pb], [XR[t]])

        def store_tile(g, t):
            row = (g * 10 + t - 2) * 128
            for half in range(2):
                s, sb_, i = stage()
                ps, pb = psum()
                for k in range(4):
                    kc = half * 4 + k
                    tr(ps[:, k * 128:(k + 1) * 128], x_res[:, kc, t * 128:(t + 1) * 128], ident_f, [XR[t], CONST], [pb])
                act(s[:], ps[:], AF.Copy, [pb], [sb_])
                dma("sp", yout[row:row + 128, half * 512:(half + 1) * 512], s[:], ch_out[i], [sb_], [OUT])

        def rows_out(wins, r0, nrows, dst, r):
            s, sb_, i = stage()
            ps, pb = psum()
            for g_, a in enumerate(wins):
                tr(ps[:, g_ * 128:(g_ + 1) * 128], a, ident_f, r + [CONST], [pb])
            act(s[:, 0:512], ps[:, :], AF.Copy, [pb], [sb_])
            dma("sp", dst, s[:, 0:512], ch_out[i], [sb_], [OUT])

        XSQ = bf("xsq")
        RS = bf("rs")

        def tiles_of(c0, c1):
            return list(range(c0 // 128, (c1 + 127) // 128))

        def norm(c0, c1, gcol):
            W = c1 - c0
            tl = tiles_of(c0, c1)
            xr = [XR[t] for t in tl]
            ha = [HA[t] for t in tl]
            for half in range(2):
                act(xsq[:, half * 4:half * 4 + 4, 0:W], x_res[:, half * 4:half * 4 + 4, c0:c1], AF.Square, xr, [XSQ])
            ps, pb = psum()
            for kc in range(8):
                mm(ps[:, 0:W], ones_bf[:], xsq[:, kc, 0:W], kc == 0, kc == 7, [XSQ, bf("ones_bf")], [pb])
            act(rs[:, 0:W], ps[:, 0:W], AF.Ln, [pb], [RS], bias=1e-6, scale=1.0 / 1024)
            act(rs[:, 0:W], rs[:, 0:W], AF.Exp, [RS], [RS], scale=-0.5)
            for kc in range(8):
                stt(h_all[:, kc, c0:c1], x_res[:, kc, c0:c1], col(gcol + kc), rs[:, 0:W], ALU.mult, ALU.mult,
                    xr + [RS, COLS], ha)

        def wout_residual(catt, CATB, c0):
            tl = tiles_of(c0, c0 + 256)
            xr = [XR[t] for t in tl]
            for pair in range(4):
                ps, pb = psum()
                for i in range(2):
                    m = pair * 2 + i
                    for kc in range(8):
                        mm(ps[:, i * 256:(i + 1) * 256], wmo[:, kc, m * 128:(m + 1) * 128], catt[:, kc, :], kc == 0, kc == 7,
                           [WMO, CATB], [pb])
                xv = x_res[:, pair * 2:pair * 2 + 2, c0:c0 + 256]
                tt(xv, ps[:].rearrange("p (a n) -> p a n", a=2), xv, ALU.add, [pb] + xr, xr)

        deferred_norm = []

        def flush_norm():
            while deferred_norm:
                deferred_norm.pop(0)()

        def mixer0(g, b, kind):
            c0 = b * 256
            tl = [2 * b, 2 * b + 1]
            ha = [HA[t] for t in tl]
            first = (g == 0 and b == 1)
            lastp = (g == 1 and b == 3)
            sample = kind == "sample"
            UE, PL, CAT, QN, KN, KTB, VB = bf("u_ext"), bf("pooled"), bf("cat"), bf("qn"), bf("kn"), bf("kTz"), bf("V_blk")
            hs = h_all[:, :, c0:c0 + 256]
            if sample:
                mset(u_ext[:], 0.0, [UE])
                mset(kTz[0][64:128, :], 0.0, [KTB])
                mset(kTz[1][0:64, :], 0.0, [KTB])
                for gI in range(4):
                    act(u_ext[:, gI, :].rearrange("p (s n) -> p s n", s=4)[:, :, 1:16], ulead_s[:, gI, :, :], AF.Copy, [bf("ulead_s")], [UE])
            else:
                cpy(u_ext[:, :, 0:16], u_c[:], [bf("u_c")], [UE])
                mset(kTz[0][64:128, :], 0.0, [KTB])
                mset(kTz[1][0:64, :], 0.0, [KTB])
                act(kTz[0][0:64, 0:128], kT_c[0:64, :], AF.Copy, [bf("kT_c")], [KTB])
                act(kTz[1][64:128, 0:128], kT_c[64:128, :], AF.Copy, [bf("kT_c")], [KTB])
                act(V_blk[:, 0, :], V_c[:], AF.Copy, [bf("V_c")], [VB])
            for pair in range(2):
                ps, pb = psum()
                for i in range(2):
                    m = pair * 2 + i
                    for kc in range(8):
                        mm(ps[:, i * 256:(i + 1) * 256], wmi[:, kc, m * 128:(m + 1) * 128], hs[:, kc, :], kc == 0, kc == 7,
                           [WMI] + ha, [pb])
                if not sample:
                    act(u_ext[:, pair * 2:pair * 2 + 2, 16:272], ps[:].rearrange("p (a n) -> p a n", a=2), AF.Copy, [pb], [UE])
                else:
                    for i in range(2):
                        m = pair * 2 + i
                        act(u_ext[:, m, :].rearrange("p (s n) -> p s n", s=4)[:, :, 16:80],
                            ps[:, i * 256:(i + 1) * 256].rearrange("p (s n) -> p s n", s=4), AF.Copy, [pb], [UE])
            QSQ = XSQ
            qps = []
            for pair in range(2):
                ps, pb = psum(pin=True)
                for i in range(2):
                    m = pair * 2 + i
                    for kc in range(8):
                        mm(ps[:, i * 256:(i + 1) * 256], wmi[:, kc, 512 + m * 128:512 + (m + 1) * 128], hs[:, kc, :], kc == 0, kc == 7,
                           [WMI] + ha, [pb])
                act(qsq[:, pair * 2:pair * 2 + 2, 0:256], ps[:].rearrange("p (a n) -> p a n", a=2), AF.Square, [pb], [QSQ])
                qps.append((ps, pb))
            psk, pbk = psum(pin=True)
            for kc in range(8):
                mm(psk[:, 0:256], wmi[:, kc, 1024:1152], hs[:, kc, :], kc == 0, kc == 7, [WMI] + ha, [pbk])
            act(qsq[:, 4, 0:256], psk[:, 0:256], AF.Square, [pbk], [QSQ])
            psv, pbv = psum()
            if not sample:
                for t_ in range(2):
                    for kc in range(8):
                        mm(psv[:, t_ * 128:(t_ + 1) * 128], h_all[:, kc, c0 + t_ * 128:c0 + (t_ + 1) * 128], wmi[:, kc, 1152:1280],
                           kc == 0, kc == 7, [WMI] + ha, [pbv])
                act(V_blk[:, 1:3, :], psv[:, 0:256].rearrange("p (a n) -> p a n", a=2), AF.Copy, [pbv], [VB])
                if lastp and (_LMASK & 1):
                    sv_, sbv_, iv_ = stage()
                    act(sv_[:, 0:128], psv[:, 128:256], AF.Copy, [pbv], [sbv_])
                    dma("sp", v_o[0:128, :], sv_[:, 0:128], ch_out[iv_], [sbv_], [OUT])
            else:
                for s in range(4):
                    for kc in range(8):
                        mm(psv[0:64, s * 128:(s + 1) * 128], h_all[:, kc, c0 + s * 64:c0 + (s + 1) * 64], wmi[:, kc, 1152:1280],
                           kc == 0, kc == 7, [WMI] + ha, [pbv])
                act(V_blk[0:64, :, :], psv[0:64, :].rearrange("p (a n) -> p a n", a=4), AF.Copy, [pbv], [VB])
                sv_, sbv_, iv_ = stage()
                act(sv_[0:64, 0:512], psv[0:64, :], AF.Copy, [pbv], [sbv_])
                for s in range(4 if (_SMASK & 1) else 0):
                    dma("sp", v_o[(1 + s) * 128 + 64:(2 + s) * 128, :], sv_[0:64, s * 128:(s + 1) * 128], ch_out[iv_], [sbv_], [OUT])
            for pair in range(3):
                RQ = bf("rq%d" % (pair % 2))
                rq = rq2[pair % 2]
                ps, pb = psum()
                n_ = 2 if pair < 2 else 1
                for i in range(n_):
                    m = pair * 2 + i
                    mm(ps[:, i * 256:(i + 1) * 256], bd_bf[:], qsq[:, m, 0:256], True, True, [bf("bd_bf"), QSQ], [pb])
                act(rq[:, 0:n_, :], ps[:, 0:n_ * 256].rearrange("p (a n) -> p a n", a=n_), AF.Ln, [pb], [RQ], bias=1e-6, scale=1.0 / 64)
                act(rq[:, 0:n_, :], rq[:, 0:n_, :], AF.Exp, [RQ], [RQ], scale=-0.5)
                if pair < 2:
                    qp, qb = qps[pair]
                    for i in range(2):
                        m = pair * 2 + i
                        stt(qn[:, m, :], qp[:, i * 256:(i + 1) * 256], gq8[:, 0:1], rq[:, i, :], ALU.mult, ALU.mult,
                            [qb, RQ, bf("gq8")], [QN])
                    unpin(qb)
                else:
                    stt(kn[:], psk[:, 0:256], col(C_GK), rq[:, 0, :], ALU.mult, ALU.mult, [pbk, RQ, COLS], [KN])
                    act(kTz[0][0:64, 128:384], kn[0:64, :], AF.Copy, [KN], [KTB])
                    act(kTz[1][64:128, 128:384], kn[64:128, :], AF.Copy, [KN], [KTB])
                    unpin(pbk)
            if lastp and (_LMASK & 4):
                s_, sb_, i_ = stage()
                ps, pb = psum()
                tr(ps[:, 0:128], kn[:, 128:256], ident_f, [KN, CONST], [pb])
                act(s_[:, 0:128], ps[:, 0:128], AF.Copy, [pb], [sb_])
                dma("sp", k_o[0:128, :], s_[:, 0:128], ch_out[i_], [sb_], [OUT])
            if sample and (_SMASK & 4):
                s_, sb_, i_ = stage()
                ps, pb = psum()
                for t_ in range(2):
                    tr(ps[:, t_ * 128:(t_ + 1) * 128], kn[:, t_ * 128:(t_ + 1) * 128], ident_f, [KN, CONST], [pb])
                act(s_[:, 0:256], ps[:, 0:256], AF.Copy, [pb], [sb_])
                for s in range(4):
                    dma("sp", k_o[(1 + s) * 128 + 64:(2 + s) * 128, :],
                        s_[(s % 2) * 64:(s % 2) * 64 + 64, (s // 2) * 128:(s // 2) * 128 + 128], ch_out[i_], [sb_], [OUT])
            SM = bf("att_sm")
            units = []
            if not sample:
                for t_i in range(2):
                    segs = [(t_i * 128, None, V_blk[:, t_i, :], 128, KTB, VB),
                            ((t_i + 1) * 128, None, V_blk[:, t_i + 1, :], 128, KTB, VB)]
                    units.append((128, t_i * 128, segs, first and t_i == 0, t_i))
            elif _SMASK & 8:
                for s in range(4):
                    segs = [(None, s, Vst[:, s, :], 128, bf("kTst"), bf("Vst")),
                            (128 + s * 64, None, V_blk[0:64, s, :], 64, KTB, VB)]
                    units.append((64, s * 64, segs, False, s // 2))
            waves = [(u, slot) for u in units for slot in range(2)]
            nW = len(waves)
            SMW = [Buf("smw%d" % wi) for wi in range(nW)]
            mset(att_sm[:], 0.0, SMW + [SM])
            W = 320 if sample else 272
            PT0, PT1 = bf("ptmp0"), bf("ptmp1")
            A_, B_ = ptmp[0], ptmp[1]

            def dview(ap2):
                if sample:
                    return ap2.rearrange("p (s n) -> p s n", s=4)[:, :, 16:80]
                return ap2[:, 16:272]

            def oview(ap2):
                if sample:
                    return ap2.rearrange("p (s n) -> p s n", s=4)
                return ap2

            def ptt(out, in0, in1, r, w):
                return P.op("pool", lambda e: e.tensor_tensor(out, in0, in1, ALU.add), reads=r, writes=w)

            def pool_group(gI, wnd):
                u_ = u_ext[:, gI, :]
                ptt(A_[:, 1:W], u_[:, 1:W], u_[:, 0:W - 1], [UE], [PT0])
                cur, curB = A_, PT0
                if wnd >= 4:
                    ptt(B_[:, 3:W], A_[:, 3:W], A_[:, 1:W - 2], [PT0], [PT1])
                    cur, curB = B_, PT1
                if wnd >= 8:
                    ptt(A_[:, 7:W], B_[:, 7:W], B_[:, 3:W - 4], [PT1], [PT0])
                    cur, curB = A_, PT0
                if wnd >= 16:
                    ptt(B_[:, 15:W], A_[:, 15:W], A_[:, 7:W - 8], [PT0], [PT1])
                    cur, curB = B_, PT1
                stt(oview(pooled[:, gI, :]), dview(cur[:, 0:W]), 1.0 / wnd, dview(u_), ALU.mult, ALU.subtract, [curB, UE], [PL])
                if first:
                    tt(ptiny[:, gI, :], cur[:, 16:32], consts[:, K_ICNT + gI * 16:K_ICNT + (gI + 1) * 16], ALU.mult,
                       [curB, CONST], [bf("ptiny")])
                    tt(pooled[:, gI, 0:16], ptiny[:, gI, :], u_[:, 16:32], ALU.subtract, [bf("ptiny"), UE, PL], [PL])

            pool_pending = []
            for gI, wnd in enumerate((2, 4, 8, 16)):
                if gI < 2:
                    pool_group(gI, wnd)
                else:
                    pool_pending.append(lambda gI=gI, wnd=wnd: pool_group(gI, wnd))
            wstate = {}
            po_tiles = {}
            last_wave_of_tile = {}
            for wi, (u, slot) in enumerate(waves):
                last_wave_of_tile[u[4]] = wi

            def kseg(slot, sg):
                c_, st_, _, n, _, _ = sg
                if st_ is not None:
                    return kTstz[slot][:, st_, 0:n]
                return kTz[slot][:, c_:c_ + n]

            def emitS(i):
                (Mq, a, segs, masked, tk), slot = waves[i]
                nk = sum(sg[3] for sg in segs)
                banks = [psum(), psum()]
                wstate[i] = banks
                for j in range(4):
                    h = slot * 4 + j
                    pS, pSb = banks[j // 2]
                    off = (j % 2) * 256
                    mm(pS[0:Mq, off:off + nk], ident_bf[:, 0:Mq], btab[:, h, 0:nk], True, False, [bf("ident_bf"), bf("btab")], [pSb])
                    if masked:
                        mm(pS[0:Mq, off:off + nk], ident_bf[:, 0:Mq], maskf[:, 0:nk], False, False, [bf("ident_bf"), bf("maskf")], [pSb])
                    ko = 0
                    for si, sg in enumerate(segs):
                        n = sg[3]
                        mm(pS[0:Mq, off + ko:off + ko + n], qn[:, j, a:a + Mq], kseg(slot, sg), False, si == len(segs) - 1,
                           [QN, sg[4]], [pSb])
                        ko += n

            def emitSoft(i):
                (Mq, a, segs, masked, tk), slot = waves[i]
                nk = sum(sg[3] for sg in segs)
                set_ = i % 2
                banks = wstate[i]
                PNB = bf("att_pn%d" % set_)
                smw = att_sm[0:Mq, i * 4:(i + 1) * 4, :]
                for j in range(4):
                    h = slot * 4 + j
                    pS, pSb = banks[j // 2]
                    off = (j % 2) * 256
                    hb = set_ * 4 + j
                    act(att_pn[hb][0:Mq, 0:nk], pS[0:Mq, off:off + nk], AF.Exp, [pSb, bf("negsink"), SMW[i]], [PNB, SMW[i]],
                        bias=negsink[0:Mq, h:h + 1], scale=1.0, accum=att_sm[0:Mq, i * 4 + j, 0:1])
                act(smw[:, :, 1:2], smw[:, :, 0:1], AF.Ln, [SMW[i]], [SMW[i]], bias=1.0, scale=1.0)
                act(smw[:, :, 2:3], smw[:, :, 1:2], AF.Exp, [SMW[i]], [SMW[i]], scale=-1.0)
                for j in range(4):
                    hb = set_ * 4 + j
                    ts(att_pn[hb][0:Mq, 0:nk], att_pn[hb][0:Mq, 0:nk], att_sm[0:Mq, i * 4 + j, 2:3], None, ALU.mult, None, [PNB, SMW[i]], [PNB])

            def emitT(i):
                (Mq, a, segs, masked, tk), slot = waves[i]
                set_ = i % 2
                PNB = bf("att_pn%d" % set_)
                PTSB = bf("att_PT%d" % set_)
                for j in range(4):
                    hb = set_ * 4 + j
                    ko = 0
                    for si, sg in enumerate(segs):
                        n = sg[3]
                        tr(psb[0:n, j * 256 + si * 128:j * 256 + si * 128 + Mq], att_pn[hb][0:Mq, ko:ko + n],
                           ident_bf[0:Mq, 0:Mq], [PNB, bf("ident_bf")], [PTB])
                        ko += n
                if Mq == 128 and all(sg[3] == 128 for sg in segs):
                    cpy(att_PT[set_][:, :], psb[:, :], [PTB], [PTSB])
                else:
                    for j in range(4):
                        for si, sg in enumerate(segs):
                            n = sg[3]
                            cpy(att_PT[set_][0:n, j * 256 + si * 128:j * 256 + si * 128 + Mq],
                                psb[0:n, j * 256 + si * 128:j * 256 + si * 128 + Mq], [PTB], [PTSB])

            def emitPV(i):
                (Mq, a, segs, masked, tk), slot = waves[i]
                set_ = i % 2
                PTSB = bf("att_PT%d" % set_)
                if tk not in po_tiles:
                    po_tiles[tk] = psum(pin=True)
                ps_o, pb_o = po_tiles[tk]
                for j in range(4):
                    for si, sg in enumerate(segs):
                        n = sg[3]
                        mm(ps_o[slot * 64:(slot + 1) * 64, j * 128 + (a % 128):j * 128 + (a % 128) + Mq],
                           sg[2][0:n, slot * 64:(slot + 1) * 64], att_PT[set_][0:n, j * 256 + si * 128:j * 256 + si * 128 + Mq],
                           si == 0, si == len(segs) - 1, [sg[5], PTSB], [pb_o])
                if last_wave_of_tile[tk] == i:
                    act(cat[:, 4:8, tk * 128:(tk + 1) * 128], ps_o[:].rearrange("p (j n) -> p j n", j=4), AF.Copy, [pb_o], [CAT])
                    unpin(pb_o)

            for i in range(nW + 2):
                if i < nW:
                    emitS(i)
                if 0 <= i - 1 < nW:
                    emitT(i - 1)
                if 0 <= i - 2 < nW:
                    emitPV(i - 2)
                if i < nW:
                    emitSoft(i)
                if pool_pending and i in (0, 1):
                    pool_pending.pop(0)()
            while pool_pending:
                pool_pending.pop(0)()
            if not sample:
                cpy(u_c[:], u_ext[:, :, 256:272], [UE], [bf("u_c")])
            flush_norm()
            for pair in range(2):
                ps, pb = psum()
                for i in range(2):
                    gI = pair * 2 + i
                    mm(ps[:, i * 256:(i + 1) * 256], pw[:, gI, :], pooled[:, gI, :], True, True, [bf("pw"), PL], [pb])
                for i in range(2):
                    gI = pair * 2 + i
                    act(cat[:, gI, :], ps[:, i * 256:(i + 1) * 256], AF.Copy, [pb, COLS], [CAT], scale=col(C_PSC + gI))
            if lastp and (_LMASK & 2):
                rows_out([u_ext[:, gI, 144:272] for gI in range(4)], 112, 16, pool_o[0:128, :], [UE])
            if sample and (_SMASK & 2):
                for s in range(4):
                    w0 = min(max(s * 80 + 80 - 128, 0), 320 - 128)
                    rows_out([u_ext[:, gI, w0:w0 + 128] for gI in range(4)], s * 80 + 64 - w0, 16, pool_o[(1 + s) * 128:(2 + s) * 128, :], [UE])
            wout_residual(cat, CAT, c0)
            if not sample:
                act(kT_c[0:64, :], kTz[0][0:64, 256:384], AF.Copy, [KTB], [bf("kT_c")])
                act(kT_c[64:128, :], kTz[1][64:128, 256:384], AF.Copy, [KTB], [bf("kT_c")])
                act(V_c[:], V_blk[:, 2, :], AF.Copy, [VB], [bf("V_c")])
            deferred_norm.append(lambda c0=c0: norm(c0, c0 + 256, C_GFFN + 0))

        PTAPS = []
        NPT = [0]

        def tap_regions():
            regs = [(wgA[:].rearrange("p a (b c) -> p (a b) c", c=128), [SLA]),
                    (wuA[:].rearrange("p a (b c) -> p (a b) c", c=128), [SLA]),
                    (wdA[:].rearrange("p a (b c) -> p (a b) c", c=128), [SLAd]),
                    (hid[0][:].rearrange("p a (b c) -> p (a b) c", c=128), [bf("hid0")]),
                    (hid[1][:].rearrange("p a (b c) -> p (a b) c", c=128), [bf("hid1")])]
            return regs

        def build_resident_taps():
            del PTAPS[:]
            t0 = 0
            for view, bufs in tap_regions():
                n = min(view.shape[1], 124 - t0)
                if n <= 0:
                    break
                wsrc = cols[:, C_CW + t0:C_CW + t0 + n]
                wbc = bass.AP(tensor=wsrc.tensor, offset=wsrc.offset, ap=[list(x) for x in wsrc.ap] + [[0, 128]])
                tt(view[:, 0:n, :], bcast_mid(ident_bf[:], n), wbc, ALU.mult, [bf("ident_bf"), COLS], bufs)
                for j in range(n):
                    PTAPS.append((view[:, j, :], bufs))
                t0 += n
            NPT[0] = t0

        def mixer1(g, b, kind):
            c0 = b * 256
            tl = [2 * b, 2 * b + 1]
            ha = [HA[t] for t in tl]
            lastp = (g == 1 and b == 3)
            sample = kind == "sample"
            halo = kind == "halo"
            GE, GBF, SIG, CAT1 = bf("glu_ext"), bf("glu_bf"), bf("zg"), bf("cat1")
            hs = h_all[:, :, c0:c0 + 256]
            W = 384 if sample else 288
            if sample:
                mset(glu_ext[:], 0.0, [GE])
                for ci in range(4):
                    act(glu_ext[:, ci, :].rearrange("p (s n) -> p s n", s=4)[:, :, 2:32], clead_s[:, ci, :, :], AF.Copy, [bf("clead_s")], [GE])
            else:
                cpy(glu_ext[:, :, 0:32], glu_c[:], [bf("glu_c")], [GE])
            for pair in range(2):
                psa, pba = psum()
                psg, pbg = psum()
                for (ps_, pb_, base) in ((psa, pba, 0), (psg, pbg, 512)):
                    for i in range(2):
                        m = pair * 2 + i
                        for kc in range(8):
                            mm(ps_[:, i * 256:(i + 1) * 256], wmi[:, kc, base + m * 128:base + (m + 1) * 128], hs[:, kc, :], kc == 0, kc == 7,
                               [WMI] + ha, [pb_])
                act(sig, psg[:].rearrange("p (a n) -> p a n", a=2), AF.Sigmoid, [pbg], [SIG])
                if not sample:
                    tt(glu_ext[:, pair * 2:pair * 2 + 2, 32:288], psa[:].rearrange("p (a n) -> p a n", a=2), sig, ALU.mult,
                       [pba, SIG], [GE])
                else:
                    for i in range(2):
                        m = pair * 2 + i
                        tt(glu_ext[:, m, :].rearrange("p (s n) -> p s n", s=4)[:, :, 32:96],
                           psa[:, i * 256:(i + 1) * 256].rearrange("p (s n) -> p s n", s=4),
                           sig[:, i, :].rearrange("p (s n) -> p s n", s=4), ALU.mult, [pba, SIG], [GE])
            flush_norm()
            if not sample:
                cpy(glu_c[:], glu_ext[:, :, 256:288], [GE], [bf("glu_c")])
            if lastp:
                rows_out([glu_ext[:, ci, 160:288] for ci in range(4)], 96, 32, conv_o[0:128, :], [GE])
            if sample and (_SMASK & 32):
                for s in range(4):
                    w0 = min(max(s * 96 + 96 - 128, 0), 384 - 128)
                    rows_out([glu_ext[:, ci, w0:w0 + 128] for ci in range(4)], s * 96 + 64 - w0, 32, conv_o[(1 + s) * 128:(2 + s) * 128, :], [GE])
            if halo:
                return
            act(glu_bf[:, :, 0:W], glu_ext[:, :, 0:W], AF.Copy, [GE], [GBF])
            CSQ = XSQ
            csq = xsq[:, 0:4, 0:256]
            c_bf = xsq[:, 4:8, 0:256]
            dgc = [0]
            conv_ps = []
            built = {}

            def tap_operand(t):
                if t < NPT[0]:
                    return PTAPS[t]
                t0 = NPT[0] + ((t - NPT[0]) // 16) * 16
                if t0 not in built:
                    nt = min(16, 124 - t0)
                    r_ = dgc[0] % 2
                    dgc[0] += 1
                    DGB = bf("dgb%d" % r_)
                    wsrc = cols[:, C_CW + t0:C_CW + t0 + nt]
                    wbc = bass.AP(tensor=wsrc.tensor, offset=wsrc.offset, ap=[list(x) for x in wsrc.ap] + [[0, 128]])
                    tt(dgb[r_][:, 0:nt, :], bcast_mid(ident_bf[:], nt), wbc, ALU.mult, [bf("ident_bf"), COLS], [DGB])
                    built[t0] = (r_, DGB)
                r_, DGB = built[t0]
                return dgb[r_][:, t - t0, :], [DGB]

            for t_ in range(NPT[0], 124, 16):
                tap_operand(t_)
            ZU = bf("zu")
            for pair in range(2):
                ps, pb = psum()
                for i in range(2):
                    m = pair * 2 + i
                    for kc in range(8):
                        mm(ps[:, i * 256:(i + 1) * 256], wmi[:, kc, 1024 + m * 128:1024 + (m + 1) * 128], hs[:, kc, :], kc == 0, kc == 7,
                           [WMI] + ha, [pb])
                act(zu[:, pair * 2:pair * 2 + 2, :], ps[:].rearrange("p (a n) -> p a n", a=2), AF.Gelu, [pb], [ZU])
            VNB, MVZ = bf("vn_bf"), bf("mvz")
            zgt = [zg[:], ctmp2[:].rearrange("p a n -> p (a n)")]
            ZGt = [[bf("zg")], [bf("ctmp0"), bf("ctmp1")]]
            for t_i in range(2):
                ps, pb = psum()
                for kc in range(8):
                    mm(ps[:], h_all[:, kc, c0 + t_i * 128:c0 + (t_i + 1) * 128], wmi[:, kc, 1536:2048], kc == 0, kc == 7, [WMI] + ha, [pb])
                act(zgt[t_i], ps[:], AF.Gelu, [pb], ZGt[t_i])
            for t_i in range(2):
                dve(lambda e, t_i=t_i: e.bn_stats(st6[:, t_i, 0:6], zgt[t_i]), ZGt[t_i], [bf("st6")])
                dve(lambda e, t_i=t_i: e.bn_aggr(mvz[:, t_i, 0:2], st6[:, t_i, 0:6]), [bf("st6")], [MVZ])
            act(mvz[:, :, 2:3], mvz[:, :, 1:2], AF.Ln, [MVZ], [MVZ], bias=1e-5, scale=1.0)
            act(mvz[:, :, 3:4], mvz[:, :, 2:3], AF.Exp, [MVZ], [MVZ], scale=-0.5)
            for t_i in range(2):
                z_ = zgt[t_i]
                ts(z_, z_, mvz[:, t_i, 0:1], mvz[:, t_i, 3:4], ALU.subtract, ALU.mult, ZGt[t_i] + [MVZ], ZGt[t_i])
                tt(z_, z_, consts[:, K_LNG:K_LNG + 512], ALU.mult, ZGt[t_i] + [CONST], ZGt[t_i])
                tt(z_, z_, consts[:, K_LNB:K_LNB + 512], ALU.add, ZGt[t_i] + [CONST], ZGt[t_i])
                act(vn_bf[:, t_i, :], z_, AF.Copy, ZGt[t_i], [VNB])
                if sample:
                    for half in range(2):
                        s = t_i * 2 + half
                        if _SMASK & 16:
                            dma_scr("sp", gv_o[s * 64:(s + 1) * 64, :], z_[half * 64:(half + 1) * 64, :], ch_misc, ZGt[t_i], [OUT])
            for pair in range(2):
                ps, pb = psum(pin=True)
                for i in range(2):
                    ci = pair * 2 + i
                    for k in range(31):
                        lw, lwb = tap_operand(ci * 31 + k)
                        if not sample:
                            mm(ps[:, i * 256:(i + 1) * 256], lw, glu_bf[:, ci, 2 + k:2 + k + 256], k == 0, k == 30, lwb + [GBF], [pb])
                        else:
                            for s in range(4):
                                P.op("pe", lambda e, o_=ps[:, i * 256 + s * 64:i * 256 + (s + 1) * 64], l_=lw,
                                     r2_=glu_bf[:, ci, s * 96 + 2 + k:s * 96 + 2 + k + 64], st_=(k == 0 and s == 0), sp_=(k == 30 and s == 3):
                                     e.matmul(o_, l_, r2_, start=st_, stop=sp_, skip_group_check=True), reads=lwb + [GBF], writes=[pb])
                for i in range(2):
                    ci = pair * 2 + i
                    act(c_bf[:, ci, :], ps[:, i * 256:(i + 1) * 256], AF.Identity, [pb, COLS], [CSQ], bias=col(C_CB + ci))
                    act(csq[:, ci, :], ps[:, i * 256:(i + 1) * 256], AF.Square, [pb, COLS], [CSQ], bias=col(C_CB + ci))
                conv_ps.append((ps, pb))
            if not sample:
                for pair in range(2):
                    ps, pb = psum()
                    for i in range(2):
                        gg = pair * 2 + i
                        for t_i in range(2):
                            mm(ps[:, i * 256 + t_i * 128:i * 256 + (t_i + 1) * 128], vn_bf[:, t_i, gg * 128:(gg + 1) * 128], wm[:, gg, :],
                               True, True, [VNB, bf("wm")], [pb])
                    for i in range(2):
                        gg = pair * 2 + i
                        GT = bf("ctmp%d" % i)
                        tt(gtmp[i][:].rearrange("p (a n) -> p a n", a=2), ps[:, i * 256:(i + 1) * 256].rearrange("p (a n) -> p a n", a=2),
                           bcast_mid(consts[:, K_GB + gg * 128:K_GB + (gg + 1) * 128], 2), ALU.add, [pb, CONST], [GT])
                        tt(cat1[:, 4 + gg, :], gtmp[i][:], zu[:, gg, :], ALU.mult, [GT, ZU], [CAT1])
            else:
                psH = [psum(pin=True), psum(pin=True)]
                for half in range(2):
                    ps, pb = psH[half]
                    for gg in range(4):
                        for t_i in range(2):
                            o_ = gg * 128 + t_i * 64
                            mm(ps[:, o_:o_ + 64], vn_bf[half * 64:(half + 1) * 64, t_i, gg * 128:(gg + 1) * 128],
                               wms[half * 64:(half + 1) * 64, gg, :], True, True, [VNB, bf("wms")], [pb])
                for gg in range(4):
                    i = gg % 2
                    GT = bf("ctmp%d" % i)
                    for half in range(2):
                        ps, pb = psH[half]
                        tt(gtmp[i][:].rearrange("p (t h n) -> p t h n", t=2, h=2)[:, :, half, :],
                           ps[:, gg * 128:(gg + 1) * 128].rearrange("p (t n) -> p t n", t=2),
                           bcast_mid(consts[:, K_GB + gg * 128:K_GB + gg * 128 + 64], 2), ALU.add, [pb, CONST], [GT])
                    tt(cat1[:, 4 + gg, :], gtmp[i][:], zu[:, gg, :], ALU.mult, [GT, ZU], [CAT1])
                unpin(psH[0][1])
                unpin(psH[1][1])
            MV = bf("mv")
            pst, pbt = psum()
            for ci in range(4):
                mm(pst[:, 0:256], ones_bf[:], c_bf[:, ci, :], ci == 0, ci == 3, [CSQ, bf("ones_bf")], [pbt])
            for ci in range(4):
                mm(pst[:, 256:512], ones_bf[:], csq[:, ci, :], ci == 0, ci == 3, [CSQ, bf("ones_bf")], [pbt])
            act(mv[:, 0, :], pst[:, 0:256], AF.Copy, [pbt], [MV], scale=1.0 / 512)
            tt(mv[:, 2, :], mv[:, 0, :], mv[:, 0, :], ALU.mult, [MV], [MV])
            stt(mv[:, 1, :], pst[:, 256:512], 1.0 / 512, mv[:, 2, :], ALU.mult, ALU.subtract, [pbt, MV], [MV])
            act(mv[:, 1, :], mv[:, 1, :], AF.Ln, [MV], [MV], bias=1e-5, scale=1.0)
            act(mv[:, 1, :], mv[:, 1, :], AF.Exp, [MV], [MV], scale=-0.5)
            for ci in range(4):
                ps, pb = conv_ps[ci // 2]
                i = ci % 2
                CT = bf("ctmp%d" % (ci % 2))
                stt(ctmp[ci % 2][:], ps[:, i * 256:(i + 1) * 256], col(C_CB + ci), mv[:, 0, :], ALU.add, ALU.subtract, [pb, MV, COLS], [CT])
                tt(ctmp[ci % 2][:], ctmp[ci % 2][:], mv[:, 1, :], ALU.mult, [CT, MV], [CT])
                act(cat1[:, ci, :], ctmp[ci % 2][:], AF.Silu, [CT, COLS], [CAT1], bias=col(C_LB + ci), scale=col(C_LG + ci))
                if i == 1:
                    unpin(pb)
            wout_residual(cat1, CAT1, c0)
            deferred_norm.append(lambda c0=c0: norm(c0, c0 + 256, C_GFFN + 8))

        def ffn(layer, blocks, after_block, next_chunk0_loader, mixer_end_ops, bg_loads=()):
            bg = list(bg_loads)
            pending = [None]
            HID = [bf("hid0"), bf("hid1")]
            SSB = [bf("ssb0"), bf("ssb1")]
            it = [0]

            def down(c, blk, hb):
                wg, wu, wd, sb_, ch = chunk_slot(c)
                nf = FCH[c]
                c0_, c1_ = blk
                W = c1_ - c0_
                xr = [XR[t] for t in tiles_of(c0_, c1_)]
                for m in range(8):
                    ps, pb = psum()
                    for fi in range(nf):
                        mm(ps[:, 0:W], wd[:, fi, m * 128:(m + 1) * 128], hid[hb][:, fi, 0:W], fi == 0, fi == nf - 1, [chunk_slot_d(c)[0], HID[hb]], [pb])
                    tt(x_res[:, m, c0_:c1_], ps[:, 0:W], x_res[:, m, c0_:c1_], ALU.add, [pb] + xr, xr)
                if c == len(FCH) - 1:
                    after_block(blk)

            load_chunk(layer, 1, extra=mixer_end_ops)
            for c in range(len(FCH)):
                wg, wu, wd, sb_, ch = chunk_slot(c)
                nf = FCH[c]
                for blk in blocks:
                    c0_, c1_ = blk
                    W = c1_ - c0_
                    ha = [HA[t] for t in tiles_of(c0_, c1_)]
                    hb = it[0] % 2
                    it[0] += 1
                    for fi in range(nf):
                        psg, pbg = psum()
                        for kc in range(8):
                            mm(psg[:, 0:W], wg[:, kc, fi * 128:(fi + 1) * 128], h_all[:, kc, c0_:c1_], kc == 0, kc == 7, [sb_] + ha, [pbg])
                        psu, pbu = psum()
                        for kc in range(8):
                            mm(psu[:, 0:W], wu[:, kc, fi * 128:(fi + 1) * 128], h_all[:, kc, c0_:c1_], kc == 0, kc == 7, [sb_] + ha, [pbu])
                        k_ = fi % 2
                        act(s_sb[k_][:, 0:W], psg[:, 0:W], AF.Silu, [pbg], [SSB[k_]])
                        tt(hid[hb][:, fi, 0:W], psu[:, 0:W], s_sb[k_][:, 0:W], ALU.mult, [pbu, SSB[k_]], [HID[hb]])
                    if pending[0] is not None:
                        down(*pending[0])
                    pending[0] = (c, blk, hb)
                if c + 2 < len(FCH):
                    if pending[0] is not None:
                        down(*pending[0])
                        pending[0] = None
                    load_chunk(layer, c + 2)
                    for _ in range(4):
                        if bg:
                            bg.pop(0)()
            if pending[0] is not None:
                down(*pending[0])
                pending[0] = None
            while bg:
                bg.pop(0)()
            if next_chunk0_loader is not None:
                next_chunk0_loader()

        setup()
        if stg >= 1:
            load_mixer_weights(0)
            load_chunk(0, 0)

        kinds = [["halo", "prompt", "prompt", "prompt", "prompt"], ["prompt", "prompt", "prompt", "prompt", "sample"]]
        ffn_blocks = [
            [[(128, 256), (256, 768), (768, 1280)], [(256, 768), (768, 1280)]],
            [[(0, 512), (512, 1024), (1024, 1280)], [(0, 512), (512, 1024), (1024, 1280)]],
        ]

        for g in range(0 if stg < 1 else (2 if stg >= 6 else 1)):
            gstg = stg if (g == 0 or stg >= 99) else {6: 2, 7: 2, 8: 4, 9: 4, 10: 1, 11: 2}[stg]
            nb0 = 4 if (g == 1 and stg == 6) else (3 if (g == 1 and stg == 11) else 5)
            nb1 = 4 if (g == 1 and stg == 8) else 5
            for b in range(5):
                for t in (2 * b, 2 * b + 1):
                    load_tile(g, t)
                norm(b * 256, b * 256 + 256, C_GMIX + 0)
            if gstg <= 1:
                for t in range(2, 10):
                    store_tile(g, t)
                continue
            for e_ in ("act", "dve", "pe"):
                fence(e_, list(SLB.rs.values()) + list(SLB.ws.values()) + list(SLBd.rs.values()) + list(SLBd.ws.values()))
            for b in range(nb0):
                mixer0(g, b, kinds[g][b])
            flush_norm()
            mix_end = [P.last.get(e_) for e_ in ("pe", "act", "dve")] + scr_dma[-1:]
            if gstg <= 2:
                for t in range(2, 10):
                    store_tile(g, t)
                continue
            bg1_jobs = load_mixer_weights(1, defer=True)

            def after_l0(blk):
                norm(blk[0], blk[1], C_GMIX + 8)

            ffn(0, ffn_blocks[g][0], after_l0, None, mix_end, bg1_jobs)
            if gstg <= 3:
                for t in range(2, 10):
                    store_tile(g, t)
                continue
            for e_ in ("act", "dve", "pe"):
                fence(e_, list(SLB.rs.values()) + list(SLB.ws.values()) + list(SLBd.rs.values()) + list(SLBd.ws.values()))
            for b in range(nb1):
                if b == (1 if kinds[g][0] == "halo" else 0):
                    build_resident_taps()
                mixer1(g, b, kinds[g][b])
            flush_norm()
            load_chunk(1, 0)
            mix_end = [P.last.get(e_) for e_ in ("pe", "act", "dve")] + scr_dma[-1:]
            if gstg <= 4:
                for t in range(2, 10):
                    store_tile(g, t)
                continue
            bg_jobs = load_mixer_weights(0, defer=True) if g == 0 else []

            def after_l1(blk, g=g):
                for t in tiles_of(blk[0], blk[1]):
                    store_tile(g, t)

            ffn(1, ffn_blocks[g][1], after_l1, (lambda: load_chunk(0, 0)) if g == 0 else None, mix_end, bg_jobs)

        P.op("sp", None, reads=[OUT], extra=list(chan_last.values()))
        P.lower(block, sems)
    return nc


def _host_consts(core):
    c = np.zeros((128, NCONST), np.float32)
    c[:, K_ID:K_ID + 128] = np.eye(128, dtype=np.float32)
    j = np.arange(128)[:, None]
    i = np.arange(128)[None, :]
    c[:, K_TRIU:K_TRIU + 128] = (j <= i).astype(np.float32)
    q = np.arange(128)[:, None]
    s = np.arange(256)[None, :]
    dist = np.abs(128 + q - s).astype(np.float32)
    qc = q // 64
    sc = s // 64
    allowed = (sc >= qc) & (sc <= qc + 2)
    nd = np.where(allowed, -dist, -1.0e6).astype(np.float32)
    c[:, K_ND:K_ND + 256] = nd
    mk = np.zeros((128, 256), np.float32)
    if core == 0:
        mk[:, 0:128] = -30000.0
    c[:, K_NDF:K_NDF + 256] = mk
    for gI, w in enumerate((2, 4, 8, 16)):
        pos = np.arange(16)
        cntv = np.minimum(pos + 1, w) if core == 0 else np.full(16, w)
        c[:, K_ICNT + gI * 16:K_ICNT + (gI + 1) * 16] = (1.0 / cntv.astype(np.float32))[None, :]
    bd = np.zeros((128, 128), np.float32)
    bd[0:64, 0:64] = 1.0
    bd[64:128, 64:128] = 1.0
    c[:, K_BD:K_BD + 128] = bd
    j64 = (np.arange(128) % 64)[:, None]
    i64 = np.arange(64)[None, :]
    c[:, K_TRIU64:K_TRIU64 + 64] = (j64 <= i64).astype(np.float32)
    return c


def kernel(x_prompt, x_sample, state_pool, state_swa_k, state_swa_v, state_conv,
           norm_mix, norm_ffn, w_in_even, q_norm, k_norm, attn_sinks, pool_w, pool_scale,
           w_out_even, w_in_odd, conv_w, conv_b, conv_ln_g, conv_ln_b, gmlp_ln_g, gmlp_ln_b,
           gmlp_w, gmlp_b, w_out_odd, ffn_gate, ffn_up, ffn_down):
    f = lambda a: np.ascontiguousarray(np.asarray(a, dtype=np.float32))
    xp = f(x_prompt)[0]
    xsm = f(x_sample).reshape(32 * 64, 1024)
    cols = np.zeros((128, NCOL), np.float32)
    nm, nf_ = f(norm_mix), f(norm_ffn)
    for l in range(2):
        cols[:, C_GMIX + 8 * l:C_GMIX + 8 * l + 8] = nm[l].reshape(8, 128).T
        cols[:, C_GFFN + 8 * l:C_GFFN + 8 * l + 8] = nf_[l].reshape(8, 128).T
    cols[:, C_PSC:C_PSC + 4] = f(pool_scale)[0].reshape(4, 128).T
    cols[:, C_GQ] = np.tile(f(q_norm)[0], 2)
    cols[:, C_GK] = np.tile(f(k_norm)[0], 2)
    cols[:, C_SINK:C_SINK + 8] = np.broadcast_to(f(attn_sinks)[0][None, :], (128, 8))
    cols[:, C_CB:C_CB + 4] = f(conv_b)[0].reshape(4, 128).T
    cols[:, C_LG:C_LG + 4] = f(conv_ln_g)[0].reshape(4, 128).T
    cols[:, C_LB:C_LB + 4] = f(conv_ln_b)[0].reshape(4, 128).T
    cw = f(conv_w)[0]
    cols[:, C_CW:C_CW + 124] = cw.reshape(31, 4, 128).transpose(2, 1, 0).reshape(128, 124)
    lng = np.broadcast_to(f(gmlp_ln_g)[0][None, :], (128, 512))
    lnb = np.broadcast_to(f(gmlp_ln_b)[0][None, :], (128, 512))
    gb = np.broadcast_to(f(gmlp_b)[0].reshape(1, 512), (128, 512))
    gwT = np.ascontiguousarray(f(gmlp_w)[0].transpose(0, 2, 1)).reshape(512, 128)
    shared = {
        "cols": cols,
        "w_in0": f(w_in_even)[0], "w_out0": f(w_out_even)[0], "pool_w": f(pool_w)[0].reshape(512, 128),
        "w_in1": f(w_in_odd)[0], "w_out1": f(w_out_odd)[0], "gwT": gwT,
        "fgate": f(ffn_gate).reshape(2048, 2816), "fup": f(ffn_up).reshape(2048, 2816), "fdown": f(ffn_down).reshape(5632, 1024),
    }
    sp, sk, sv, scv = f(state_pool)[0], f(state_swa_k)[0], f(state_swa_v)[0], f(state_conv)[0]
    in_maps = []
    for c in range(N_CORES):
        xin = np.zeros((2560, 1024), np.float32)
        if c > 0:
            xin[0:256] = xp[2048 * c - 256:2048 * c]
        xin[256:2304] = xp[2048 * c:2048 * (c + 1)]
        xin[2304:2560] = xsm[256 * c:256 * (c + 1)]
        cst = _host_consts(c)
        cst[:, K_LNG:K_LNG + 512] = lng
        cst[:, K_LNB:K_LNB + 512] = lnb
        cst[:, K_GB:K_GB + 512] = gb
        m = dict(shared)
        m.update({
            "xin": xin, "consts": cst,
            "st_pool": np.ascontiguousarray(sp[4 * c:4 * c + 4].reshape(60, 512)),
            "st_k": np.ascontiguousarray(sk[4 * c:4 * c + 4].reshape(512, 128)),
            "st_v": np.ascontiguousarray(sv[4 * c:4 * c + 4].reshape(512, 128)),
            "st_conv": np.ascontiguousarray(scv[4 * c:4 * c + 4].reshape(120, 512)),
        })
        in_maps.append(m)
    nc = build_nc(_STAGE)
    res = run_bass_kernel_spmd(nc, in_maps, core_ids=list(range(N_CORES)))
    if _STAGE < 99:
        return res.results
    R = res.results
    y_prompt = np.concatenate([R[c]["yout"][0:2048] for c in range(N_CORES)], 0)[None]
    y_sample = np.concatenate([R[c]["yout"][2048:2304] for c in range(N_CORES)], 0).reshape(32, 64, 1024)
    def _rows(a, n):
        a = a.reshape(5, 128, 512)
        out = [a[0, 128 - n:128]]
        for s_ in range(4):
            w = 320 if n == 16 else 384
            per = 80 if n == 16 else 96
            w0 = min(max(s_ * per + per - 128, 0), w - 128)
            r0 = s_ * per + 64 - w0
            out.append(a[1 + s_, r0:r0 + n])
        return np.stack(out, 0)

    po = [_rows(R[c]["pool_o"], 16) for c in range(N_CORES)]
    ko = [R[c]["k_o"].reshape(5, 128, 2, 64) for c in range(N_CORES)]
    vo = [R[c]["v_o"].reshape(5, 128, 2, 64) for c in range(N_CORES)]
    co = [_rows(R[c]["conv_o"], 32) for c in range(N_CORES)]
    gvo = [R[c]["gv_o"].reshape(4, 64, 512) for c in range(N_CORES)]
    pool_prompt = po[7][0:1, 1:16][None]
    pool_sample = np.concatenate([p[1:5, 1:16] for p in po], 0)[None]
    k_prompt = ko[7][0:1][None]
    k_sample = np.concatenate([k[1:5] for k in ko], 0)[None]
    v_prompt = vo[7][0:1][None]
    v_sample = np.concatenate([v[1:5] for v in vo], 0)[None]
    conv_prompt = co[7][0:1, 2:32][None]
    conv_sample = np.concatenate([x[1:5, 2:32] for x in co], 0)[None]
    gv_sample = np.concatenate(gvo, 0)[None]
    outs = (y_prompt, y_sample, pool_prompt, pool_sample, k_prompt, k_sample, v_prompt, v_sample,
            conv_prompt, conv_sample, gv_sample)
    return tuple(np.ascontiguousarray(o, dtype=np.float32) for o in outs)
```
